# Optimizing a Trainium2 kernel written in Bass

```python
import jax, jax.numpy as jnp
from jax import lax
import numpy as np

D_MODEL = 1024
BATCH = 2
SEQ = 8192
DEPTH = 1
DEC_BATCH = 128
DEC_SEQ = 8
PAST_LEN = 2048
PAGE_SIZE = 128

D_MIX = D_MODEL
HG_HEADS = 8
HG_DK = 64
HG_DV = 64
HG_QK = HG_HEADS * HG_DK
HG_WIDTH = HG_HEADS * HG_DV
HG_CHUNK = 64
SB_HEADS = 8
SB_DH = 64
SB_WIDTH = SB_HEADS * SB_DH
SB_QBLOCK = 128
SB_SCALE = SB_DH ** -0.5
SB_BIAS_INIT = -7.0
N_MEM = 256
CA_HEADS = 4
CA_DH = D_MODEL // CA_HEADS
D_FF = 2816
CONV_W = 3
RMS_EPS = 1e-6
D_IN = 2 * HG_QK + 2 * HG_WIDTH + 3 * SB_WIDTH

kernel_name = "hymba_hgrn2_stickbreaking_convffn_step"


def rmsnorm(x, g):
    xf = x.astype(jnp.float32)
    y = xf * lax.rsqrt(jnp.mean(xf * xf, axis=-1, keepdims=True) + RMS_EPS)
    return (y * g.astype(jnp.float32)).astype(x.dtype)


def hgrn2_recurrence(q, logf, k, v, s0, chunk):
    n, t, h, dk = q.shape
    dv = v.shape[-1]
    nc = t // chunk

    def to_chunks(a):
        return a.reshape(n, nc, chunk, h, a.shape[-1]).transpose(1, 0, 2, 3, 4)

    causal = jnp.tril(jnp.ones((chunk, chunk), dtype=bool))[None, :, :, None, None]

    def step(s, inp):
        qc, lfc, kc, vc = inp
        b = jnp.cumsum(lfc, axis=1)
        o_inter = jnp.einsum('nthk,nhkv->nthv', qc * jnp.exp(b), s)
        diff = jnp.where(causal, b[:, :, None] - b[:, None, :], 0.0)
        decay = jnp.where(causal, jnp.exp(diff), 0.0)
        att = jnp.einsum('nthk,ntshk,nshk->nhts', qc, decay, kc)
        o_intra = jnp.einsum('nhts,nshv->nthv', att, vc)
        b_last = b[:, -1]
        s_new = jnp.exp(b_last)[..., None] * s + jnp.einsum(
            'nshk,nshv->nhkv', kc * jnp.exp(b_last[:, None] - b), vc)
        return s_new, o_inter + o_intra

    s_t, o = lax.scan(step, s0, (to_chunks(q), to_chunks(logf), to_chunks(k), to_chunks(v)))
    return o.transpose(1, 0, 2, 3, 4).reshape(n, t, h, dv), s_t


def sb_attend(q, k, v, bias, q_pos, k_pos):
    z = jnp.einsum('nqhd,nkhd->nhqk', q.astype(jnp.float32), k.astype(jnp.float32)) * SB_SCALE
    z = z + bias.astype(jnp.float32)[None, :, None, None]
    strict = (k_pos[None, :] < q_pos[:, None])[None, None]
    c = jnp.where(strict, jax.nn.log_sigmoid(-z), 0.0)
    later = lax.cumsum(c, axis=3, reverse=True) - c
    a = jnp.where(strict, jnp.exp(jax.nn.log_sigmoid(z) + later), 0.0)
    return jnp.einsum('nhqk,nkhd->nqhd', a, v.astype(jnp.float32))


def stick_breaking(q, k, v, bias, q_offset):
    n, t, h, d = q.shape
    qb = min(SB_QBLOCK, t)
    nb = t // qb
    k_pos = jnp.arange(k.shape[1])

    def one_block(i):
        q_blk = lax.dynamic_slice_in_dim(q, i * qb, qb, axis=1)
        q_pos = q_offset + i * qb + jnp.arange(qb)
        return sb_attend(q_blk, k, v, bias, q_pos, k_pos)

    o = lax.map(one_block, jnp.arange(nb))
    return o.transpose(1, 0, 2, 3, 4).reshape(n, t, h, d)


def token_mix(h, p, lb, s0, past_k, past_v, chunk):
    n, t, _ = h.shape
    f32 = jnp.float32
    proj = h @ p['w_in']
    sizes = [HG_QK, HG_QK, HG_WIDTH, HG_WIDTH, SB_WIDTH, SB_WIDTH, SB_WIDTH]
    cuts = [int(c) for c in np.cumsum(sizes)[:-1]]
    hq, hf, hi, hgate, sq, sk, sv = jnp.split(proj, cuts, axis=-1)
    q = hq.reshape(n, t, HG_HEADS, HG_DK).astype(f32)
    f = lb + (1.0 - lb) * jax.nn.sigmoid(hf.reshape(n, t, HG_HEADS, HG_DK).astype(f32))
    vin = hi.reshape(n, t, HG_HEADS, HG_DV).astype(f32)
    o_hg, s_t = hgrn2_recurrence(q, jnp.log(f), 1.0 - f, vin, s0.astype(f32), chunk)
    o_hg = o_hg * lax.rsqrt(jnp.mean(o_hg * o_hg, axis=-1, keepdims=True) + RMS_EPS)
    o_hg = o_hg.reshape(n, t, HG_WIDTH) * p['hg_norm'].astype(f32) * jax.nn.silu(hgate.astype(f32))
    qs = sq.reshape(n, t, SB_HEADS, SB_DH)
    ks = sk.reshape(n, t, SB_HEADS, SB_DH)
    vs = sv.reshape(n, t, SB_HEADS, SB_DH)
    if past_k is None:
        keys, vals, offset = ks, vs, 0
    else:
        keys = jnp.concatenate([past_k.astype(ks.dtype), ks], axis=1)
        vals = jnp.concatenate([past_v.astype(vs.dtype), vs], axis=1)
        offset = past_k.shape[1]
    o_sb = stick_breaking(qs, keys, vals, p['sb_bias'], offset).reshape(n, t, SB_WIDTH)
    mixed = jnp.concatenate([o_hg, o_sb], axis=-1).astype(h.dtype) @ p['w_o']
    return mixed, s_t, ks, vs


def cross_attend(h, mk, mv, p):
    n, t, _ = h.shape
    q = (h @ p['w_cq']).reshape(n, t, CA_HEADS, CA_DH)
    s = jnp.einsum('nthd,nmhd->nhtm', q.astype(jnp.float32), mk.astype(jnp.float32)) * (CA_DH ** -0.5)
    pr = jax.nn.softmax(s, axis=-1)
    o = jnp.einsum('nhtm,nmhd->nthd', pr, mv.astype(jnp.float32))
    return o.reshape(n, t, CA_HEADS * CA_DH).astype(h.dtype) @ p['w_co']


def conv_ffn(h, p, buf):
    t = h.shape[1]
    u = h @ p['w_up']
    ext = jnp.concatenate([buf.astype(u.dtype), u], axis=1)
    c = p['conv_b'] + sum(p['conv_w'][j] * ext[:, j:j + t] for j in range(CONV_W))
    gate, val = jnp.split(c, 2, axis=-1)
    y = (jax.nn.gelu(gate, approximate=True) * val) @ p['w_down']
    return y, ext[:, t:]


def layer_forward(x, p, lb, s0, past_k, past_v, buf, mk, mv, chunk):
    mixed, s_t, k_new, v_new = token_mix(rmsnorm(x, p['g_mix_pre']), p, lb, s0, past_k, past_v, chunk)
    x = x + rmsnorm(mixed, p['g_mix_post'])
    x = x + rmsnorm(cross_attend(rmsnorm(x, p['g_ca_pre']), mk, mv, p), p['g_ca_post'])
    y, buf_new = conv_ffn(rmsnorm(x, p['g_ffn_pre']), p, buf)
    x = x + rmsnorm(y, p['g_ffn_post'])
    return x, s_t, k_new, v_new, buf_new


def memory_kv(mem, g, w_ck, w_cv):
    n = mem.shape[0]
    mn = rmsnorm(mem, g)
    mk = (mn @ w_ck).reshape(n, N_MEM, CA_HEADS, CA_DH)
    mv = (mn @ w_cv).reshape(n, N_MEM, CA_HEADS, CA_DH)
    return mk, mv


def setup_inputs(seed: int = 0) -> dict:
    key = jax.random.key(seed)
    ks = jax.random.split(key, 32)
    f32 = jnp.float32

    def nrm(k, shape, scale):
        return jax.random.normal(k, shape, f32) * scale

    def gain(k, shape):
        return 1.0 + 0.1 * jax.random.normal(k, shape, f32)

    n_pages = PAST_LEN // PAGE_SIZE
    n_pool = (DEC_BATCH * n_pages * 5) // 4
    perm = jax.random.permutation(ks[0], n_pool)
    page_table = perm[:DEC_BATCH * n_pages].reshape(DEC_BATCH, n_pages).astype(jnp.int32)
    return {
        'x_prompt': nrm(ks[1], (BATCH, SEQ, D_MODEL), 1.0),
        'x_sample': nrm(ks[2], (DEC_BATCH, DEC_SEQ, D_MODEL), 1.0),
        'cache_sb_k': nrm(ks[3], (DEPTH, n_pool, PAGE_SIZE, SB_HEADS, SB_DH), 1.0),
        'cache_sb_v': nrm(ks[4], (DEPTH, n_pool, PAGE_SIZE, SB_HEADS, SB_DH), 1.0),
        'state_hgrn': nrm(ks[5], (DEPTH, DEC_BATCH, HG_HEADS, HG_DK, HG_DV), 0.5),
        'state_ffn_conv': nrm(ks[6], (DEPTH, DEC_BATCH, CONV_W - 1, 2 * D_FF), 1.0),
        'cache_mem_k': nrm(ks[7], (DEPTH, DEC_BATCH, N_MEM, CA_HEADS, CA_DH), 1.0),
        'cache_mem_v': nrm(ks[8], (DEPTH, DEC_BATCH, N_MEM, CA_HEADS, CA_DH), 1.0),
        'page_table': page_table,
        'mem_prompt': nrm(ks[9], (BATCH, N_MEM, D_MODEL), 1.0),
        'w_in': nrm(ks[10], (DEPTH, D_MODEL, D_IN), D_MODEL ** -0.5),
        'hg_norm': gain(ks[11], (DEPTH, HG_WIDTH)),
        'hg_lb': nrm(ks[12], (DEPTH + 1, HG_QK), 0.5),
        'sb_bias': SB_BIAS_INIT + nrm(ks[29], (DEPTH, SB_HEADS), 0.5),
        'w_o': nrm(ks[13], (DEPTH, D_MIX, D_MODEL), D_MIX ** -0.5),
        'g_mix_pre': gain(ks[14], (DEPTH, D_MODEL)),
        'g_mix_post': gain(ks[15], (DEPTH, D_MODEL)),
        'g_ca_pre': gain(ks[16], (DEPTH, D_MODEL)),
        'g_ca_post': gain(ks[17], (DEPTH, D_MODEL)),
        'g_mem': gain(ks[18], (DEPTH, D_MODEL)),
        'w_cq': nrm(ks[19], (DEPTH, D_MODEL, D_MODEL), D_MODEL ** -0.5),
        'w_ck': nrm(ks[20], (DEPTH, D_MODEL, D_MODEL), D_MODEL ** -0.5),
        'w_cv': nrm(ks[21], (DEPTH, D_MODEL, D_MODEL), D_MODEL ** -0.5),
        'w_co': nrm(ks[22], (DEPTH, D_MODEL, D_MODEL), D_MODEL ** -0.5),
        'g_ffn_pre': gain(ks[23], (DEPTH, D_MODEL)),
        'g_ffn_post': gain(ks[24], (DEPTH, D_MODEL)),
        'w_up': nrm(ks[25], (DEPTH, D_MODEL, 2 * D_FF), D_MODEL ** -0.5),
        'conv_w': nrm(ks[26], (DEPTH, CONV_W, 2 * D_FF), CONV_W ** -0.5),
        'conv_b': nrm(ks[27], (DEPTH, 2 * D_FF), 0.01),
        'w_down': nrm(ks[28], (DEPTH, D_FF, D_MODEL), D_FF ** -0.5),
    }


def reference(x_prompt, x_sample, cache_sb_k, cache_sb_v, state_hgrn, state_ffn_conv,
              cache_mem_k, cache_mem_v, page_table, mem_prompt,
              w_in, hg_norm, hg_lb, sb_bias, w_o, g_mix_pre, g_mix_post, g_ca_pre, g_ca_post, g_mem,
              w_cq, w_ck, w_cv, w_co, g_ffn_pre, g_ffn_post, w_up, conv_w, conv_b, w_down):
    n_prompt, seq_len, _ = x_prompt.shape
    n_dec, dec_len, _ = x_sample.shape
    n_pages = page_table.shape[1]
    past_len = n_pages * PAGE_SIZE
    lower_bounds = jnp.cumsum(jax.nn.softmax(hg_lb.astype(jnp.float32), axis=0), axis=0)

    yp, ys = x_prompt, x_sample
    kp_l, vp_l, sp_l, bp_l, mkp_l, mvp_l = [], [], [], [], [], []
    ks_l, vs_l, ss_l, bs_l = [], [], [], []
    for l in range(DEPTH):
        p = {'w_in': w_in[l], 'hg_norm': hg_norm[l], 'sb_bias': sb_bias[l], 'w_o': w_o[l],
             'g_mix_pre': g_mix_pre[l], 'g_mix_post': g_mix_post[l],
             'g_ca_pre': g_ca_pre[l], 'g_ca_post': g_ca_post[l],
             'w_cq': w_cq[l], 'w_co': w_co[l],
             'g_ffn_pre': g_ffn_pre[l], 'g_ffn_post': g_ffn_post[l],
             'w_up': w_up[l], 'conv_w': conv_w[l], 'conv_b': conv_b[l], 'w_down': w_down[l]}
        lb = lower_bounds[l].reshape(HG_HEADS, HG_DK)

        mk_p, mv_p = memory_kv(mem_prompt, g_mem[l], w_ck[l], w_cv[l])
        s0_p = jnp.zeros((n_prompt, HG_HEADS, HG_DK, HG_DV), jnp.float32)
        buf_p = jnp.zeros((n_prompt, CONV_W - 1, 2 * D_FF), x_prompt.dtype)
        yp, s_p, k_p, v_p, nb_p = layer_forward(yp, p, lb, s0_p, None, None, buf_p,
                                                mk_p, mv_p, min(HG_CHUNK, seq_len))

        past_k = cache_sb_k[l][page_table].reshape(n_dec, past_len, SB_HEADS, SB_DH)
        past_v = cache_sb_v[l][page_table].reshape(n_dec, past_len, SB_HEADS, SB_DH)
        ys, s_s, k_s, v_s, nb_s = layer_forward(ys, p, lb, state_hgrn[l], past_k, past_v,
                                                state_ffn_conv[l], cache_mem_k[l], cache_mem_v[l],
                                                dec_len)

        kp_l.append(k_p); vp_l.append(v_p); sp_l.append(s_p); bp_l.append(nb_p)
        mkp_l.append(mk_p); mvp_l.append(mv_p)
        ks_l.append(k_s); vs_l.append(v_s); ss_l.append(s_s); bs_l.append(nb_s)

    return (yp, ys,
            jnp.stack(kp_l), jnp.stack(vp_l), jnp.stack(sp_l), jnp.stack(bp_l),
            jnp.stack(mkp_l), jnp.stack(mvp_l),
            jnp.stack(ks_l), jnp.stack(vs_l), jnp.stack(ss_l), jnp.stack(bs_l))
```

```python
import contextlib
import os
import types
import numpy as np
import concourse.bass as bass
import concourse.mybir as mybir
from concourse.bass_utils import run_bass_kernel_spmd

F32 = mybir.dt.float32
BF16 = mybir.dt.bfloat16
I32 = mybir.dt.int32
AF = mybir.ActivationFunctionType
ALU = mybir.AluOpType

NSP = 47
NSO = 17
NKB = NSP + NSO
D = 1024
DFF = 2816
EPS = 1e-6


class Reg:
    __slots__ = ("w", "rs", "name", "excl")

    def __init__(self, name=""):
        self.w = None
        self.rs = []
        self.name = name
        self.excl = False


class Buf:
    def __init__(self, t, name):
        self.t = t
        self.r = Reg(name)

    def __getitem__(self, k):
        return self.t[k]


class Fw:
    COMPUTE = ("pe", "act", "dve", "pool")

    def __init__(self, nc):
        self.nc = nc
        self.streams = {e: [] for e in ("pe", "act", "dve", "pool", "sp")}
        self.sems = {}
        self.semval = {}
        self.waited = {e: {} for e in self.streams}
        self._cms = []
        for e in self.COMPUTE:
            self._newsem("E_" + e)
        self.nd = 0

    def _newsem(self, key):
        cm = self.nc.semaphore(key)
        h = cm.__enter__()
        self._cms.append(cm)
        self.sems[key] = h
        self.semval[key] = 0
        return key

    def dmasem(self, name=None):
        self.nd += 1
        return self._newsem("D_%s_%d" % (name or "x", self.nd))

    def _wait(self, eng, toks):
        need = {}
        for t in toks:
            if t is None:
                continue
            k, v = t
            if v > need.get(k, 0):
                need[k] = v
        for k, v in need.items():
            if v > self.waited[eng].get(k, 0):
                self.waited[eng][k] = v
                self.streams[eng].append(("wait", k, v))

    @staticmethod
    def freeze(fn):
        if fn.__closure__ is None:
            return fn
        cells = []
        for c in fn.__closure__:
            try:
                cells.append(types.CellType(c.cell_contents))
            except ValueError:
                cells.append(c)
        return types.FunctionType(fn.__code__, fn.__globals__, fn.__name__, fn.__defaults__, tuple(cells))

    def op(self, eng, fn, reads=(), writes=(), dsem=None):
        fn = self.freeze(fn)
        toks = []
        own = "E_" + eng
        reads = [b.r if isinstance(b, Buf) else b for b in reads]
        writes = [b.r if isinstance(b, Buf) else b for b in writes]
        writes = writes + [r for r in reads if r.excl and r not in writes]
        for r in reads:
            toks.append(r.w)
        for w in writes:
            toks.append(w.w)
            toks.extend(w.rs)
        if eng == "pe":
            toks = [t for t in toks if t is not None and t[0] != own]
        self._wait(eng, toks)
        if dsem is not None:
            k, inc = dsem, 16
        else:
            k, inc = own, 1
        self.semval[k] += inc
        tok = (k, self.semval[k])
        self.streams[eng].append(("op", fn, k, inc))
        for w in writes:
            w.w = tok
            w.rs = []
        for r in reads:
            if r not in writes:
                r.rs.append(tok)
        return tok

    def barrier(self):
        allt = [(k, v) for k, v in self.semval.items() if v > 0]
        for e in self.streams:
            self._wait(e, allt)

    def replay(self):
        nc = self.nc
        names = {"pe": "tensor", "act": "scalar", "dve": "vector", "pool": "gpsimd", "sp": "sync"}
        with nc.Block() as block:
            for e, bn in names.items():
                items = self.streams[e]

                def body(engine, items=items):
                    for it in items:
                        if it[0] == "wait":
                            engine.wait_ge(self.sems[it[1]], it[2])
                        else:
                            it[1](engine).then_inc(self.sems[it[2]], it[3])
                getattr(block, bn)(body)

    def close(self):
        for cm in reversed(self._cms):
            cm.__exit__(None, None, None)


class Rot:
    def __init__(self, bufs):
        self.bufs = bufs
        self.i = 0

    def next(self):
        b = self.bufs[self.i % len(self.bufs)]
        self.i += 1
        return b


def build_program(n_pool=2560):
    nc = bass.Bass("TRN2", target_bir_lowering=False)
    fw = Fw(nc)
    es = contextlib.ExitStack()

    def din(name, shape, dt=F32):
        return Buf(nc.dram_tensor(name, list(shape), dt, kind="ExternalInput").ap(), name)

    def dout(name, shape, dt=F32):
        return Buf(nc.dram_tensor(name, list(shape), dt, kind="ExternalOutput").ap(), name)

    def dscr(name, shape, dt):
        return Buf(nc.dram_tensor(name, list(shape), dt, kind="Internal").ap(), name)

    def sb(name, shape, dt=F32, stack=None):
        return Buf((stack or es).enter_context(nc.sbuf_tensor(name, list(shape), dt)), name)

    def ps(name, shape, dt=F32, stack=None):
        b = Buf((stack or es).enter_context(nc.psum_tensor(name, list(shape), dt)), name)
        b.r.excl = True
        return b

    ARENA = 72 * 1024
    arena_t = es.enter_context(nc.sbuf_tensor("arena", [128, ARENA // 2], BF16))
    cur = [0]

    def phase(base):
        cur[0] = base

    def ar(name, shape, dt=F32):
        esz = 4 if dt == F32 else 2
        n = 1
        for d in shape[1:]:
            n *= d
        nb = (n * esz + 31) // 32 * 32
        off = cur[0]
        cur[0] += nb
        assert cur[0] <= ARENA, (name, cur[0])
        v = arena_t[0:shape[0], off // 2:(off + n * esz) // 2]
        if dt == F32:
            v = v.bitcast(F32)
        if len(shape) == 3:
            v = v.rearrange("p (a b) -> p a b", a=shape[1])
        elif len(shape) == 4:
            v = v.rearrange("p (a b c) -> p a b c", a=shape[1], b=shape[2])
        return Buf(v, name)

    xa = din("xa", [NSP * 128, D])
    xb = din("xb", [NSO * 128, D])
    kvalid = din("kvalid", [128, NKB])
    halo_valid = din("halo_valid", [128, 1])
    mem = din("mem", [256, D])
    w_in = din("w_in", [D, 3584])
    w_o = din("w_o", [D, D])
    w_cq = din("w_cq", [D, D])
    w_ck = din("w_ck", [D, D])
    w_cv = din("w_cv", [D, D])
    w_co = din("w_co", [D, D])
    w_up = din("w_up", [D, 2 * DFF])
    w_down = din("w_down", [DFF, D])
    gvecs = din("gvecs", [128, 7, 8])
    hgn = din("hgn", [64, 8])
    lbraw = din("lbraw", [128, 2, 512])
    sbb = din("sbb", [128, 8])
    convp = din("convp", [128, 4, 44])
    cst = din("cst", [128, 8, 128])
    dmask_d = din("dmask", [128, 4, 512])

    xs_d = din("xs", [128, D])
    csk = din("csk", [n_pool * 128, 512])
    csv = din("csv", [n_pool * 128, 512])
    pt_d = din("pt", [256], I32)
    iota_d = din("iotaf", [128, 1])
    sh_d = din("sh", [16, 8, 64, 64])
    sc_d = din("sc", [32, 2 * DFF])
    cmk = din("cmk", [16, 256, D])
    cmv = din("cmv", [16, 256, D])
    cst8 = din("cst8", [128, 4, 128])
    smask_d = din("smask", [128, 16, 8])
    ys_o = dout("ys", [128, D])
    ks_o = dout("ksout", [128, 512])
    vs_o = dout("vsout", [128, 512])
    hss_o = dout("hss", [16, 8, 64, 64])
    cvs_o = dout("cvs", [32, 2 * DFF])

    y_o = dout("y", [2048, D])
    k_o = dout("kout", [2048, 512])
    v_o = dout("vout", [2048, 512])
    hs_o = dout("hstate", [8, 64, 64])
    cv_o = dout("convout", [2, 2 * DFF])
    mk_o = dout("memk", [256, D])
    mv_o = dout("memv", [256, D])

    kT_scr = dscr("kT_scr", [8, 64, NKB * 128], BF16)
    v_scr = dscr("v_scr", [8, 128, NKB, 64], BF16)

    outs = [y_o, k_o, v_o, hs_o, cv_o, mk_o, mv_o]

    cstf = sb("cstf", [128, 8, 128])
    cstb = sb("cstb", [128, 8, 128], BF16)
    IDENT, TRILI, TRIUS, BDM, UINC, LSTR, ONES, CSEL = range(8)
    dmaskf = sb("dmaskf", [128, 4, 512], BF16)
    phase(0)
    dmask32 = ar("dmask32", [128, 4, 512])
    gv = sb("gv", [128, 7, 8])
    hgn_sb = sb("hgn_sb", [64, 8])
    lb = sb("lb", [128, 512])
    oml = sb("oml", [128, 512])
    lbtmp = ar("lbtmp", [128, 2, 512])
    sbb_sb = sb("sbb_sb", [128, 8])
    kval_sb = sb("kval_sb", [128, NKB])
    biasv = sb("biasv", [128, NKB, 8])
    halo_sb = sb("halo_sb", [128, 1])
    convp_sb = sb("convp_sb", [128, 4, 44])
    epsb = sb("epsb", [128, 1])
    S = sb("S", [64, 8, 64])
    S16 = sb("S16", [64, 8, 64], BF16)
    Sb16 = sb("Sb16", [64, 8, 64], BF16)
    ubuf = sb("ubuf", [128, 44, 2])

    c8f = sb("c8f", [128, 4, 128])
    c8b = sb("c8b", [128, 4, 128], BF16)
    smask = sb("smask_sb", [128, 16, 8])
    expb = sb("expb", [128, 8])
    iotaf = sb("iotaf_sb", [128, 1])
    ptb = sb("ptb", [128, 256], I32)
    ptf = sb("ptf", [128, 256])
    idx = sb("idx", [128, 256], I32)

    def E(eng, fn, reads=(), writes=()):
        return fw.op(eng, fn, reads, writes)

    def dma(eng, out_ap, in_ap, reads, writes, sem, **kw):
        return fw.op(eng, lambda e: e.dma_start(out=out_ap, in_=in_ap, **kw), reads, writes, dsem=sem)

    dma("sp", cstf[:], cst[:], [cst], [cstf], fw.dmasem("setup"))
    dma("sp", dmask32[:], dmask_d[:], [dmask_d], [dmask32], fw.dmasem("setup"))
    dma("sp", gv[:], gvecs[:], [gvecs], [gv], fw.dmasem("setup"))
    dma("sp", hgn_sb[:], hgn[:], [hgn], [hgn_sb], fw.dmasem("setup"))
    dma("sp", lbtmp[:], lbraw[:], [lbraw], [lbtmp], fw.dmasem("setup"))
    dma("sp", sbb_sb[:], sbb[:], [sbb], [sbb_sb], fw.dmasem("setup"))
    dma("sp", kval_sb[:], kvalid[:], [kvalid], [kval_sb], fw.dmasem("setup"))
    dma("sp", halo_sb[:], halo_valid[:], [halo_valid], [halo_sb], fw.dmasem("setup"))
    dma("sp", convp_sb[:], convp[:], [convp], [convp_sb], fw.dmasem("setup"))
    dma("sp", c8f[:], cst8[:], [cst8], [c8f], fw.dmasem("setup"))
    dma("sp", smask[:], smask_d[:], [smask_d], [smask], fw.dmasem("setup"))
    dma("sp", iotaf[:], iota_d[:], [iota_d], [iotaf], fw.dmasem("setup"))
    dma("sp", ptb[:], pt_d[:].partition_broadcast(128), [pt_d], [ptb], fw.dmasem("setup"))
    E("dve", lambda e: e.tensor_copy(out=c8b[:], in_=c8f[:]), [c8f], [c8b])
    E("act", lambda e: e.activation(out=expb[:], in_=sbb_sb[:], func=AF.Exp), [sbb_sb], [expb])
    E("dve", lambda e: e.tensor_copy(out=ptf[:], in_=ptb[:]), [ptb], [ptf])
    E("dve", lambda e: e.tensor_scalar(out=ptf[:], in0=ptf[:], scalar1=128.0, scalar2=iotaf[:, 0:1], op0=ALU.mult, op1=ALU.add), [ptf, iotaf], [ptf])
    E("dve", lambda e: e.tensor_copy(out=idx[:], in_=ptf[:]), [ptf], [idx])
    E("dve", lambda e: e.tensor_copy(out=cstb[:], in_=cstf[:]), [cstf], [cstb])
    E("dve", lambda e: e.tensor_copy(out=dmaskf[:], in_=dmask32[:]), [dmask32], [dmaskf])
    E("pool", lambda e: e.memset(epsb[:], EPS), [], [epsb])
    E("pool", lambda e: e.memset(S[:], 0.0), [], [S])
    E("pool", lambda e: e.memset(S16[:], 0.0), [], [S16])
    E("pool", lambda e: e.memset(ubuf[:], 0.0), [], [ubuf])
    E("dve", lambda e: e.tensor_tensor(out=lb[:], in0=lbtmp[:, 0, :], in1=lbtmp[:, 1, :], op=ALU.subtract), [lbtmp], [lb])
    E("act", lambda e: e.activation(out=lb[:], in_=lb[:], func=AF.Sigmoid), [lb], [lb])
    E("dve", lambda e: e.tensor_scalar(out=oml[:], in0=lb[:], scalar1=-1.0, scalar2=1.0, op0=ALU.mult, op1=ALU.add), [lb], [oml])
    E("dve", lambda e: e.tensor_tensor(out=biasv[:], in0=kval_sb[:].unsqueeze(2).to_broadcast([128, NKB, 8]),
                                       in1=sbb_sb[:].unsqueeze(1).to_broadcast([128, NKB, 8]), op=ALU.add),
      [kval_sb, sbb_sb], [biasv])

    P = [ps("P%d" % i, [128, 512]) for i in range(6)]
    PT = [ps("PT%d" % i, [128, 1024], BF16) for i in range(2)]

    wst_rot = Rot([sb("wst%d" % i, [128, 8, 512]) for i in range(1)])
    wbf_rot = Rot([sb("wbf%d" % i, [128, 8, 512], BF16) for i in range(2)])
    wsem = [fw.dmasem("w") for _ in range(1)]
    wcnt = [0]

    def load_w(wd, c0, ncols, kp=128, kc=8, r0=0):
        i = 0
        st = wst_rot.next()
        wb = wbf_rot.next()
        src = wd[r0:r0 + kc * kp, c0:c0 + ncols].rearrange("(c p) n -> p c n", p=kp)
        if kc * ncols <= 8 * 512:
            stv = st.t[0:kp].rearrange("p a b -> p (a b)")[:, 0:kc * ncols].rearrange("p (c n) -> p c n", c=kc)
            wbv = wb.t[0:kp].rearrange("p a b -> p (a b)")[:, 0:kc * ncols].rearrange("p (c n) -> p c n", c=kc)
        else:
            raise ValueError("weight block too large")
        dma("sp", stv, src, [wd], [st], wsem[i])
        E("pool", lambda e: e.tensor_copy(out=wbv, in_=stv), [st], [wb])
        return wb, wbv

    sqb = sb("sqb", [128, 8, 512], BF16)
    rstd = sb("rstd", [128, 512])

    def rms_T(srcT, nt, scale_n):
        for c in range(8):
            E("act", lambda e, c=c: e.activation(out=sqb[:, c, 0:nt], in_=srcT[:, c, 0:nt], func=AF.Square), [srcT], [sqb])
        pb = P[5]
        for c in range(8):
            E("pe", lambda e, c=c: e.matmul(pb[:, 0:nt], lhsT=cstb[:, ONES, :], rhs=sqb[:, c, 0:nt], start=(c == 0), stop=(c == 7)),
              [cstb, sqb], [pb])
        E("act", lambda e: e.activation(out=rstd[:, 0:nt], in_=pb[:, 0:nt], func=AF.Ln, scale=1.0 / scale_n, bias=epsb[:]), [pb, epsb], [rstd])
        E("act", lambda e: e.activation(out=rstd[:, 0:nt], in_=rstd[:, 0:nt], func=AF.Exp, scale=-0.5), [rstd], [rstd])

    def prenorm(xT, hT, nt, gi):
        rms_T(xT, nt, 1024.0)
        for c in range(8):
            E("dve", lambda e, c=c: e.scalar_tensor_tensor(out=hT[:, c, 0:nt], in0=xT[:, c, 0:nt], scalar=gv[:, gi, c:c + 1],
                                                            in1=rstd[:, 0:nt], op0=ALU.mult, op1=ALU.mult), [xT, gv, rstd], [hT])

    phase(0)
    brT = ar("brT", [128, 8, 512])

    def postnorm_add(xT, nt, gi):
        rms_T(brT, nt, 1024.0)
        for c in range(8):
            E("dve", lambda e, c=c: e.tensor_tensor(out=brT[:, c, 0:nt], in0=brT[:, c, 0:nt], in1=rstd[:, 0:nt], op=ALU.mult), [brT, rstd], [brT])
            E("dve", lambda e, c=c: e.scalar_tensor_tensor(out=xT[:, c, 0:nt], in0=brT[:, c, 0:nt], scalar=gv[:, gi, c:c + 1],
                                                            in1=xT[:, c, 0:nt], op0=ALU.mult, op1=ALU.add), [brT, gv, xT], [xT])

    prot = Rot([P[0], P[1]])

    def linear_fm_to_brT(inT, kp, kc, wd, nt):
        for q in range(2):
            wb, wbv = load_w(wd, q * 512, 512)
            for b4 in range(4):
                blk = q * 4 + b4
                pb = prot.next()
                for k in range(8):
                    E("pe", lambda e: e.matmul(pb[:, 0:nt], lhsT=wbv[:, k, b4 * 128:(b4 + 1) * 128], rhs=inT[:, k, 0:nt], start=(k == 0), stop=(k == 7)), [wb, inT], [pb])
                E("act", lambda e: e.activation(out=brT[:, blk, 0:nt], in_=pb[:, 0:nt], func=AF.Copy), [pb], [brT])

    phase(0)
    xt_rot = Rot([ar("xt%d" % i, [128, D]) for i in range(2)])
    xsem = [fw.dmasem("x") for _ in range(2)]
    xcnt = [0]
    xhi = ar("xhi", [128, D], BF16)
    xlo = ar("xlo", [128, D], BF16)
    xtmp = ar("xtmp", [128, 8, 128])

    def load_xT(src, row0, xT, col0):
        i = xcnt[0] % 2
        xcnt[0] += 1
        xt = xt_rot.next()
        dma("sp", xt[:], src[row0:row0 + 128, :], [src], [xt], xsem[i])
        E("dve", lambda e: e.tensor_copy(out=xhi[:], in_=xt[:]), [xt], [xhi])
        E("pool", lambda e: e.tensor_tensor(out=xlo[:], in0=xt[:], in1=xhi[:], op=ALU.subtract), [xt, xhi], [xlo])
        for c in range(8):
            E("pe", lambda e, c=c: e.transpose(out=PT[0][:, c * 128:(c + 1) * 128], in_=xhi[:, c * 128:(c + 1) * 128], identity=cstb[:, IDENT, :]),
              [xhi, cstb], [PT[0]])
        for c in range(8):
            E("pe", lambda e, c=c: e.transpose(out=PT[1][:, c * 128:(c + 1) * 128], in_=xlo[:, c * 128:(c + 1) * 128], identity=cstb[:, IDENT, :]),
              [xlo, cstb], [PT[1]])
        E("act", lambda e: e.activation(out=xtmp[:], in_=PT[0][:].rearrange("p (c n) -> p c n", c=8), func=AF.Copy), [PT[0]], [xtmp])
        E("dve", lambda e: e.tensor_tensor(out=xT[:, :, col0:col0 + 128], in0=xtmp[:], in1=PT[1][:].rearrange("p (c n) -> p c n", c=8), op=ALU.add),
          [xtmp, PT[1]], [xT])

    xT = sb("xT", [128, 8, 512])
    hT = sb("hT", [128, 8, 512], BF16)
    phase(0)
    sqT = ar("sqT", [64, 8, 512], BF16)
    qT = ar("qT", [64, 8, 512], BF16)
    gT = ar("gT", [64, 8, 512], BF16)
    f_sb = ar("f_sb", [128, 512])
    lf = ar("lf", [128, 512])
    lfh = ar("lfh", [128, 512], BF16)
    lfl = ar("lfl", [128, 512], BF16)
    k16 = ar("k16", [128, 512], BF16)
    kdd = ar("kdd", [128, 512], BF16)
    eD = ar("eD", [128, 512])
    v16 = ar("v16", [128, 512], BF16)
    vm = ar("vm", [128, 8, 2, 64], BF16)
    ebT = ar("ebT", [64, 8, 128])
    enbT = ar("enbT", [64, 8, 128])
    ebl = ar("ebl", [64, 8, 2])
    qtT = ar("qtT", [64, 8, 128], BF16)
    ktT = ar("ktT", [64, 8, 128], BF16)
    attm = ar("attm", [128, 8, 128], BF16)
    stmp = ar("stmp", [64, 8, 64])
    osq = ar("osq", [64, 8, 128], BF16)
    orst = enbT
    otmp = ebT
    mixT = sb("mixT", [64, 16, 512], BF16)
    kvout = ar("kvout", [128, 512])
    kvsem = fw.dmasem("kv")
    scrsem = fw.dmasem("scrk")
    scrsemV = fw.dmasem("scrv")

    def tm_proj(wbv, wb, s, pb):
        for k in range(8):
            E("pe", lambda e, k=k: e.matmul(pb[:], lhsT=hT[:, k, s * 128:(s + 1) * 128], rhs=wbv[:, k, :], start=(k == 0), stop=(k == 7)), [hT, wb], [pb])

    def fm_proj64(wbv, wb, h, nt, pb):
        for k in range(8):
            E("pe", lambda e, k=k: e.matmul(pb[0:64, 0:nt], lhsT=wbv[:, k, h * 64:(h + 1) * 64], rhs=hT[:, k, 0:nt], start=(k == 0), stop=(k == 7)), [wb, hT], [pb])

    def hgrn_subtile(s, own, hf_ps, hi_ps, smp=False):
        tus = c8b[:, 1, :] if smp else cstb[:, TRIUS, :]
        tli = c8b[:, 0, :] if smp else cstb[:, TRILI, :]
        bdm = c8f[:, 2, :] if smp else cstf[:, BDM, :]
        cbuf = c8b if smp else cstb
        cfbuf = c8f if smp else cstf
        E("act", lambda e: e.activation(out=f_sb[:], in_=hf_ps[:], func=AF.Sigmoid), [hf_ps], [f_sb])
        E("dve", lambda e: e.tensor_tensor(out=f_sb[:], in0=f_sb[:], in1=oml[:], op=ALU.mult), [f_sb, oml], [f_sb])
        E("dve", lambda e: e.tensor_tensor(out=f_sb[:], in0=f_sb[:], in1=lb[:], op=ALU.add), [f_sb, lb], [f_sb])
        E("act", lambda e: e.activation(out=lf[:], in_=f_sb[:], func=AF.Ln), [f_sb], [lf])
        E("dve", lambda e: e.tensor_scalar(out=k16[:], in0=f_sb[:], scalar1=-1.0, scalar2=1.0, op0=ALU.mult, op1=ALU.add), [f_sb], [k16])
        E("dve", lambda e: e.tensor_copy(out=lfh[:], in_=lf[:]), [lf], [lfh])
        E("pool", lambda e: e.tensor_tensor(out=lfl[:], in0=lf[:], in1=lfh[:], op=ALU.subtract), [lf, lfh], [lfl])
        E("act", lambda e: e.activation(out=v16[:], in_=hi_ps[:], func=AF.Copy), [hi_ps], [v16])
        pd = P[4]
        E("pe", lambda e: e.matmul(pd[:], lhsT=tus, rhs=lfh[:], start=True, stop=False), [cbuf, lfh], [pd])
        E("pe", lambda e: e.matmul(pd[:], lhsT=tus, rhs=lfl[:], start=False, stop=True), [cbuf, lfl], [pd])
        E("act", lambda e: e.activation(out=eD[:], in_=pd[:], func=AF.Exp), [pd], [eD])
        E("dve", lambda e: e.tensor_tensor(out=kdd[:], in0=k16[:], in1=eD[:], op=ALU.mult), [k16, eD], [kdd])
        if not smp:
            E("pool", lambda e: e.tensor_tensor(out=vm[:], in0=v16[:].rearrange("p (h v) -> p h v", h=8).unsqueeze(2).to_broadcast([128, 8, 2, 64]),
                                                in1=cstb[:, CSEL, 0:2].unsqueeze(1).unsqueeze(3).to_broadcast([128, 8, 2, 64]), op=ALU.mult),
              [v16, cstb], [vm])
        pbt = P[2], P[3]
        for h in range(8):
            pb_ = pbt[h // 4]
            o = pb_[0:64, (h % 4) * 128:(h % 4 + 1) * 128]
            E("pe", lambda e, h=h, o=o: e.matmul(o, lhsT=lfh[:, h * 64:(h + 1) * 64], rhs=tli, start=True, stop=False), [lfh, cbuf], [pb_])
            E("pe", lambda e, h=h, o=o: e.matmul(o, lhsT=lfl[:, h * 64:(h + 1) * 64], rhs=tli, start=False, stop=True), [lfl, cbuf], [pb_])
        for half in range(2):
            pb_ = pbt[half]
            E("act", lambda e, half=half, pb_=pb_: e.activation(out=ebT[:, half * 4:(half + 1) * 4, :], in_=pb_[0:64, :].rearrange("p (h t) -> p h t", h=4), func=AF.Exp),
              [pb_], [ebT])
            if own:
                E("act", lambda e, half=half, pb_=pb_: e.activation(out=enbT[:, half * 4:(half + 1) * 4, :], in_=pb_[0:64, :].rearrange("p (h t) -> p h t", h=4), func=AF.Exp, scale=-1.0),
                  [pb_], [enbT])
        if smp:
            E("dve", lambda e: e.tensor_copy(out=ebl16[:], in_=ebT[:].rearrange("p h (c t) -> p h c t", c=16)[:, :, :, 7]), [ebT], [ebl16])
        else:
            E("dve", lambda e: e.tensor_copy(out=ebl[:], in_=ebT[:].rearrange("p h (c t) -> p h c t", c=2)[:, :, :, 63]), [ebT], [ebl])
        if own:
            E("dve", lambda e: e.tensor_tensor(out=qtT[:], in0=qT[:, :, s * 128:(s + 1) * 128], in1=ebT[:], op=ALU.mult), [qT, ebT], [qtT])
            for h in range(8):
                E("pe", lambda e, h=h: e.transpose(out=PT[0][0:64, h * 128:(h + 1) * 128], in_=k16[:, h * 64:(h + 1) * 64], identity=cstb[:, IDENT, :]), [k16, cstb], [PT[0]])
            E("dve", lambda e: e.tensor_tensor(out=ktT[:], in0=PT[0][0:64, :].rearrange("p (h t) -> p h t", h=8), in1=enbT[:], op=ALU.mult), [PT[0], enbT], [ktT])
            pat = P[2], P[3]
            for h in range(8):
                pb_ = pat[h // 4]
                E("pe", lambda e, h=h, pb_=pb_: e.matmul(pb_[:, (h % 4) * 128:(h % 4 + 1) * 128], lhsT=ktT[:, h, :], rhs=qtT[:, h, :], start=True, stop=True), [ktT, qtT], [pb_])
            for half in range(2):
                pb_ = pat[half]
                E("dve", lambda e, half=half, pb_=pb_: e.tensor_tensor(out=attm[:, half * 4:(half + 1) * 4, :], in0=pb_[:].rearrange("p (h t) -> p h t", h=4),
                                                                      in1=bdm.unsqueeze(1).to_broadcast([128, 4, 128]), op=ALU.mult), [pb_, cfbuf], [attm])
        po = P[2], P[3]
        if smp:
            for h in range(8):
                s0f = s0f_rot.next()
                s16h = s16h_rot.next()
                vmh = vmh_rot.next()
                dma("sp", s0f[:], sh_d[:, h, :, :].rearrange("s k v -> k s v"), [sh_d], [s0f], s0f.sem)
                E("pool", lambda e: e.tensor_copy(out=s16h[:], in_=s0f[:]), [s0f], [s16h])
                E("pool", lambda e: e.tensor_tensor(out=vmh[:], in0=v16[:, h * 64:(h + 1) * 64].unsqueeze(1).to_broadcast([128, 16, 64]),
                                                    in1=c8b[:, 3, 0:16].unsqueeze(2).to_broadcast([128, 16, 64]), op=ALU.mult), [v16, c8b], [vmh])
                for half in range(2):
                    pb_ = P[half]
                    E("pe", lambda e: e.matmul(pb_[0:64, :], lhsT=kdd[:, h * 64:(h + 1) * 64], rhs=vmh[:, half * 8:(half + 1) * 8, :].rearrange("p c v -> p (c v)"),
                                               start=True, stop=True), [kdd, vmh], [pb_])
                E("dve", lambda e: e.tensor_tensor(out=stmp16[:], in0=s0f[:], in1=ebl16[:, h, :].unsqueeze(2).to_broadcast([64, 16, 64]), op=ALU.mult), [s0f, ebl16], [stmp16])
                for half in range(2):
                    pb_ = P[half]
                    E("dve", lambda e: e.tensor_tensor(out=s0f[:, half * 8:(half + 1) * 8, :], in0=stmp16[:, half * 8:(half + 1) * 8, :],
                                                       in1=pb_[0:64, :].rearrange("p (c v) -> p c v", c=8), op=ALU.add), [stmp16, pb_], [s0f])
                dma("pool", hss_o[:, h, :, :].rearrange("s k v -> k s v"), s0f[:], [s0f], [hss_o], s0f.sem2)
                pb_ = po[h // 4]
                c0 = (h % 4) * 128
                E("pe", lambda e: e.matmul(pb_[0:64, c0:c0 + 128], lhsT=v16[:, h * 64:(h + 1) * 64], rhs=attm[:, h, :], start=True, stop=False), [v16, attm], [pb_])
                for c in range(16):
                    E("pe", lambda e: e.matmul(pb_[0:64, c0 + c * 8:c0 + c * 8 + 8], lhsT=s16h[:, c, :], rhs=qtT[:, h, c * 8:c * 8 + 8], start=False, stop=(c == 15)),
                      [s16h, qtT], [pb_])
        else:
            pp = P[0], P[1]
            for h in range(8):
                pb_ = pp[h // 4]
                E("pe", lambda e, h=h, pb_=pb_: e.matmul(pb_[0:64, (h % 4) * 128:(h % 4 + 1) * 128], lhsT=kdd[:, h * 64:(h + 1) * 64], rhs=vm[:, h, :, :].rearrange("p c v -> p (c v)"),
                                                         start=True, stop=True), [kdd, vm], [pb_])
            po = P[2], P[3]

            def chain(c, dst16):
                E("dve", lambda e: e.tensor_tensor(out=stmp[:], in0=S[:], in1=ebl[:, :, c:c + 1].to_broadcast([64, 8, 64]), op=ALU.mult), [S, ebl], [stmp])
                for half in range(2):
                    pb_ = pp[half]
                    E("dve", lambda e: e.tensor_tensor(out=S[:, half * 4:(half + 1) * 4, :], in0=stmp[:, half * 4:(half + 1) * 4, :],
                                                       in1=pb_[0:64, :].rearrange("p (h c v) -> p h c v", h=4, c=2)[:, :, c, :], op=ALU.add), [stmp, pb_], [S])
                E("pool", lambda e: e.tensor_copy(out=dst16[:], in_=S[:]), [S], [dst16])

            chain(0, Sb16)
            if own:
                for h in range(8):
                    pb_ = po[h // 4]
                    c0 = (h % 4) * 128
                    E("pe", lambda e: e.matmul(pb_[0:64, c0:c0 + 128], lhsT=v16[:, h * 64:(h + 1) * 64], rhs=attm[:, h, :], start=True, stop=False), [v16, attm], [pb_])
                    E("pe", lambda e: e.matmul(pb_[0:64, c0:c0 + 64], lhsT=S16[:, h, :], rhs=qtT[:, h, 0:64], start=False, stop=False), [S16, qtT], [pb_])
                    E("pe", lambda e: e.matmul(pb_[0:64, c0 + 64:c0 + 128], lhsT=Sb16[:, h, :], rhs=qtT[:, h, 64:128], start=False, stop=True), [Sb16, qtT], [pb_])
            chain(1, S16)
        if own:
            for half in range(2):
                pb_ = po[half]
                E("act", lambda e, half=half, pb_=pb_: e.activation(out=osq[:, half * 4:(half + 1) * 4, :], in_=pb_[0:64, :].rearrange("p (h t) -> p h t", h=4), func=AF.Square), [pb_], [osq])
            pr = P[0], P[1]
            for half in range(2):
                pb_ = pr[half]
                E("pe", lambda e, half=half, pb_=pb_: e.matmul(pb_[0:64, :], lhsT=cstb[0:64, ONES, 0:64], rhs=osq[:, half * 4:(half + 1) * 4, :].rearrange("p h t -> p (h t)"),
                                                               start=True, stop=True), [cstb, osq], [pb_])
                E("act", lambda e, half=half, pb_=pb_: e.activation(out=orst[:, half * 4:(half + 1) * 4, :], in_=pb_[0:64, :].rearrange("p (h t) -> p h t", h=4), func=AF.Ln,
                                                                    scale=1.0 / 64, bias=epsb[0:64, :]), [pb_, epsb], [orst])
            E("act", lambda e: e.activation(out=orst[:], in_=orst[:], func=AF.Exp, scale=-0.5), [orst], [orst])
            for half in range(2):
                pb_ = po[half]
                E("dve", lambda e, half=half, pb_=pb_: e.tensor_tensor(out=otmp[:, half * 4:(half + 1) * 4, :], in0=pb_[0:64, :].rearrange("p (h t) -> p h t", h=4),
                                                                      in1=orst[:, half * 4:(half + 1) * 4, :], op=ALU.mult), [pb_, orst], [otmp])
            E("dve", lambda e: e.tensor_tensor(out=otmp[:], in0=otmp[:], in1=hgn_sb[:].unsqueeze(2).to_broadcast([64, 8, 128]), op=ALU.mult), [otmp, hgn_sb], [otmp])
            E("dve", lambda e: e.tensor_tensor(out=mixT[:, 0:8, s * 128:(s + 1) * 128], in0=otmp[:], in1=gT[:, :, s * 128:(s + 1) * 128], op=ALU.mult), [otmp, gT], [mixT])

    C_HQ, C_HF, C_HI, C_HG, C_SQ, C_SK, C_SV = 0, 512, 1024, 1536, 2048, 2560, 3072
    kst = ar("kst", [64, 8, 512], BF16)

    def token_mix_proj(nsub, kb0, own, out_row0, kdst=None, vdst=None, smp=False):
        kdst = kdst or k_o
        vdst = vdst or v_o
        nt = nsub * 128
        wb, wbv = load_w(w_in, C_SK, 512)
        for h in range(8):
            pb = prot.next()
            fm_proj64(wbv, wb, h, nt, pb)
            E("act", lambda e, h=h, pb=pb: e.activation(out=kst[:, h, 0:nt], in_=pb[0:64, 0:nt], func=AF.Copy), [pb], [kst])
        if not smp:
            dma("pool", kT_scr[:, :, kb0 * 128:kb0 * 128 + nt].rearrange("h d t -> d h t"), kst[:, :, 0:nt], [kst], [kT_scr], scrsem)
        if own and out_row0 is not None:
            for s in range(nsub):
                pb = prot.next()
                tm_proj(wbv, wb, s, pb)
                E("act", lambda e, pb=pb: e.activation(out=kvout[:], in_=pb[:], func=AF.Copy), [pb], [kvout])
                dma("pool", kdst[out_row0 + s * 128:out_row0 + (s + 1) * 128, :], kvout[:], [kvout], [kdst], kvsem)
        wb, wbv = load_w(w_in, C_SV, 512)
        for s in range(nsub):
            pb = prot.next()
            tm_proj(wbv, wb, s, pb)
            E("act", lambda e, pb=pb: e.activation(out=v16[:], in_=pb[:], func=AF.Copy), [pb], [v16])
            if smp:
                E("pool", lambda e: e.tensor_copy(out=svs16[:], in_=v16[:]), [v16], [svs16])
            else:
                dma("pool", v_scr[:, :, kb0 + s, :].rearrange("h p v -> p h v"), v16[:].rearrange("p (h v) -> p h v", h=8), [v16], [v_scr], scrsemV)
            if own and out_row0 is not None:
                E("dve", lambda e, pb=pb: e.tensor_copy(out=kvout[:], in_=pb[:]), [pb], [kvout])
                dma("pool", vdst[out_row0 + s * 128:out_row0 + (s + 1) * 128, :], kvout[:], [kvout], [vdst], kvsem)
        if own:
            wb, wbv = load_w(w_in, C_HQ, 512)
            for h in range(8):
                pb = prot.next()
                fm_proj64(wbv, wb, h, nt, pb)
                E("act", lambda e, h=h, pb=pb: e.activation(out=qT[:, h, 0:nt], in_=pb[0:64, 0:nt], func=AF.Copy), [pb], [qT])
            wb, wbv = load_w(w_in, C_HG, 512)
            for h in range(8):
                pb = prot.next()
                fm_proj64(wbv, wb, h, nt, pb)
                E("act", lambda e, h=h, pb=pb: e.activation(out=gT[:, h, 0:nt], in_=pb[0:64, 0:nt], func=AF.Silu), [pb], [gT])
            wb, wbv = load_w(w_in, C_SQ, 512)
            for h in range(8):
                pb = prot.next()
                fm_proj64(wbv, wb, h, nt, pb)
                E("act", lambda e, h=h, pb=pb: e.activation(out=sqT[:, h, 0:nt], in_=pb[0:64, 0:nt], func=AF.Copy), [pb], [sqT])
        wbf_, wbfv = load_w(w_in, C_HF, 512)
        wbi_, wbiv = load_w(w_in, C_HI, 512)
        for s in range(nsub):
            p_hf, p_hi = P[0], P[1]
            tm_proj(wbfv, wbf_, s, p_hf)
            tm_proj(wbiv, wbi_, s, p_hi)
            hgrn_subtile(s, own, p_hf, p_hi, smp)

    phase(8 * 1024)
    kT_rot = Rot([ar("kTh%d" % i, [64, NKB * 128], BF16) for i in range(1)])
    vh_rot = Rot([ar("vh%d" % i, [128, NKB, 64], BF16) for i in range(1)])
    kvh_sem = [fw.dmasem("kvhk"), fw.dmasem("kvhv")]
    kvh_cnt = [0]
    e_rot = Rot([ar("e_sb%d" % i, [128, 512]) for i in range(3)])
    sp_rot = Rot([ar("sp16_%d" % i, [128, 512], BF16) for i in range(3)])
    g_rot = Rot([ar("g_sb%d" % i, [128, 512]) for i in range(2)])
    a_rot = Rot([ar("a16_%d" % i, [128, 512], BF16) for i in range(2)])
    z_rot = Rot([P[0], P[1]])

    def sb_attention(nq, kb_hi, kb_diag0, qcol0):
        for h in range(8):
            i = 0
            kvh_cnt[0] += 1
            kTh = kT_rot.next()
            vh = vh_rot.next()
            nk = kb_hi * 128
            dma("sp", kTh[:, 0:nk], kT_scr[h, :, 0:nk], [kT_scr], [kTh], kvh_sem[0])
            dma("sp", vh[:, 0:kb_hi, :], v_scr[h, :, 0:kb_hi, :], [v_scr], [vh], kvh_sem[1])
            pc, po_ = P[2], P[3]
            order = list(range(kb_hi - 1, -1, -1))
            st1 = {}

            def stage1(kb):
                pz = z_rot.next()
                E("pe", lambda e: e.matmul(pz[:, 0:nq], lhsT=kTh[:, kb * 128:(kb + 1) * 128], rhs=sqT[:, h, 0:nq], start=True, stop=True), [kTh, sqT], [pz])
                eb_ = e_rot.next()
                E("act", lambda e: e.activation(out=eb_[:, 0:nq], in_=pz[:, 0:nq], func=AF.Exp, scale=0.125, bias=biasv[:, kb, h:h + 1]), [pz, biasv], [eb_])
                if kb >= kb_diag0:
                    E("pool", lambda e: e.tensor_tensor(out=eb_[:, 0:nq], in0=eb_[:, 0:nq], in1=dmaskf[:, kb - kb_diag0, 0:nq], op=ALU.mult), [eb_, dmaskf], [eb_])
                sp_ = sp_rot.next()
                E("act", lambda e: e.activation(out=sp_[:, 0:nq], in_=eb_[:, 0:nq], func=AF.Ln, bias=1.0), [eb_], [sp_])
                st1[kb] = (eb_, sp_)

            def stage2(idx):
                kb = order[idx]
                eb_, sp_ = st1[kb]
                if idx > 0:
                    spp = st1[order[idx - 1]][1]
                    E("pe", lambda e: e.matmul(pc[:, 0:nq], lhsT=cstb[:, LSTR, :], rhs=spp[:, 0:nq], start=False, stop=False), [cstb, spp], [pc])
                E("pe", lambda e: e.matmul(pc[:, 0:nq], lhsT=cstb[:, UINC, :], rhs=sp_[:, 0:nq], start=(idx == 0), stop=(idx == len(order) - 1)), [cstb, sp_], [pc])
                g_ = g_rot.next()
                E("act", lambda e: e.activation(out=g_[:, 0:nq], in_=pc[:, 0:nq], func=AF.Exp, scale=-1.0), [pc], [g_])
                a_ = a_rot.next()
                E("dve", lambda e: e.tensor_tensor(out=a_[:, 0:nq], in0=eb_[:, 0:nq], in1=g_[:, 0:nq], op=ALU.mult), [eb_, g_], [a_])
                E("pe", lambda e: e.matmul(po_[0:64, 0:nq], lhsT=vh[:, kb, :], rhs=a_[:, 0:nq], start=(idx == 0), stop=(idx == len(order) - 1)), [vh, a_], [po_])
                if idx > 0:
                    del st1[order[idx - 1]]

            stage1(order[0])
            for idx in range(len(order)):
                if idx + 1 < len(order):
                    stage1(order[idx + 1])
                stage2(idx)
            E("act", lambda e: e.activation(out=mixT[:, 8 + h, qcol0:qcol0 + nq], in_=po_[0:64, 0:nq], func=AF.Copy), [po_], [mixT])

    mkT = sb("mkT", [128, 8, 256], BF16)
    mv16 = sb("mv16", [128, 2, D], BF16)
    phase(16 * 1024)
    qcT = ar("qcT", [128, 8, 512], BF16)
    pT16 = ar("pT16", [128, 2, 512], BF16)
    rden = ar("rden", [128, 512])
    ocT = ar("ocT", [128, 8, 512], BF16)
    memout = ar("memout", [128, D])
    memsem = fw.dmasem("mem")

    def memory_kv():
        KMK = int(os.environ.get("KMK", "9"))
        for s in range(2):
            load_xT(mem, s * 128, xT, s * 128)
        if KMK < 2:
            return
        prenorm(xT, hT, 256, 4)
        if KMK < 3:
            return
        for q in range(2):
            wb, wbv = load_w(w_ck, q * 512, 512)
            if KMK < 4:
                continue
            for b4 in range(4):
                blk = q * 4 + b4
                pb = prot.next()
                for k in range(8):
                    E("pe", lambda e, k=k, b4=b4, pb=pb: e.matmul(pb[:, 0:256], lhsT=wbv[:, k, b4 * 128:(b4 + 1) * 128], rhs=hT[:, k, 0:256], start=(k == 0), stop=(k == 7)), [wb, hT], [pb])
                E("act", lambda e, blk=blk, pb=pb: e.activation(out=mkT[:, blk, :], in_=pb[:, 0:256], func=AF.Copy), [pb], [mkT])
            if KMK < 5:
                continue
            for s in range(2):
                pb = prot.next()
                tm_proj(wbv, wb, s, pb)
                E("act", lambda e, pb=pb: e.activation(out=memout[:, q * 512:(q + 1) * 512], in_=pb[:], func=AF.Copy), [pb], [memout])
                dma("pool", mk_o[s * 128:(s + 1) * 128, q * 512:(q + 1) * 512], memout[:, q * 512:(q + 1) * 512], [memout], [mk_o], memsem)
        if KMK < 6:
            return
        for q in range(2):
            wb, wbv = load_w(w_cv, q * 512, 512)
            for s in range(2):
                pb = prot.next()
                tm_proj(wbv, wb, s, pb)
                E("act", lambda e, pb=pb: e.activation(out=memout[:, q * 512:(q + 1) * 512], in_=pb[:], func=AF.Copy), [pb], [memout])
                E("dve", lambda e, pb=pb, s=s: e.tensor_copy(out=mv16[:, s, q * 512:(q + 1) * 512], in_=pb[:]), [pb], [mv16])
                dma("pool", mv_o[s * 128:(s + 1) * 128, q * 512:(q + 1) * 512], memout[:, q * 512:(q + 1) * 512], [memout], [mv_o], memsem)

    def cross_attn(nt):
        prenorm(xT, hT, nt, 2)
        for q in range(2):
            wb, wbv = load_w(w_cq, q * 512, 512)
            for b4 in range(4):
                blk = q * 4 + b4
                pb = prot.next()
                for k in range(8):
                    E("pe", lambda e, k=k, b4=b4, pb=pb: e.matmul(pb[:, 0:nt], lhsT=wbv[:, k, b4 * 128:(b4 + 1) * 128], rhs=hT[:, k, 0:nt], start=(k == 0), stop=(k == 7)), [wb, hT], [pb])
                E("act", lambda e, blk=blk, pb=pb: e.activation(out=qcT[:, blk, 0:nt], in_=pb[:, 0:nt], func=AF.Copy), [pb], [qcT])
        for hd in range(4):
            for mb in range(2):
                pb = prot.next()
                for j in range(2):
                    E("pe", lambda e, j=j, pb=pb, mb=mb: e.matmul(pb[:, 0:nt], lhsT=mkT[:, 2 * hd + j, mb * 128:(mb + 1) * 128], rhs=qcT[:, 2 * hd + j, 0:nt], start=(j == 0), stop=(j == 1)),
                      [mkT, qcT], [pb])
                E("act", lambda e, pb=pb, mb=mb: e.activation(out=pT16[:, mb, 0:nt], in_=pb[:, 0:nt], func=AF.Exp, scale=1.0 / 16), [pb], [pT16])
            pdn = P[4]
            for mb in range(2):
                E("pe", lambda e, mb=mb: e.matmul(pdn[:, 0:nt], lhsT=cstb[:, ONES, :], rhs=pT16[:, mb, 0:nt], start=(mb == 0), stop=(mb == 1)), [cstb, pT16], [pdn])
            E("dve", lambda e: e.reciprocal(out=rden[:, 0:nt], in_=pdn[:, 0:nt]), [pdn], [rden])
            for j in range(2):
                pb = prot.next()
                for mb in range(2):
                    E("pe", lambda e, mb=mb, pb=pb, j=j: e.matmul(pb[:, 0:nt], lhsT=mv16[:, mb, (2 * hd + j) * 128:(2 * hd + j + 1) * 128], rhs=pT16[:, mb, 0:nt], start=(mb == 0), stop=(mb == 1)),
                      [mv16, pT16], [pb])
                E("dve", lambda e, pb=pb, j=j: e.tensor_tensor(out=ocT[:, 2 * hd + j, 0:nt], in0=pb[:, 0:nt], in1=rden[:, 0:nt], op=ALU.mult), [pb, rden], [ocT])
        linear_fm_to_brT(ocT, 128, 8, w_co, nt)
        postnorm_add(xT, nt, 3)

    phase(16 * 1024)
    uext = [ar("uext%d" % i, [128, 514]) for i in range(2)]
    c_g = ar("c_g", [128, 512])
    c_v = ar("c_v", [128, 512])
    t_a = ar("t_a", [128, 512])
    t_b = ar("t_b", [128, 512])
    actT = ar("actT", [128, 22, 512], BF16)
    cvrow = ar("cvrow", [2, 512])
    cvsem = fw.dmasem("cv")

    def conv_block_s(blk, pb, ue, cdst):
        u3 = ue[:, 0:160].rearrange("p (s t) -> p s t", t=10)
        c3 = cdst[:, 0:128].rearrange("p (s t) -> p s t", t=8)
        E("act", lambda e: e.activation(out=u3[:, :, 2:10], in_=pb[:, 0:128].rearrange("p (s t) -> p s t", t=8), func=AF.Copy), [pb], [ue])
        E("dve", lambda e: e.tensor_copy(out=u3[:, :, 0:2], in_=ubuf_s[:, blk, :, :]), [ubuf_s], [ue])
        E("dve", lambda e: e.tensor_scalar(out=c3, in0=u3[:, :, 0:8], scalar1=convp_sb[:, 0, blk:blk + 1], scalar2=convp_sb[:, 3, blk:blk + 1], op0=ALU.mult, op1=ALU.add),
          [ue, convp_sb], [cdst])
        E("dve", lambda e: e.scalar_tensor_tensor(out=c3, in0=u3[:, :, 1:9], scalar=convp_sb[:, 1, blk:blk + 1], in1=c3, op0=ALU.mult, op1=ALU.add), [ue, convp_sb, cdst], [cdst])
        E("dve", lambda e: e.scalar_tensor_tensor(out=c3, in0=u3[:, :, 2:10], scalar=convp_sb[:, 2, blk:blk + 1], in1=c3, op0=ALU.mult, op1=ALU.add), [ue, convp_sb, cdst], [cdst])

    def conv_block(blk, pb, ue, nt, cdst, use_halo_mask):
        E("act", lambda e: e.activation(out=ue[:, 2:2 + nt], in_=pb[:, 0:nt], func=AF.Copy), [pb], [ue])
        if use_halo_mask:
            E("dve", lambda e: e.tensor_scalar(out=ue[:, 0:2], in0=ubuf[:, blk, :], scalar1=halo_sb[:, 0:1], scalar2=None, op0=ALU.mult), [ubuf, halo_sb], [ue])
        else:
            E("dve", lambda e: e.tensor_copy(out=ue[:, 0:2], in_=ubuf[:, blk, :]), [ubuf], [ue])
        E("dve", lambda e: e.tensor_copy(out=ubuf[:, blk, :], in_=ue[:, nt:nt + 2]), [ue], [ubuf])
        E("dve", lambda e: e.tensor_scalar(out=cdst[:, 0:nt], in0=ue[:, 0:nt], scalar1=convp_sb[:, 0, blk:blk + 1], scalar2=convp_sb[:, 3, blk:blk + 1], op0=ALU.mult, op1=ALU.add),
          [ue, convp_sb], [cdst])
        E("dve", lambda e: e.scalar_tensor_tensor(out=cdst[:, 0:nt], in0=ue[:, 1:1 + nt], scalar=convp_sb[:, 1, blk:blk + 1], in1=cdst[:, 0:nt], op0=ALU.mult, op1=ALU.add),
          [ue, convp_sb, cdst], [cdst])
        E("dve", lambda e: e.scalar_tensor_tensor(out=cdst[:, 0:nt], in0=ue[:, 2:2 + nt], scalar=convp_sb[:, 2, blk:blk + 1], in1=cdst[:, 0:nt], op0=ALU.mult, op1=ALU.add),
          [ue, convp_sb, cdst], [cdst])

    def conv_ffn(nt, first, last, smp=False):
        prenorm(xT, hT, nt, 5)
        for q in range(11):
            wbg, wbgv = load_w(w_up, q * 256, 256)
            wbv_, wbvv = load_w(w_up, DFF + q * 256, 256)
            for b2 in range(2):
                i = 2 * q + b2
                pg, pv = P[0], P[1]
                for k in range(8):
                    E("pe", lambda e, k=k: e.matmul(pg[:, 0:nt], lhsT=wbgv[:, k, b2 * 128:(b2 + 1) * 128], rhs=hT[:, k, 0:nt], start=(k == 0), stop=(k == 7)), [wbg, hT], [pg])
                for k in range(8):
                    E("pe", lambda e, k=k: e.matmul(pv[:, 0:nt], lhsT=wbvv[:, k, b2 * 128:(b2 + 1) * 128], rhs=hT[:, k, 0:nt], start=(k == 0), stop=(k == 7)), [wbv_, hT], [pv])
                if smp:
                    conv_block_s(i, pg, uext[0], c_g)
                    conv_block_s(22 + i, pv, uext[1], c_v)
                else:
                    conv_block(i, pg, uext[0], nt, c_g, first)
                    conv_block(22 + i, pv, uext[1], nt, c_v, first)
                E("pool", lambda e: e.tensor_tensor(out=t_a[:, 0:nt], in0=c_g[:, 0:nt], in1=c_g[:, 0:nt], op=ALU.mult), [c_g], [t_a])
                E("pool", lambda e: e.tensor_scalar(out=t_a[:, 0:nt], in0=t_a[:, 0:nt], scalar1=0.044715, scalar2=1.0, op0=ALU.mult, op1=ALU.add), [t_a], [t_a])
                E("pool", lambda e: e.tensor_tensor(out=t_a[:, 0:nt], in0=t_a[:, 0:nt], in1=c_g[:, 0:nt], op=ALU.mult), [t_a, c_g], [t_a])
                E("act", lambda e: e.activation(out=t_a[:, 0:nt], in_=t_a[:, 0:nt], func=AF.Sigmoid, scale=1.5957691216057308), [t_a], [t_a])
                E("pool", lambda e: e.tensor_tensor(out=t_b[:, 0:nt], in0=c_g[:, 0:nt], in1=c_v[:, 0:nt], op=ALU.mult), [c_g, c_v], [t_b])
                E("dve", lambda e, i=i: e.tensor_tensor(out=actT[:, i, 0:nt], in0=t_a[:, 0:nt], in1=t_b[:, 0:nt], op=ALU.mult), [t_a, t_b], [actT])
            if last and smp:
                for (wbX, wbXv, c0) in ((wbg, wbgv, q * 256), (wbv_, wbvv, DFF + q * 256)):
                    pb = P[4]
                    for j in range(2):
                        for k in range(8):
                            E("pe", lambda e: e.matmul(pb[32 * j:32 * j + 16, 0:256], lhsT=hT[:, k, 0:128].rearrange("p (s t) -> p s t", t=8)[:, :, 6 + j], rhs=wbXv[:, k, :],
                                                       start=(k == 0), stop=(k == 7)), [hT, wbX], [pb])
                    E("act", lambda e: e.activation(out=cvrow_s[0:48, :], in_=pb[0:48, 0:256], func=AF.Copy), [pb], [cvrow_s])
                    for j in range(2):
                        dma("pool", cvs_o[:, c0:c0 + 256].rearrange("(s j) n -> j s n", j=2)[j], cvrow_s[32 * j:32 * j + 16, :], [cvrow_s], [cvs_o], cvsem)
            elif last:
                for (wbX, wbXv, c0) in ((wbg, wbgv, q * 256), (wbv_, wbvv, DFF + q * 256)):
                    pb = P[4]
                    for k in range(8):
                        E("pe", lambda e, k=k, wbXv=wbXv: e.matmul(pb[0:2, 0:256], lhsT=hT[:, k, nt - 2:nt], rhs=wbXv[:, k, :], start=(k == 0), stop=(k == 7)), [hT, wbX], [pb])
                    E("act", lambda e: e.activation(out=cvrow[:, 0:256], in_=pb[0:2, 0:256], func=AF.Copy), [pb], [cvrow])
                    dma("pool", cv_o[:, c0:c0 + 256], cvrow[:, 0:256], [cvrow], [cv_o], cvsem)
        for blk in range(8):
            wb, wbv = load_w(w_down, blk * 128, 128, 128, 22)
            pb = prot.next()
            for kk in range(22):
                E("pe", lambda e, kk=kk: e.matmul(pb[:, 0:nt], lhsT=wbv[:, kk, :], rhs=actT[:, kk, 0:nt], start=(kk == 0), stop=(kk == 21)), [wb, actT], [pb])
            E("act", lambda e: e.activation(out=brT[:, blk, 0:nt], in_=pb[:, 0:nt], func=AF.Copy), [pb], [brT])
        postnorm_add(xT, nt, 6)

    def w_o_proj(nt):
        for q in range(2):
            wbs = []
            for part in range(2):
                wbs.append(load_w(w_o, q * 512, 512, kp=64, kc=8, r0=part * 512))
            for b4 in range(4):
                blk = q * 4 + b4
                pb = prot.next()
                for kk in range(16):
                    wb, wbv = wbs[kk // 8]
                    E("pe", lambda e, kk=kk, wbv=wbv, pb=pb, b4=b4: e.matmul(pb[:, 0:nt], lhsT=wbv[:, kk % 8, b4 * 128:(b4 + 1) * 128], rhs=mixT[:, kk, 0:nt], start=(kk == 0), stop=(kk == 15)),
                      [wb, mixT], [pb])
                E("act", lambda e, blk=blk, pb=pb: e.activation(out=brT[:, blk, 0:nt], in_=pb[:, 0:nt], func=AF.Copy), [pb], [brT])

    phase(0)
    yhi = ar("yhi", [128, 8, 128], BF16)
    ylo = ar("ylo", [128, 8, 128], BF16)
    yout = ar("yout", [128, D])
    ysem = fw.dmasem("y")

    def store_y(dst, row0, col0):
        E("dve", lambda e: e.tensor_copy(out=yhi[:], in_=xT[:, :, col0:col0 + 128]), [xT], [yhi])
        E("pool", lambda e: e.tensor_tensor(out=ylo[:], in0=xT[:, :, col0:col0 + 128], in1=yhi[:], op=ALU.subtract), [xT, yhi], [ylo])
        for c in range(8):
            E("pe", lambda e, c=c: e.transpose(out=PT[0][:, c * 128:(c + 1) * 128], in_=yhi[:, c, :], identity=cstb[:, IDENT, :]), [yhi, cstb], [PT[0]])
        for c in range(8):
            E("pe", lambda e, c=c: e.transpose(out=PT[1][:, c * 128:(c + 1) * 128], in_=ylo[:, c, :], identity=cstb[:, IDENT, :]), [ylo, cstb], [PT[1]])
        E("act", lambda e: e.activation(out=yout[:], in_=PT[0][:], func=AF.Copy), [PT[0]], [yout])
        E("dve", lambda e: e.tensor_tensor(out=yout[:], in0=yout[:], in1=PT[1][:], op=ALU.add), [yout, PT[1]], [yout])
        dma("pool", dst[row0:row0 + 128, :], yout[:], [yout], [dst], ysem)

    STG = os.environ.get("KSTG", "mabswcfyS")
    NA = int(os.environ.get("KNA", "12"))
    NB = int(os.environ.get("KNB", "5"))
    fw.barrier()
    if "m" in STG:
        memory_kv()
    fw.barrier()

    a_tiles = [4] * 11 + [3]
    sub0 = 0
    for nsub in (a_tiles[:NA] if 'a' in STG else []):
        for s in range(nsub):
            load_xT(xa, (sub0 + s) * 128, xT, s * 128)
        fw.barrier()
        prenorm(xT, hT, nsub * 128, 0)
        token_mix_proj(nsub, sub0, False, None)
        fw.barrier()
        sub0 += nsub

    b_tiles = [1, 4, 4, 4, 4]
    sub0 = 0
    for ti, nsub in enumerate(b_tiles[:NB] if 'b' in STG else []):
        nt = nsub * 128
        for s in range(nsub):
            load_xT(xb, (sub0 + s) * 128, xT, s * 128)
        fw.barrier()
        prenorm(xT, hT, nt, 0)
        token_mix_proj(nsub, NSP + sub0, True, (sub0 - 1) * 128 if ti > 0 else None)
        fw.barrier()
        if 's' in STG:
            sb_attention(nt, NSP + sub0 + nsub, NSP + sub0, 0)
        fw.barrier()
        if 'w' in STG:
            w_o_proj(nt)
            postnorm_add(xT, nt, 1)
        fw.barrier()
        if 'c' in STG:
            cross_attn(nt)
        fw.barrier()
        if 'f' in STG:
            conv_ffn(nt, ti == 1, ti == len(b_tiles) - 1)
        fw.barrier()
        if ti > 0 and 'y' in STG:
            for s in range(nsub):
                store_y(y_o, (sub0 - 1 + s) * 128, s * 128)
        fw.barrier()
        sub0 += nsub

    if "S" in STG:
        fw.barrier()
        phase(0)
        sqT = ar("sqT_s", [64, 8, 128], BF16)
        kst = ar("kst_s", [64, 8, 128], BF16)
        svs16 = ar("svs16", [128, 512], BF16)
        phase(8 * 1024)
        qT = ar("qT_s", [64, 8, 128], BF16)
        gT = ar("gT_s", [64, 8, 128], BF16)
        f_sb = ar("f_s", [128, 512]); lf = ar("lf_s", [128, 512]); lfh = ar("lfh_s", [128, 512], BF16); lfl = ar("lfl_s", [128, 512], BF16)
        k16 = ar("k16_s", [128, 512], BF16); kdd = ar("kdd_s", [128, 512], BF16); eD = ar("eD_s", [128, 512]); v16 = ar("v16_s", [128, 512], BF16)
        ebT = ar("ebT_s", [64, 8, 128]); enbT = ar("enbT_s", [64, 8, 128]); orst = enbT; otmp = ebT
        qtT = ar("qtT_s", [64, 8, 128], BF16); ktT = ar("ktT_s", [64, 8, 128], BF16); attm = ar("attm_s", [128, 8, 128], BF16)
        osq = ar("osq_s", [64, 8, 128], BF16); kvout = ar("kvout_s", [128, 512])
        ebl16 = ar("ebl16", [64, 8, 16]); stmp16 = ar("stmp16", [64, 16, 64])
        s0f_rot = Rot([ar("s0f%d" % i, [64, 16, 64]) for i in range(2)])
        s16h_rot = Rot([ar("s16h%d" % i, [64, 16, 64], BF16) for i in range(2)])
        vmh_rot = Rot([ar("vmh%d" % i, [128, 16, 64], BF16) for i in range(2)])
        for b_ in s0f_rot.bufs:
            b_.sem = fw.dmasem("sh")
            b_.sem2 = fw.dmasem("hss")
        load_xT(xs_d, 0, xT, 0)
        fw.barrier()
        prenorm(xT, hT, 128, 0)
        token_mix_proj(1, 0, True, 0, ks_o, vs_o, True)
        fw.barrier()

        phase(8 * 1024)
        pgK_rot = Rot([ar("pgK%d" % i, [128, 512]) for i in range(2)])
        pgV_rot = Rot([ar("pgV%d" % i, [128, 512]) for i in range(2)])
        for b_ in pgK_rot.bufs + pgV_rot.bufs:
            b_.sem = fw.dmasem("pg")
        pgK16 = ar("pgK16", [128, 512], BF16)
        KTp_rot = Rot([ar("KTp%d" % i, [64, 8, 128], BF16) for i in range(2)])
        V16 = ar("V16", [128, 16, 512], BF16)
        e_s = ar("e_s", [128, 17, 64]); g_s = ar("g_s", [128, 17, 64])
        sp_s = ar("sp_s", [128, 17, 64], BF16); a_s = ar("a_s", [128, 17, 64], BF16)
        zb = [P[0], P[1], P[2]]
        po_s = [P[3], P[4]]
        for sq in range(16):
            for pg in range(16):
                pk = pgK_rot.next(); pv = pgV_rot.next(); ktp = KTp_rot.next()
                col = sq * 16 + pg
                fw.op("pool", lambda e: e.indirect_dma_start(out=pk[:, :], out_offset=None, in_=csk[:, :],
                                                             in_offset=bass.IndirectOffsetOnAxis(ap=idx[:, col:col + 1], axis=0)), [csk, idx], [pk], dsem=pk.sem)
                fw.op("pool", lambda e: e.indirect_dma_start(out=pv[:, :], out_offset=None, in_=csv[:, :],
                                                             in_offset=bass.IndirectOffsetOnAxis(ap=idx[:, col:col + 1], axis=0)), [csv, idx], [pv], dsem=pv.sem)
                E("dve", lambda e: e.tensor_copy(out=pgK16[:], in_=pk[:]), [pk], [pgK16])
                E("pool", lambda e: e.tensor_copy(out=V16[:, pg, :], in_=pv[:]), [pv], [V16])
                for h in range(8):
                    E("pe", lambda e: e.transpose(out=PT[0][0:64, h * 128:(h + 1) * 128], in_=pgK16[:, h * 64:(h + 1) * 64], identity=cstb[:, IDENT, :]), [pgK16, cstb], [PT[0]])
                E("act", lambda e: e.activation(out=ktp[:], in_=PT[0][0:64, :].rearrange("p (h t) -> p h t", h=8), func=AF.Copy), [PT[0]], [ktp])
                zbk = zb[pg // 8]
                for h in range(8):
                    c0 = (pg % 8) * 64 + h * 8
                    E("pe", lambda e: e.matmul(zbk[:, c0:c0 + 8], lhsT=ktp[:, h, :], rhs=sqT[:, h, sq * 8:sq * 8 + 8], start=True, stop=True), [ktp, sqT], [zbk])
            for h in range(8):
                E("pe", lambda e: e.matmul(zb[2][:, h * 8:h * 8 + 8], lhsT=kst[:, h, 0:128], rhs=sqT[:, h, sq * 8:sq * 8 + 8], start=True, stop=True), [kst, sqT], [zb[2]])
            for bk in range(3):
                nb = 8 if bk < 2 else 1
                E("act", lambda e: e.activation(out=e_s[:, bk * 8:bk * 8 + nb, :], in_=zb[bk][:, 0:nb * 64].rearrange("p (b c) -> p b c", c=64), func=AF.Exp, scale=0.125),
                  [zb[bk]], [e_s])
            E("dve", lambda e: e.tensor_tensor(out=e_s[:].rearrange("p b (h q) -> p b h q", h=8), in0=e_s[:].rearrange("p b (h q) -> p b h q", h=8),
                                               in1=expb[:].unsqueeze(1).unsqueeze(3).to_broadcast([128, 17, 8, 8]), op=ALU.mult), [e_s, expb], [e_s])
            E("dve", lambda e: e.tensor_tensor(out=e_s[:, 16, :].rearrange("p (h q) -> p h q", h=8), in0=e_s[:, 16, :].rearrange("p (h q) -> p h q", h=8),
                                               in1=smask[:, sq, :].unsqueeze(1).to_broadcast([128, 8, 8]), op=ALU.mult), [e_s, smask], [e_s])
            E("act", lambda e: e.activation(out=sp_s[:], in_=e_s[:], func=AF.Ln, bias=1.0), [e_s], [sp_s])
            for blk in range(17):
                zbk = zb[blk // 8]
                c0 = (blk % 8) * 64
                E("pe", lambda e: e.matmul(zbk[:, c0:c0 + 64], lhsT=cstb[:, UINC, :], rhs=sp_s[:, blk, :], start=True, stop=(blk == 16)), [cstb, sp_s], [zbk])
                for b2 in range(blk + 1, 17):
                    E("pe", lambda e: e.matmul(zbk[:, c0:c0 + 64], lhsT=cstb[:, ONES, :], rhs=sp_s[:, b2, :], start=False, stop=(b2 == 16)), [cstb, sp_s], [zbk])
            for bk in range(3):
                nb = 8 if bk < 2 else 1
                E("act", lambda e: e.activation(out=g_s[:, bk * 8:bk * 8 + nb, :], in_=zb[bk][:, 0:nb * 64].rearrange("p (b c) -> p b c", c=64), func=AF.Exp, scale=-1.0),
                  [zb[bk]], [g_s])
            E("dve", lambda e: e.tensor_tensor(out=a_s[:], in0=e_s[:], in1=g_s[:], op=ALU.mult), [e_s, g_s], [a_s])
            for h in range(8):
                pob = po_s[h // 4]
                c0 = (h % 4) * 128 + sq * 8
                for blk in range(17):
                    lh = V16[:, blk, h * 64:(h + 1) * 64] if blk < 16 else svs16[:, h * 64:(h + 1) * 64]
                    E("pe", lambda e: e.matmul(pob[0:64, c0:c0 + 8], lhsT=lh, rhs=a_s[:, blk, h * 8:h * 8 + 8], start=(blk == 0), stop=(blk == 16)),
                      [V16, svs16, a_s], [pob])
        for half in range(2):
            E("act", lambda e: e.activation(out=mixT[:, 8 + half * 4:8 + (half + 1) * 4, 0:128], in_=po_s[half][0:64, :].rearrange("p (h t) -> p h t", h=4), func=AF.Copy),
              [po_s[half]], [mixT])
        fw.barrier()
        w_o_proj(128)
        postnorm_add(xT, 128, 1)
        fw.barrier()

        phase(16 * 1024)
        qcT = ar("qcT_s", [128, 8, 512], BF16); pT16 = ar("pT16_s", [128, 2, 512], BF16); rden = ar("rden_s", [128, 512]); ocT = ar("ocT_s", [128, 8, 512], BF16)
        mks_rot = Rot([ar("mks%d" % i, [128, 2, D]) for i in range(1)])
        mvs_rot = Rot([ar("mvs%d" % i, [128, 2, D]) for i in range(1)])
        mk16 = ar("mk16", [128, 2, D], BF16); mv16s = ar("mv16s", [128, 2, D], BF16); mkTs = ar("mkTs", [128, 8, 256], BF16)
        pTs = ar("pTs", [128, 64], BF16); rdens = ar("rdens", [128, 32])
        cmsem = [fw.dmasem("cmk"), fw.dmasem("cmv")]
        prenorm(xT, hT, 128, 2)
        for q in range(2):
            wb, wbv = load_w(w_cq, q * 512, 512)
            for b4 in range(4):
                blk = q * 4 + b4
                pb = prot.next()
                for k in range(8):
                    E("pe", lambda e: e.matmul(pb[:, 0:128], lhsT=wbv[:, k, b4 * 128:(b4 + 1) * 128], rhs=hT[:, k, 0:128], start=(k == 0), stop=(k == 7)), [wb, hT], [pb])
                E("act", lambda e: e.activation(out=qcT[:, blk, 0:128], in_=pb[:, 0:128], func=AF.Copy), [pb], [qcT])
        for sq in range(16):
            mks = mks_rot.next(); mvs = mvs_rot.next()
            dma("sp", mks[:], cmk[sq].rearrange("(b p) n -> p b n", p=128), [cmk], [mks], cmsem[0])
            dma("sp", mvs[:], cmv[sq].rearrange("(b p) n -> p b n", p=128), [cmv], [mvs], cmsem[1])
            E("dve", lambda e: e.tensor_copy(out=mk16[:], in_=mks[:]), [mks], [mk16])
            E("pool", lambda e: e.tensor_copy(out=mv16s[:], in_=mvs[:]), [mvs], [mv16s])
            for mb in range(2):
                for blk in range(8):
                    E("pe", lambda e: e.transpose(out=PT[0][:, blk * 128:(blk + 1) * 128], in_=mk16[:, mb, blk * 128:(blk + 1) * 128], identity=cstb[:, IDENT, :]), [mk16, cstb], [PT[0]])
                E("act", lambda e: e.activation(out=mkTs[:, :, mb * 128:(mb + 1) * 128], in_=PT[0][:].rearrange("p (b m) -> p b m", b=8), func=AF.Copy), [PT[0]], [mkTs])
            psc, pdn, ppv = P[0], P[1], P[2]
            for hd in range(4):
                for mb in range(2):
                    c0 = (hd * 2 + mb) * 8
                    for j in range(2):
                        E("pe", lambda e: e.matmul(psc[:, c0:c0 + 8], lhsT=mkTs[:, 2 * hd + j, mb * 128:(mb + 1) * 128], rhs=qcT[:, 2 * hd + j, sq * 8:sq * 8 + 8],
                                                   start=(j == 0), stop=(j == 1)), [mkTs, qcT], [psc])
            E("act", lambda e: e.activation(out=pTs[:], in_=psc[:, 0:64], func=AF.Exp, scale=1.0 / 16), [psc], [pTs])
            for hd in range(4):
                for mb in range(2):
                    c0 = (hd * 2 + mb) * 8
                    E("pe", lambda e: e.matmul(pdn[:, hd * 8:hd * 8 + 8], lhsT=cstb[:, ONES, :], rhs=pTs[:, c0:c0 + 8], start=(mb == 0), stop=(mb == 1)), [cstb, pTs], [pdn])
            E("dve", lambda e: e.reciprocal(out=rdens[:], in_=pdn[:, 0:32]), [pdn], [rdens])
            for hd in range(4):
                for j in range(2):
                    for mb in range(2):
                        c0 = (hd * 2 + mb) * 8
                        E("pe", lambda e: e.matmul(ppv[:, (hd * 2 + j) * 8:(hd * 2 + j) * 8 + 8], lhsT=mv16s[:, mb, (2 * hd + j) * 128:(2 * hd + j + 1) * 128], rhs=pTs[:, c0:c0 + 8],
                                                   start=(mb == 0), stop=(mb == 1)), [mv16s, pTs], [ppv])
            E("dve", lambda e: e.tensor_tensor(out=ocT[:, :, sq * 8:sq * 8 + 8].rearrange("p (hd j) t -> p hd j t", j=2), in0=ppv[:, 0:64].rearrange("p (hd j t) -> p hd j t", hd=4, j=2),
                                               in1=rdens[:].rearrange("p (hd t) -> p hd t", hd=4).unsqueeze(2).to_broadcast([128, 4, 2, 8]), op=ALU.mult), [ppv, rdens], [ocT])
        linear_fm_to_brT(ocT, 128, 8, w_co, 128)
        postnorm_add(xT, 128, 3)
        fw.barrier()

        phase(16 * 1024)
        uext = [ar("uext_s%d" % i, [128, 514]) for i in range(2)]
        c_g = ar("c_g_s", [128, 512]); c_v = ar("c_v_s", [128, 512]); t_a = ar("t_a_s", [128, 512]); t_b = ar("t_b_s", [128, 512])
        actT = ar("actT_s", [128, 22, 512], BF16)
        cvrow_s = ar("cvrow_s", [64, 256])
        ubuf_s = ar("ubuf_s", [128, 44, 16, 2])
        sct = ar("sct", [32, 1408]); schi = ar("schi", [32, 1408], BF16); sclo = ar("sclo", [32, 1408], BF16); utmp = ar("utmp", [128, 352])
        scsem = fw.dmasem("sc")
        for ci in range(4):
            dma("sp", sct[:], sc_d[:, ci * 1408:(ci + 1) * 1408], [sc_d], [sct], scsem)
            E("dve", lambda e: e.tensor_copy(out=schi[:], in_=sct[:]), [sct], [schi])
            E("pool", lambda e: e.tensor_tensor(out=sclo[:], in0=sct[:], in1=schi[:], op=ALU.subtract), [sct, schi], [sclo])
            for b in range(11):
                E("pe", lambda e: e.transpose(out=PT[0][:, b * 32:(b + 1) * 32], in_=schi[:, b * 128:(b + 1) * 128], identity=cstb[0:32, IDENT, 0:32]), [schi, cstb], [PT[0]])
            for b in range(11):
                E("pe", lambda e: e.transpose(out=PT[1][:, b * 32:(b + 1) * 32], in_=sclo[:, b * 128:(b + 1) * 128], identity=cstb[0:32, IDENT, 0:32]), [sclo, cstb], [PT[1]])
            E("act", lambda e: e.activation(out=utmp[:], in_=PT[0][:, 0:352], func=AF.Copy), [PT[0]], [utmp])
            E("dve", lambda e: e.tensor_tensor(out=ubuf_s[:, ci * 11:(ci + 1) * 11, :, :].rearrange("p b s j -> p (b s j)"), in0=utmp[:], in1=PT[1][:, 0:352], op=ALU.add),
              [utmp, PT[1]], [ubuf_s])
        conv_ffn(128, False, True, True)
        fw.barrier()
        phase(0)
        yhi = ar("yhi_s", [128, 8, 128], BF16); ylo = ar("ylo_s", [128, 8, 128], BF16); yout = ar("yout_s", [128, D])
        store_y(ys_o, 0, 0)
        fw.barrier()

    hsem = fw.dmasem("hs")
    dma("pool", hs_o[:].rearrange("h k v -> k h v"), S[:], [S], [hs_o], hsem)

    fw.barrier()
    fw.replay()
    es.close()
    fw.close()
    return nc


_CACHE = {}


def _consts():
    c = np.zeros((128, 8, 128), np.float32)
    i = np.arange(128)
    same = (i[:, None] // 64) == (i[None, :] // 64)
    c[:, 0, :] = np.eye(128)
    c[:, 1, :] = ((i[:, None] <= i[None, :]) & same)
    c[:, 2, :] = ((i[:, None] > i[None, :]) & same)
    c[:, 3, :] = ((i[:, None] <= i[None, :]) & same)
    c[:, 4, :] = (i[:, None] >= i[None, :])
    c[:, 5, :] = (i[:, None] < i[None, :])
    c[:, 6, :] = 1.0
    c[:, 7, 0] = (i < 64)
    c[:, 7, 1] = (i >= 64)
    q = np.arange(512)
    dm = np.zeros((128, 4, 512), np.float32)
    for b in range(4):
        dm[:, b, :] = ((b * 128 + i)[:, None] < q[None, :])
    return c, dm


def _consts8():
    i = np.arange(128)
    same = (i[:, None] // 8) == (i[None, :] // 8)
    c = np.zeros((128, 4, 128), np.float32)
    c[:, 0, :] = ((i[:, None] <= i[None, :]) & same)
    c[:, 1, :] = ((i[:, None] > i[None, :]) & same)
    c[:, 2, :] = ((i[:, None] <= i[None, :]) & same)
    c[:, 3, 0:16] = (i[:, None] // 8 == np.arange(16)[None, :])
    sm = np.zeros((128, 16, 8), np.float32)
    for sq in range(16):
        sm[:, sq, :] = ((i[:, None] // 8 == sq) & ((i[:, None] % 8) < np.arange(8)[None, :]))
    return c, sm


def kernel(x_prompt, x_sample, cache_sb_k, cache_sb_v, state_hgrn, state_ffn_conv,
           cache_mem_k, cache_mem_v, page_table, mem_prompt,
           w_in, hg_norm, hg_lb, sb_bias, w_o, g_mix_pre, g_mix_post, g_ca_pre, g_ca_post, g_mem,
           w_cq, w_ck, w_cv, w_co, g_ffn_pre, g_ffn_post, w_up, conv_w, conv_b, w_down):
    f = np.float32
    csk_full = np.asarray(cache_sb_k, f)[0]
    n_pool = csk_full.shape[0]
    key = ("nc", n_pool)
    if key not in _CACHE:
        _CACHE[key] = build_program(n_pool)
    nc = _CACHE[key]
    csk_flat = csk_full.reshape(n_pool * 128, 512)
    csv_flat = np.asarray(cache_sb_v, f)[0].reshape(n_pool * 128, 512)
    c8, sm8 = _consts8()
    cst, dm = _consts()
    gs = np.stack([np.asarray(g, f)[0].reshape(8, 128).T for g in (g_mix_pre, g_mix_post, g_ca_pre, g_ca_post, g_mem, g_ffn_pre, g_ffn_post)], axis=1)
    hgn = np.ascontiguousarray(np.asarray(hg_norm, f)[0].reshape(8, 64).T)
    lbraw = np.ascontiguousarray(np.broadcast_to(np.asarray(hg_lb, f)[None], (128, 2, 512)))
    sbb = np.ascontiguousarray(np.broadcast_to(np.asarray(sb_bias, f)[0][None], (128, 8)))
    cw = np.asarray(conv_w, f)[0]
    cb = np.asarray(conv_b, f)[0]
    convp = np.ascontiguousarray(np.stack([cw[0], cw[1], cw[2], cb], 0).reshape(4, 44, 128).transpose(2, 0, 1))
    shared = dict(w_in=np.asarray(w_in, f)[0], w_o=np.asarray(w_o, f)[0], w_cq=np.asarray(w_cq, f)[0], w_ck=np.asarray(w_ck, f)[0],
                  w_cv=np.asarray(w_cv, f)[0], w_co=np.asarray(w_co, f)[0], w_up=np.asarray(w_up, f)[0], w_down=np.asarray(w_down, f)[0],
                  gvecs=np.ascontiguousarray(gs), hgn=hgn, lbraw=lbraw, sbb=sbb, convp=convp, cst=cst, dmask=dm,
                  csk=csk_flat, csv=csv_flat, cst8=c8, smask=sm8, iotaf=np.arange(128, dtype=f).reshape(128, 1))
    xsmp = np.asarray(x_sample, f)
    shg = np.asarray(state_hgrn, f)[0]
    sfc = np.asarray(state_ffn_conv, f)[0]
    cmk_ = np.asarray(cache_mem_k, f)[0]
    cmv_ = np.asarray(cache_mem_v, f)[0]
    ptab = np.asarray(page_table, np.int32)
    xp = np.asarray(x_prompt, f)
    in_maps = []
    for c in range(8):
        b, j = c // 4, c % 4
        lo = 2048 * j - 128 - NSP * 128
        full = np.zeros((NKB * 128, D), f)
        src_lo = max(lo, 0)
        full[src_lo - lo:] = xp[b, src_lo:2048 * j + 2048]
        kvalid = np.zeros((128, NKB), f)
        nvalid_from = (src_lo - lo) // 128
        kvalid[:, :nvalid_from] = -30000.0
        m = dict(shared)
        m.update(xa=np.ascontiguousarray(full[:NSP * 128]), xb=np.ascontiguousarray(full[NSP * 128:]), kvalid=kvalid,
                 halo_valid=np.full((128, 1), 0.0 if j == 0 else 1.0, f), mem=np.asarray(mem_prompt, f)[b],
                 xs=np.ascontiguousarray(xsmp[16 * c:16 * c + 16].reshape(128, D)),
                 pt=np.ascontiguousarray(ptab[16 * c:16 * c + 16].reshape(256)),
                 sh=np.ascontiguousarray(shg[16 * c:16 * c + 16]),
                 sc=np.ascontiguousarray(sfc[16 * c:16 * c + 16].reshape(32, 2 * DFF)),
                 cmk=np.ascontiguousarray(cmk_[16 * c:16 * c + 16].reshape(16, 256, D)),
                 cmv=np.ascontiguousarray(cmv_[16 * c:16 * c + 16].reshape(16, 256, D)))
        in_maps.append(m)
    res = run_bass_kernel_spmd(nc, in_maps, core_ids=list(range(8)))
    R = res.results
    yp = np.zeros((2, 8192, D), f)
    kp = np.zeros((1, 2, 8192, 8, 64), f)
    vp = np.zeros((1, 2, 8192, 8, 64), f)
    for c in range(8):
        b, j = c // 4, c % 4
        yp[b, 2048 * j:2048 * j + 2048] = R[c]["y"]
        kp[0, b, 2048 * j:2048 * j + 2048] = R[c]["kout"].reshape(2048, 8, 64)
        vp[0, b, 2048 * j:2048 * j + 2048] = R[c]["vout"].reshape(2048, 8, 64)
    hsp = np.stack([R[3]["hstate"], R[7]["hstate"]])[None]
    cvp = np.stack([R[3]["convout"], R[7]["convout"]])[None]
    mkp = np.stack([R[0]["memk"], R[4]["memk"]]).reshape(1, 2, 256, 4, 256)
    mvp = np.stack([R[0]["memv"], R[4]["memv"]]).reshape(1, 2, 256, 4, 256)
    ys = np.concatenate([R[c]["ys"].reshape(16, 8, D) for c in range(8)], 0)
    ks = np.concatenate([R[c]["ksout"].reshape(16, 8, 8, 64) for c in range(8)], 0)[None]
    vs = np.concatenate([R[c]["vsout"].reshape(16, 8, 8, 64) for c in range(8)], 0)[None]
    hss = np.concatenate([R[c]["hss"] for c in range(8)], 0)[None]
    cvs = np.concatenate([R[c]["cvs"].reshape(16, 2, 2 * DFF) for c in range(8)], 0)[None]
    return (yp, ys, kp, vp, hsp.astype(f), cvp.astype(f), mkp, mvp, ks, vs, hss, cvs)
```

```python
import contextlib
import os
import types
import numpy as np
import concourse.bass as bass
import concourse.mybir as mybir
from concourse.bass_utils import run_bass_kernel_spmd

F32 = mybir.dt.float32
BF16 = mybir.dt.bfloat16
I32 = mybir.dt.int32
AF = mybir.ActivationFunctionType
ALU = mybir.AluOpType

NSP = 47
NSO = 17
NKB = NSP + NSO
D = 1024
DFF = 2816
EPS = 1e-6


class Reg:
    __slots__ = ("w", "rs", "name", "excl")

    def __init__(self, name=""):
        self.w = None
        self.rs = []
        self.name = name
        self.excl = False


class Buf:
    def __init__(self, t, name):
        self.t = t
        self.r = Reg(name)

    def __getitem__(self, k):
        return self.t[k]


class Fw:
    COMPUTE = ("pe", "act", "dve", "pool")

    def __init__(self, nc):
        self.nc = nc
        self.streams = {e: [] for e in ("pe", "act", "dve", "pool", "sp")}
        self.sems = {}
        self.semval = {}
        self.waited = {e: {} for e in self.streams}
        self._cms = []
        for e in self.COMPUTE:
            self._newsem("E_" + e)
        self.nd = 0

    def _newsem(self, key):
        cm = self.nc.semaphore(key)
        h = cm.__enter__()
        self._cms.append(cm)
        self.sems[key] = h
        self.semval[key] = 0
        return key

    def dmasem(self, name=None):
        self.nd += 1
        return self._newsem("D_%s_%d" % (name or "x", self.nd))

    def _wait(self, eng, toks):
        need = {}
        for t in toks:
            if t is None:
                continue
            k, v = t
            if v > need.get(k, 0):
                need[k] = v
        for k, v in need.items():
            if v > self.waited[eng].get(k, 0):
                self.waited[eng][k] = v
                self.streams[eng].append(("wait", k, v))

    @staticmethod
    def freeze(fn):
        if fn.__closure__ is None:
            return fn
        cells = []
        for c in fn.__closure__:
            try:
                cells.append(types.CellType(c.cell_contents))
            except ValueError:
                cells.append(c)
        return types.FunctionType(fn.__code__, fn.__globals__, fn.__name__, fn.__defaults__, tuple(cells))

    def op(self, eng, fn, reads=(), writes=(), dsem=None):
        fn = self.freeze(fn)
        toks = []
        own = "E_" + eng
        reads = [b.r if isinstance(b, Buf) else b for b in reads]
        writes = [b.r if isinstance(b, Buf) else b for b in writes]
        writes = writes + [r for r in reads if r.excl and r not in writes]
        for r in reads:
            toks.append(r.w)
        for w in writes:
            toks.append(w.w)
            toks.extend(w.rs)
        if eng == "pe":
            toks = [t for t in toks if t is not None and t[0] != own]
        self._wait(eng, toks)
        if dsem is not None:
            k, inc = dsem, 16
        else:
            k, inc = own, 1
        self.semval[k] += inc
        tok = (k, self.semval[k])
        self.streams[eng].append(("op", fn, k, inc))
        for w in writes:
            w.w = tok
            w.rs = []
        for r in reads:
            if r not in writes:
                r.rs.append(tok)
        return tok

    def barrier(self):
        allt = [(k, v) for k, v in self.semval.items() if v > 0]
        for e in self.streams:
            self._wait(e, allt)

    def replay(self):
        nc = self.nc
        names = {"pe": "tensor", "act": "scalar", "dve": "vector", "pool": "gpsimd", "sp": "sync"}
        with nc.Block() as block:
            for e, bn in names.items():
                items = self.streams[e]

                def body(engine, items=items):
                    for it in items:
                        if it[0] == "wait":
                            engine.wait_ge(self.sems[it[1]], it[2])
                        else:
                            it[1](engine).then_inc(self.sems[it[2]], it[3])
                getattr(block, bn)(body)

    def close(self):
        for cm in reversed(self._cms):
            cm.__exit__(None, None, None)


class Rot:
    def __init__(self, bufs):
        self.bufs = bufs
        self.i = 0

    def next(self):
        b = self.bufs[self.i % len(self.bufs)]
        self.i += 1
        return b


def build_program(n_pool=2560):
    nc = bass.Bass("TRN2", target_bir_lowering=False)
    fw = Fw(nc)
    es = contextlib.ExitStack()

    def din(name, shape, dt=F32):
        return Buf(nc.dram_tensor(name, list(shape), dt, kind="ExternalInput").ap(), name)

    def dout(name, shape, dt=F32):
        return Buf(nc.dram_tensor(name, list(shape), dt, kind="ExternalOutput").ap(), name)

    def dscr(name, shape, dt):
        return Buf(nc.dram_tensor(name, list(shape), dt, kind="Internal").ap(), name)

    def sb(name, shape, dt=F32, stack=None):
        return Buf((stack or es).enter_context(nc.sbuf_tensor(name, list(shape), dt)), name)

    def ps(name, shape, dt=F32, stack=None):
        b = Buf((stack or es).enter_context(nc.psum_tensor(name, list(shape), dt)), name)
        b.r.excl = True
        return b

    ARENA = 72 * 1024
    arena_t = es.enter_context(nc.sbuf_tensor("arena", [128, ARENA // 2], BF16))
    cur = [0]

    def phase(base):
        cur[0] = base

    def ar(name, shape, dt=F32):
        esz = 4 if dt == F32 else 2
        n = 1
        for d in shape[1:]:
            n *= d
        nb = (n * esz + 31) // 32 * 32
        off = cur[0]
        cur[0] += nb
        assert cur[0] <= ARENA, (name, cur[0])
        v = arena_t[0:shape[0], off // 2:(off + n * esz) // 2]
        if dt == F32:
            v = v.bitcast(F32)
        if len(shape) == 3:
            v = v.rearrange("p (a b) -> p a b", a=shape[1])
        elif len(shape) == 4:
            v = v.rearrange("p (a b c) -> p a b c", a=shape[1], b=shape[2])
        return Buf(v, name)

    xa = din("xa", [NSP * 128, D])
    xb = din("xb", [NSO * 128, D])
    kvalid = din("kvalid", [128, NKB])
    halo_valid = din("halo_valid", [128, 1])
    mem = din("mem", [256, D])
    w_in = din("w_in", [D, 3584])
    w_o = din("w_o", [D, D])
    w_cq = din("w_cq", [D, D])
    w_ck = din("w_ck", [D, D])
    w_cv = din("w_cv", [D, D])
    w_co = din("w_co", [D, D])
    w_up = din("w_up", [D, 2 * DFF])
    w_down = din("w_down", [DFF, D])
    gvecs = din("gvecs", [128, 7, 8])
    hgn = din("hgn", [64, 8])
    lbraw = din("lbraw", [128, 2, 512])
    sbb = din("sbb", [128, 8])
    convp = din("convp", [128, 4, 44])
    cst = din("cst", [128, 8, 128])
    dmask_d = din("dmask", [128, 4, 512])

    xs_d = din("xs", [128, D])
    csk = din("csk", [n_pool * 128, 512])
    csv = din("csv", [n_pool * 128, 512])
    pt_d = din("pt", [256], I32)
    iota_d = din("iotaf", [128, 1])
    sh_d = din("sh", [16, 8, 64, 64])
    sc_d = din("sc", [32, 2 * DFF])
    cmk = din("cmk", [16, 256, D])
    cmv = din("cmv", [16, 256, D])
    cst8 = din("cst8", [128, 4, 128])
    smask_d = din("smask", [128, 16, 8])
    ys_o = dout("ys", [128, D])
    ks_o = dout("ksout", [128, 512])
    vs_o = dout("vsout", [128, 512])
    hss_o = dout("hss", [16, 8, 64, 64])
    cvs_o = dout("cvs", [32, 2 * DFF])

    y_o = dout("y", [2048, D])
    k_o = dout("kout", [2048, 512])
    v_o = dout("vout", [2048, 512])
    hs_o = dout("hstate", [8, 64, 64])
    cv_o = dout("convout", [2, 2 * DFF])
    mk_o = dout("memk", [256, D])
    mv_o = dout("memv", [256, D])

    WB = {}
    for wd_, shp in ((w_in, [D, 3584]), (w_o, [D, D]), (w_cq, [D, D]), (w_ck, [D, D]), (w_cv, [D, D]), (w_co, [D, D]), (w_up, [D, 2 * DFF]), (w_down, [DFF, D])):
        WB[wd_.r.name] = dscr(wd_.r.name + "_bf", shp, BF16)
    WD = {w_.r.name: w_ for w_ in (w_in, w_o, w_cq, w_ck, w_cv, w_co, w_up, w_down)}
    kT_scr = dscr("kT_scr", [8, 64, NKB * 128], BF16)
    v_scr = dscr("v_scr", [8, 128, NKB, 64], BF16)

    outs = [y_o, k_o, v_o, hs_o, cv_o, mk_o, mv_o]

    cstf = sb("cstf", [128, 8, 128])
    cstb = sb("cstb", [128, 8, 128], BF16)
    IDENT, TRILI, TRIUS, BDM, UINC, LSTR, ONES, CSEL = range(8)
    dmaskf = sb("dmaskf", [128, 4, 512], BF16)
    phase(0)
    dmask32 = ar("dmask32", [128, 4, 512])
    gv = sb("gv", [128, 7, 8])
    hgn_sb = sb("hgn_sb", [64, 8])
    lb = sb("lb", [128, 512])
    oml = sb("oml", [128, 512])
    lbtmp = ar("lbtmp", [128, 2, 512])
    sbb_sb = sb("sbb_sb", [128, 8])
    kval_sb = sb("kval_sb", [128, NKB])
    biasv = sb("biasv", [128, NKB, 8])
    halo_sb = sb("halo_sb", [128, 1])
    convp_sb = sb("convp_sb", [128, 4, 44])
    epsb = sb("epsb", [128, 1])
    S = sb("S", [64, 8, 64])
    S16 = sb("S16", [64, 8, 64], BF16)
    Sb16 = sb("Sb16", [64, 8, 64], BF16)
    ubuf = sb("ubuf", [128, 44, 2])

    c8f = sb("c8f", [128, 4, 128])
    c8b = sb("c8b", [128, 4, 128], BF16)
    smask = sb("smask_sb", [128, 16, 8])
    expb = sb("expb", [128, 8])
    iotaf = sb("iotaf_sb", [128, 1])
    ptb = sb("ptb", [128, 256], I32)
    ptf = sb("ptf", [128, 256])
    idx = sb("idx", [128, 256], I32)

    def E(eng, fn, reads=(), writes=()):
        return fw.op(eng, fn, reads, writes)

    def dma(eng, out_ap, in_ap, reads, writes, sem, **kw):
        return fw.op(eng, lambda e: e.dma_start(out=out_ap, in_=in_ap, **kw), reads, writes, dsem=sem)

    dma("sp", cstf[:], cst[:], [cst], [cstf], fw.dmasem("setup"))
    dma("sp", dmask32[:], dmask_d[:], [dmask_d], [dmask32], fw.dmasem("setup"))
    dma("sp", gv[:], gvecs[:], [gvecs], [gv], fw.dmasem("setup"))
    dma("sp", hgn_sb[:], hgn[:], [hgn], [hgn_sb], fw.dmasem("setup"))
    dma("sp", lbtmp[:], lbraw[:], [lbraw], [lbtmp], fw.dmasem("setup"))
    dma("sp", sbb_sb[:], sbb[:], [sbb], [sbb_sb], fw.dmasem("setup"))
    dma("sp", kval_sb[:], kvalid[:], [kvalid], [kval_sb], fw.dmasem("setup"))
    dma("sp", halo_sb[:], halo_valid[:], [halo_valid], [halo_sb], fw.dmasem("setup"))
    dma("sp", convp_sb[:], convp[:], [convp], [convp_sb], fw.dmasem("setup"))
    dma("sp", c8f[:], cst8[:], [cst8], [c8f], fw.dmasem("setup"))
    dma("sp", smask[:], smask_d[:], [smask_d], [smask], fw.dmasem("setup"))
    dma("sp", iotaf[:], iota_d[:], [iota_d], [iotaf], fw.dmasem("setup"))
    dma("sp", ptb[:], pt_d[:].partition_broadcast(128), [pt_d], [ptb], fw.dmasem("setup"))
    E("dve", lambda e: e.tensor_copy(out=c8b[:], in_=c8f[:]), [c8f], [c8b])
    E("act", lambda e: e.activation(out=expb[:], in_=sbb_sb[:], func=AF.Exp), [sbb_sb], [expb])
    E("dve", lambda e: e.tensor_copy(out=ptf[:], in_=ptb[:]), [ptb], [ptf])
    E("dve", lambda e: e.tensor_scalar(out=ptf[:], in0=ptf[:], scalar1=128.0, scalar2=iotaf[:, 0:1], op0=ALU.mult, op1=ALU.add), [ptf, iotaf], [ptf])
    E("dve", lambda e: e.tensor_copy(out=idx[:], in_=ptf[:]), [ptf], [idx])
    E("dve", lambda e: e.tensor_copy(out=cstb[:], in_=cstf[:]), [cstf], [cstb])
    E("dve", lambda e: e.tensor_copy(out=dmaskf[:], in_=dmask32[:]), [dmask32], [dmaskf])
    E("pool", lambda e: e.memset(epsb[:], EPS), [], [epsb])
    E("pool", lambda e: e.memset(S[:], 0.0), [], [S])
    E("pool", lambda e: e.memset(S16[:], 0.0), [], [S16])
    E("pool", lambda e: e.memset(ubuf[:], 0.0), [], [ubuf])
    E("dve", lambda e: e.tensor_tensor(out=lb[:], in0=lbtmp[:, 0, :], in1=lbtmp[:, 1, :], op=ALU.subtract), [lbtmp], [lb])
    E("act", lambda e: e.activation(out=lb[:], in_=lb[:], func=AF.Sigmoid), [lb], [lb])
    E("dve", lambda e: e.tensor_scalar(out=oml[:], in0=lb[:], scalar1=-1.0, scalar2=1.0, op0=ALU.mult, op1=ALU.add), [lb], [oml])
    E("dve", lambda e: e.tensor_tensor(out=biasv[:], in0=kval_sb[:].unsqueeze(2).to_broadcast([128, NKB, 8]),
                                       in1=sbb_sb[:].unsqueeze(1).to_broadcast([128, NKB, 8]), op=ALU.add),
      [kval_sb, sbb_sb], [biasv])

    P = [ps("P%d" % i, [128, 512]) for i in range(6)]
    PT = [ps("PT%d" % i, [128, 1024], BF16) for i in range(2)]

    wst_rot = Rot([sb("wst%d" % i, [128, 2048]) for i in range(2)])
    wbf_rot = Rot([sb("wbf%d" % i, [128, 8, 512], BF16) for i in range(3)])
    for b_ in wst_rot.bufs + wbf_rot.bufs:
        b_.sem = fw.dmasem("w")
        b_.sem2 = fw.dmasem("wo")

    def precast_weights():
        n = 0
        for name, wb_d in WB.items():
            wd = WD[name]
            K, N = wd.t.shape
            for r0 in range(0, K, 128):
                for c0 in range(0, N, 2048):
                    w = min(2048, N - c0)
                    st = wst_rot.next()
                    tb = wbf_rot.next()
                    tbv = tb.t.rearrange("p a b -> p (a b)")[:, 0:w]
                    dma("sp", st[:, 0:w], wd[r0:r0 + 128, c0:c0 + w], [wd], [st], st.sem)
                    if n % 2 == 0:
                        E("dve", lambda e: e.tensor_copy(out=tbv, in_=st[:, 0:w]), [st], [tb])
                    else:
                        E("act", lambda e: e.activation(out=tbv, in_=st[:, 0:w], func=AF.Copy), [st], [tb])
                    dma("pool", wb_d[r0:r0 + 128, c0:c0 + w], tbv, [tb], [wb_d], tb.sem2)
                    n += 1

    def load_w(wd, c0, ncols, kp=128, kc=8, r0=0):
        wb = wbf_rot.next()
        wsrc = WB[wd.r.name]
        src = wsrc[r0:r0 + kc * kp, c0:c0 + ncols].rearrange("(c p) n -> p c n", p=kp)
        assert kc * ncols <= 8 * 512
        wbv = wb.t[0:kp].rearrange("p a b -> p (a b)")[:, 0:kc * ncols].rearrange("p (c n) -> p c n", c=kc)
        dma("sp", wbv, src, [wsrc], [wb], wb.sem)
        return wb, wbv

    sqb = sb("sqb", [128, 8, 512], BF16)
    rstd = sb("rstd", [128, 512])

    def rms_T(srcT, nt, scale_n):
        for c in range(8):
            E("act", lambda e, c=c: e.activation(out=sqb[:, c, 0:nt], in_=srcT[:, c, 0:nt], func=AF.Square), [srcT], [sqb])
        pb = P[5]
        for c in range(8):
            E("pe", lambda e, c=c: e.matmul(pb[:, 0:nt], lhsT=cstb[:, ONES, :], rhs=sqb[:, c, 0:nt], start=(c == 0), stop=(c == 7)),
              [cstb, sqb], [pb])
        E("act", lambda e: e.activation(out=rstd[:, 0:nt], in_=pb[:, 0:nt], func=AF.Ln, scale=1.0 / scale_n, bias=epsb[:]), [pb, epsb], [rstd])
        E("act", lambda e: e.activation(out=rstd[:, 0:nt], in_=rstd[:, 0:nt], func=AF.Exp, scale=-0.5), [rstd], [rstd])

    def prenorm(xT, hT, nt, gi):
        rms_T(xT, nt, 1024.0)
        for c in range(8):
            E("dve", lambda e, c=c: e.scalar_tensor_tensor(out=hT[:, c, 0:nt], in0=xT[:, c, 0:nt], scalar=gv[:, gi, c:c + 1],
                                                            in1=rstd[:, 0:nt], op0=ALU.mult, op1=ALU.mult), [xT, gv, rstd], [hT])

    phase(0)
    brT = ar("brT", [128, 8, 512])

    def postnorm_add(xT, nt, gi):
        rms_T(brT, nt, 1024.0)
        for c in range(8):
            E("dve", lambda e, c=c: e.tensor_tensor(out=brT[:, c, 0:nt], in0=brT[:, c, 0:nt], in1=rstd[:, 0:nt], op=ALU.mult), [brT, rstd], [brT])
            E("dve", lambda e, c=c: e.scalar_tensor_tensor(out=xT[:, c, 0:nt], in0=brT[:, c, 0:nt], scalar=gv[:, gi, c:c + 1],
                                                            in1=xT[:, c, 0:nt], op0=ALU.mult, op1=ALU.add), [brT, gv, xT], [xT])

    prot = Rot([P[0], P[1]])

    def linear_fm_to_brT(inT, kp, kc, wd, nt):
        for q in range(2):
            wb, wbv = load_w(wd, q * 512, 512)
            for b4 in range(4):
                blk = q * 4 + b4
                pb = prot.next()
                for k in range(8):
                    E("pe", lambda e: e.matmul(pb[:, 0:nt], lhsT=wbv[:, k, b4 * 128:(b4 + 1) * 128], rhs=inT[:, k, 0:nt], start=(k == 0), stop=(k == 7)), [wb, inT], [pb])
                E("act", lambda e: e.activation(out=brT[:, blk, 0:nt], in_=pb[:, 0:nt], func=AF.Copy), [pb], [brT])

    phase(0)
    xt_rot = Rot([ar("xt%d" % i, [128, D]) for i in range(2)])
    xsem = [fw.dmasem("x") for _ in range(2)]
    xcnt = [0]
    xhi = ar("xhi", [128, D], BF16)
    xlo = ar("xlo", [128, D], BF16)
    xtmp = ar("xtmp", [128, 8, 128])

    def load_xT(src, row0, xT, col0):
        i = xcnt[0] % 2
        xcnt[0] += 1
        xt = xt_rot.next()
        dma("sp", xt[:], src[row0:row0 + 128, :], [src], [xt], xsem[i])
        E("dve", lambda e: e.tensor_copy(out=xhi[:], in_=xt[:]), [xt], [xhi])
        E("pool", lambda e: e.tensor_tensor(out=xlo[:], in0=xt[:], in1=xhi[:], op=ALU.subtract), [xt, xhi], [xlo])
        for c in range(8):
            E("pe", lambda e, c=c: e.transpose(out=PT[0][:, c * 128:(c + 1) * 128], in_=xhi[:, c * 128:(c + 1) * 128], identity=cstb[:, IDENT, :]),
              [xhi, cstb], [PT[0]])
        for c in range(8):
            E("pe", lambda e, c=c: e.transpose(out=PT[1][:, c * 128:(c + 1) * 128], in_=xlo[:, c * 128:(c + 1) * 128], identity=cstb[:, IDENT, :]),
              [xlo, cstb], [PT[1]])
        E("act", lambda e: e.activation(out=xtmp[:], in_=PT[0][:].rearrange("p (c n) -> p c n", c=8), func=AF.Copy), [PT[0]], [xtmp])
        E("dve", lambda e: e.tensor_tensor(out=xT[:, :, col0:col0 + 128], in0=xtmp[:], in1=PT[1][:].rearrange("p (c n) -> p c n", c=8), op=ALU.add),
          [xtmp, PT[1]], [xT])

    xT = sb("xT", [128, 8, 512])
    hT = sb("hT", [128, 8, 512], BF16)
    phase(0)
    sqT = ar("sqT", [64, 8, 512], BF16)
    qT = ar("qT", [64, 8, 512], BF16)
    gT = ar("gT", [64, 8, 512], BF16)
    f_sb = ar("f_sb", [128, 512])
    lf = ar("lf", [128, 512])
    lfh = ar("lfh", [128, 512], BF16)
    lfl = ar("lfl", [128, 512], BF16)
    k16 = ar("k16", [128, 512], BF16)
    kdd = ar("kdd", [128, 512], BF16)
    eD = ar("eD", [128, 512])
    v16 = ar("v16", [128, 512], BF16)
    vm = ar("vm", [128, 8, 2, 64], BF16)
    ebT = ar("ebT", [64, 8, 128])
    enbT = ar("enbT", [64, 8, 128])
    ebl = ar("ebl", [64, 8, 2])
    qtT = ar("qtT", [64, 8, 128], BF16)
    ktT = ar("ktT", [64, 8, 128], BF16)
    attm = ar("attm", [128, 8, 128], BF16)
    stmp = ar("stmp", [64, 8, 64])
    osq = ar("osq", [64, 8, 128], BF16)
    orst = enbT
    otmp = ebT
    mixT = sb("mixT", [64, 16, 512], BF16)
    kvout = ar("kvout", [128, 512])
    kvsem = fw.dmasem("kv")
    scrsem = fw.dmasem("scrk")
    scrsemV = fw.dmasem("scrv")

    def tm_proj(wbv, wb, s, pb):
        for k in range(8):
            E("pe", lambda e, k=k: e.matmul(pb[:], lhsT=hT[:, k, s * 128:(s + 1) * 128], rhs=wbv[:, k, :], start=(k == 0), stop=(k == 7)), [hT, wb], [pb])

    def fm_proj64(wbv, wb, h, nt, pb):
        for k in range(8):
            E("pe", lambda e, k=k: e.matmul(pb[0:64, 0:nt], lhsT=wbv[:, k, h * 64:(h + 1) * 64], rhs=hT[:, k, 0:nt], start=(k == 0), stop=(k == 7)), [wb, hT], [pb])

    def hgrn_subtile(s, own, hf_ps, hi_ps, smp=False):
        tus = c8b[:, 1, :] if smp else cstb[:, TRIUS, :]
        tli = c8b[:, 0, :] if smp else cstb[:, TRILI, :]
        bdm = c8f[:, 2, :] if smp else cstf[:, BDM, :]
        cbuf = c8b if smp else cstb
        cfbuf = c8f if smp else cstf
        E("act", lambda e: e.activation(out=f_sb[:], in_=hf_ps[:], func=AF.Sigmoid), [hf_ps], [f_sb])
        E("dve", lambda e: e.tensor_tensor(out=f_sb[:], in0=f_sb[:], in1=oml[:], op=ALU.mult), [f_sb, oml], [f_sb])
        E("dve", lambda e: e.tensor_tensor(out=f_sb[:], in0=f_sb[:], in1=lb[:], op=ALU.add), [f_sb, lb], [f_sb])
        E("act", lambda e: e.activation(out=lf[:], in_=f_sb[:], func=AF.Ln), [f_sb], [lf])
        E("dve", lambda e: e.tensor_scalar(out=k16[:], in0=f_sb[:], scalar1=-1.0, scalar2=1.0, op0=ALU.mult, op1=ALU.add), [f_sb], [k16])
        E("dve", lambda e: e.tensor_copy(out=lfh[:], in_=lf[:]), [lf], [lfh])
        E("pool", lambda e: e.tensor_tensor(out=lfl[:], in0=lf[:], in1=lfh[:], op=ALU.subtract), [lf, lfh], [lfl])
        E("act", lambda e: e.activation(out=v16[:], in_=hi_ps[:], func=AF.Copy), [hi_ps], [v16])
        pd = P[4]
        E("pe", lambda e: e.matmul(pd[:], lhsT=tus, rhs=lfh[:], start=True, stop=False), [cbuf, lfh], [pd])
        E("pe", lambda e: e.matmul(pd[:], lhsT=tus, rhs=lfl[:], start=False, stop=True), [cbuf, lfl], [pd])
        E("act", lambda e: e.activation(out=eD[:], in_=pd[:], func=AF.Exp), [pd], [eD])
        E("dve", lambda e: e.tensor_tensor(out=kdd[:], in0=k16[:], in1=eD[:], op=ALU.mult), [k16, eD], [kdd])
        if not smp:
            E("pool", lambda e: e.tensor_tensor(out=vm[:], in0=v16[:].rearrange("p (h v) -> p h v", h=8).unsqueeze(2).to_broadcast([128, 8, 2, 64]),
                                                in1=cstb[:, CSEL, 0:2].unsqueeze(1).unsqueeze(3).to_broadcast([128, 8, 2, 64]), op=ALU.mult),
              [v16, cstb], [vm])
        pbt = P[2], P[3]
        for h in range(8):
            pb_ = pbt[h // 4]
            o = pb_[0:64, (h % 4) * 128:(h % 4 + 1) * 128]
            E("pe", lambda e, h=h, o=o: e.matmul(o, lhsT=lfh[:, h * 64:(h + 1) * 64], rhs=tli, start=True, stop=False), [lfh, cbuf], [pb_])
            E("pe", lambda e, h=h, o=o: e.matmul(o, lhsT=lfl[:, h * 64:(h + 1) * 64], rhs=tli, start=False, stop=True), [lfl, cbuf], [pb_])
        for half in range(2):
            pb_ = pbt[half]
            E("act", lambda e, half=half, pb_=pb_: e.activation(out=ebT[:, half * 4:(half + 1) * 4, :], in_=pb_[0:64, :].rearrange("p (h t) -> p h t", h=4), func=AF.Exp),
              [pb_], [ebT])
            if own:
                E("act", lambda e, half=half, pb_=pb_: e.activation(out=enbT[:, half * 4:(half + 1) * 4, :], in_=pb_[0:64, :].rearrange("p (h t) -> p h t", h=4), func=AF.Exp, scale=-1.0),
                  [pb_], [enbT])
        if smp:
            E("dve", lambda e: e.tensor_copy(out=ebl16[:], in_=ebT[:].rearrange("p h (c t) -> p h c t", c=16)[:, :, :, 7]), [ebT], [ebl16])
        else:
            E("dve", lambda e: e.tensor_copy(out=ebl[:], in_=ebT[:].rearrange("p h (c t) -> p h c t", c=2)[:, :, :, 63]), [ebT], [ebl])
        if own:
            E("dve", lambda e: e.tensor_tensor(out=qtT[:], in0=qT[:, :, s * 128:(s + 1) * 128], in1=ebT[:], op=ALU.mult), [qT, ebT], [qtT])
            for h in range(8):
                E("pe", lambda e, h=h: e.transpose(out=PT[0][0:64, h * 128:(h + 1) * 128], in_=k16[:, h * 64:(h + 1) * 64], identity=cstb[:, IDENT, :]), [k16, cstb], [PT[0]])
            E("dve", lambda e: e.tensor_tensor(out=ktT[:], in0=PT[0][0:64, :].rearrange("p (h t) -> p h t", h=8), in1=enbT[:], op=ALU.mult), [PT[0], enbT], [ktT])
            pat = P[2], P[3]
            for h in range(8):
                pb_ = pat[h // 4]
                E("pe", lambda e, h=h, pb_=pb_: e.matmul(pb_[:, (h % 4) * 128:(h % 4 + 1) * 128], lhsT=ktT[:, h, :], rhs=qtT[:, h, :], start=True, stop=True), [ktT, qtT], [pb_])
            for half in range(2):
                pb_ = pat[half]
                E("dve", lambda e, half=half, pb_=pb_: e.tensor_tensor(out=attm[:, half * 4:(half + 1) * 4, :], in0=pb_[:].rearrange("p (h t) -> p h t", h=4),
                                                                      in1=bdm.unsqueeze(1).to_broadcast([128, 4, 128]), op=ALU.mult), [pb_, cfbuf], [attm])
        po = P[2], P[3]
        if smp:
            for h in range(8):
                s0f = s0f_rot.next()
                s16h = s16h_rot.next()
                vmh = vmh_rot.next()
                dma("sp", s0f[:], sh_d[:, h, :, :].rearrange("s k v -> k s v"), [sh_d], [s0f], s0f.sem)
                E("pool", lambda e: e.tensor_copy(out=s16h[:], in_=s0f[:]), [s0f], [s16h])
                E("pool", lambda e: e.tensor_tensor(out=vmh[:], in0=v16[:, h * 64:(h + 1) * 64].unsqueeze(1).to_broadcast([128, 16, 64]),
                                                    in1=c8b[:, 3, 0:16].unsqueeze(2).to_broadcast([128, 16, 64]), op=ALU.mult), [v16, c8b], [vmh])
                for half in range(2):
                    pb_ = P[half]
                    E("pe", lambda e: e.matmul(pb_[0:64, :], lhsT=kdd[:, h * 64:(h + 1) * 64], rhs=vmh[:, half * 8:(half + 1) * 8, :].rearrange("p c v -> p (c v)"),
                                               start=True, stop=True), [kdd, vmh], [pb_])
                E("dve", lambda e: e.tensor_tensor(out=stmp16[:], in0=s0f[:], in1=ebl16[:, h, :].unsqueeze(2).to_broadcast([64, 16, 64]), op=ALU.mult), [s0f, ebl16], [stmp16])
                for half in range(2):
                    pb_ = P[half]
                    E("dve", lambda e: e.tensor_tensor(out=s0f[:, half * 8:(half + 1) * 8, :], in0=stmp16[:, half * 8:(half + 1) * 8, :],
                                                       in1=pb_[0:64, :].rearrange("p (c v) -> p c v", c=8), op=ALU.add), [stmp16, pb_], [s0f])
                dma("pool", hss_o[:, h, :, :].rearrange("s k v -> k s v"), s0f[:], [s0f], [hss_o], s0f.sem2)
                pb_ = po[h // 4]
                c0 = (h % 4) * 128
                E("pe", lambda e: e.matmul(pb_[0:64, c0:c0 + 128], lhsT=v16[:, h * 64:(h + 1) * 64], rhs=attm[:, h, :], start=True, stop=False), [v16, attm], [pb_])
                for c in range(16):
                    E("pe", lambda e: e.matmul(pb_[0:64, c0 + c * 8:c0 + c * 8 + 8], lhsT=s16h[:, c, :], rhs=qtT[:, h, c * 8:c * 8 + 8], start=False, stop=(c == 15)),
                      [s16h, qtT], [pb_])
        else:
            pp = P[0], P[1]
            for h in range(8):
                pb_ = pp[h // 4]
                E("pe", lambda e, h=h, pb_=pb_: e.matmul(pb_[0:64, (h % 4) * 128:(h % 4 + 1) * 128], lhsT=kdd[:, h * 64:(h + 1) * 64], rhs=vm[:, h, :, :].rearrange("p c v -> p (c v)"),
                                                         start=True, stop=True), [kdd, vm], [pb_])
            po = P[2], P[3]

            def chain(c, dst16):
                E("dve", lambda e: e.tensor_tensor(out=stmp[:], in0=S[:], in1=ebl[:, :, c:c + 1].to_broadcast([64, 8, 64]), op=ALU.mult), [S, ebl], [stmp])
                for half in range(2):
                    pb_ = pp[half]
                    E("dve", lambda e: e.tensor_tensor(out=S[:, half * 4:(half + 1) * 4, :], in0=stmp[:, half * 4:(half + 1) * 4, :],
                                                       in1=pb_[0:64, :].rearrange("p (h c v) -> p h c v", h=4, c=2)[:, :, c, :], op=ALU.add), [stmp, pb_], [S])
                E("pool", lambda e: e.tensor_copy(out=dst16[:], in_=S[:]), [S], [dst16])

            chain(0, Sb16)
            if own:
                for h in range(8):
                    pb_ = po[h // 4]
                    c0 = (h % 4) * 128
                    E("pe", lambda e: e.matmul(pb_[0:64, c0:c0 + 128], lhsT=v16[:, h * 64:(h + 1) * 64], rhs=attm[:, h, :], start=True, stop=False), [v16, attm], [pb_])
                    E("pe", lambda e: e.matmul(pb_[0:64, c0:c0 + 64], lhsT=S16[:, h, :], rhs=qtT[:, h, 0:64], start=False, stop=False), [S16, qtT], [pb_])
                    E("pe", lambda e: e.matmul(pb_[0:64, c0 + 64:c0 + 128], lhsT=Sb16[:, h, :], rhs=qtT[:, h, 64:128], start=False, stop=True), [Sb16, qtT], [pb_])
            chain(1, S16)
        if own:
            for half in range(2):
                pb_ = po[half]
                E("act", lambda e, half=half, pb_=pb_: e.activation(out=osq[:, half * 4:(half + 1) * 4, :], in_=pb_[0:64, :].rearrange("p (h t) -> p h t", h=4), func=AF.Square), [pb_], [osq])
            pr = P[0], P[1]
            for half in range(2):
                pb_ = pr[half]
                E("pe", lambda e, half=half, pb_=pb_: e.matmul(pb_[0:64, :], lhsT=cstb[0:64, ONES, 0:64], rhs=osq[:, half * 4:(half + 1) * 4, :].rearrange("p h t -> p (h t)"),
                                                               start=True, stop=True), [cstb, osq], [pb_])
                E("act", lambda e, half=half, pb_=pb_: e.activation(out=orst[:, half * 4:(half + 1) * 4, :], in_=pb_[0:64, :].rearrange("p (h t) -> p h t", h=4), func=AF.Ln,
                                                                    scale=1.0 / 64, bias=epsb[0:64, :]), [pb_, epsb], [orst])
            E("act", lambda e: e.activation(out=orst[:], in_=orst[:], func=AF.Exp, scale=-0.5), [orst], [orst])
            for half in range(2):
                pb_ = po[half]
                E("dve", lambda e, half=half, pb_=pb_: e.tensor_tensor(out=otmp[:, half * 4:(half + 1) * 4, :], in0=pb_[0:64, :].rearrange("p (h t) -> p h t", h=4),
                                                                      in1=orst[:, half * 4:(half + 1) * 4, :], op=ALU.mult), [pb_, orst], [otmp])
            E("dve", lambda e: e.tensor_tensor(out=otmp[:], in0=otmp[:], in1=hgn_sb[:].unsqueeze(2).to_broadcast([64, 8, 128]), op=ALU.mult), [otmp, hgn_sb], [otmp])
            E("dve", lambda e: e.tensor_tensor(out=mixT[:, 0:8, s * 128:(s + 1) * 128], in0=otmp[:], in1=gT[:, :, s * 128:(s + 1) * 128], op=ALU.mult), [otmp, gT], [mixT])

    C_HQ, C_HF, C_HI, C_HG, C_SQ, C_SK, C_SV = 0, 512, 1024, 1536, 2048, 2560, 3072
    kst = ar("kst", [64, 8, 512], BF16)

    def token_mix_proj(nsub, kb0, own, out_row0, kdst=None, vdst=None, smp=False):
        kdst = kdst or k_o
        vdst = vdst or v_o
        nt = nsub * 128
        wb, wbv = load_w(w_in, C_SK, 512)
        for h in range(8):
            pb = prot.next()
            fm_proj64(wbv, wb, h, nt, pb)
            E("act", lambda e, h=h, pb=pb: e.activation(out=kst[:, h, 0:nt], in_=pb[0:64, 0:nt], func=AF.Copy), [pb], [kst])
        if not smp:
            dma("pool", kT_scr[:, :, kb0 * 128:kb0 * 128 + nt].rearrange("h d t -> d h t"), kst[:, :, 0:nt], [kst], [kT_scr], scrsem)
        if own and out_row0 is not None:
            for s in range(nsub):
                pb = prot.next()
                tm_proj(wbv, wb, s, pb)
                E("act", lambda e, pb=pb: e.activation(out=kvout[:], in_=pb[:], func=AF.Copy), [pb], [kvout])
                dma("pool", kdst[out_row0 + s * 128:out_row0 + (s + 1) * 128, :], kvout[:], [kvout], [kdst], kvsem)
        wb, wbv = load_w(w_in, C_SV, 512)
        for s in range(nsub):
            pb = prot.next()
            tm_proj(wbv, wb, s, pb)
            E("act", lambda e, pb=pb: e.activation(out=v16[:], in_=pb[:], func=AF.Copy), [pb], [v16])
            if smp:
                E("pool", lambda e: e.tensor_copy(out=svs16[:], in_=v16[:]), [v16], [svs16])
            else:
                dma("pool", v_scr[:, :, kb0 + s, :].rearrange("h p v -> p h v"), v16[:].rearrange("p (h v) -> p h v", h=8), [v16], [v_scr], scrsemV)
            if own and out_row0 is not None:
                E("dve", lambda e, pb=pb: e.tensor_copy(out=kvout[:], in_=pb[:]), [pb], [kvout])
                dma("pool", vdst[out_row0 + s * 128:out_row0 + (s + 1) * 128, :], kvout[:], [kvout], [vdst], kvsem)
        if own:
            wb, wbv = load_w(w_in, C_HQ, 512)
            for h in range(8):
                pb = prot.next()
                fm_proj64(wbv, wb, h, nt, pb)
                E("act", lambda e, h=h, pb=pb: e.activation(out=qT[:, h, 0:nt], in_=pb[0:64, 0:nt], func=AF.Copy), [pb], [qT])
            wb, wbv = load_w(w_in, C_HG, 512)
            for h in range(8):
                pb = prot.next()
                fm_proj64(wbv, wb, h, nt, pb)
                E("act", lambda e, h=h, pb=pb: e.activation(out=gT[:, h, 0:nt], in_=pb[0:64, 0:nt], func=AF.Silu), [pb], [gT])
            wb, wbv = load_w(w_in, C_SQ, 512)
            for h in range(8):
                pb = prot.next()
                fm_proj64(wbv, wb, h, nt, pb)
                E("act", lambda e, h=h, pb=pb: e.activation(out=sqT[:, h, 0:nt], in_=pb[0:64, 0:nt], func=AF.Copy), [pb], [sqT])
        wbf_, wbfv = load_w(w_in, C_HF, 512)
        wbi_, wbiv = load_w(w_in, C_HI, 512)
        for s in range(nsub):
            p_hf, p_hi = P[0], P[1]
            tm_proj(wbfv, wbf_, s, p_hf)
            tm_proj(wbiv, wbi_, s, p_hi)
            hgrn_subtile(s, own, p_hf, p_hi, smp)

    phase(8 * 1024)
    kT_rot = Rot([ar("kTh%d" % i, [64, NKB * 128], BF16) for i in range(1)])
    vh_rot = Rot([ar("vh%d" % i, [128, NKB, 64], BF16) for i in range(1)])
    kvh_sem = [fw.dmasem("kvhk"), fw.dmasem("kvhv")]
    kvh_cnt = [0]
    e_rot = Rot([ar("e_sb%d" % i, [128, 512]) for i in range(3)])
    sp_rot = Rot([ar("sp16_%d" % i, [128, 512], BF16) for i in range(3)])
    g_rot = Rot([ar("g_sb%d" % i, [128, 512]) for i in range(2)])
    a_rot = Rot([ar("a16_%d" % i, [128, 512], BF16) for i in range(2)])
    z_rot = Rot([P[0], P[1]])

    def sb_attention(nq, kb_hi, kb_diag0, qcol0):
        for h in range(8):
            i = 0
            kvh_cnt[0] += 1
            kTh = kT_rot.next()
            vh = vh_rot.next()
            nk = kb_hi * 128
            dma("sp", kTh[:, 0:nk], kT_scr[h, :, 0:nk], [kT_scr], [kTh], kvh_sem[0])
            dma("sp", vh[:, 0:kb_hi, :], v_scr[h, :, 0:kb_hi, :], [v_scr], [vh], kvh_sem[1])
            pc, po_ = P[2], P[3]
            order = list(range(kb_hi - 1, -1, -1))
            st1 = {}

            def stage1(kb):
                pz = z_rot.next()
                E("pe", lambda e: e.matmul(pz[:, 0:nq], lhsT=kTh[:, kb * 128:(kb + 1) * 128], rhs=sqT[:, h, 0:nq], start=True, stop=True), [kTh, sqT], [pz])
                eb_ = e_rot.next()
                E("act", lambda e: e.activation(out=eb_[:, 0:nq], in_=pz[:, 0:nq], func=AF.Exp, scale=0.125, bias=biasv[:, kb, h:h + 1]), [pz, biasv], [eb_])
                if kb >= kb_diag0:
                    E("pool", lambda e: e.tensor_tensor(out=eb_[:, 0:nq], in0=eb_[:, 0:nq], in1=dmaskf[:, kb - kb_diag0, 0:nq], op=ALU.mult), [eb_, dmaskf], [eb_])
                sp_ = sp_rot.next()
                E("act", lambda e: e.activation(out=sp_[:, 0:nq], in_=eb_[:, 0:nq], func=AF.Ln, bias=1.0), [eb_], [sp_])
                st1[kb] = (eb_, sp_)

            def stage2(idx):
                kb = order[idx]
                eb_, sp_ = st1[kb]
                if idx > 0:
                    spp = st1[order[idx - 1]][1]
                    E("pe", lambda e: e.matmul(pc[:, 0:nq], lhsT=cstb[:, LSTR, :], rhs=spp[:, 0:nq], start=False, stop=False), [cstb, spp], [pc])
                E("pe", lambda e: e.matmul(pc[:, 0:nq], lhsT=cstb[:, UINC, :], rhs=sp_[:, 0:nq], start=(idx == 0), stop=(idx == len(order) - 1)), [cstb, sp_], [pc])
                g_ = g_rot.next()
                E("act", lambda e: e.activation(out=g_[:, 0:nq], in_=pc[:, 0:nq], func=AF.Exp, scale=-1.0), [pc], [g_])
                a_ = a_rot.next()
                E("dve", lambda e: e.tensor_tensor(out=a_[:, 0:nq], in0=eb_[:, 0:nq], in1=g_[:, 0:nq], op=ALU.mult), [eb_, g_], [a_])
                E("pe", lambda e: e.matmul(po_[0:64, 0:nq], lhsT=vh[:, kb, :], rhs=a_[:, 0:nq], start=(idx == 0), stop=(idx == len(order) - 1)), [vh, a_], [po_])
                if idx > 0:
                    del st1[order[idx - 1]]

            stage1(order[0])
            for idx in range(len(order)):
                if idx + 1 < len(order):
                    stage1(order[idx + 1])
                stage2(idx)
            E("act", lambda e: e.activation(out=mixT[:, 8 + h, qcol0:qcol0 + nq], in_=po_[0:64, 0:nq], func=AF.Copy), [po_], [mixT])

    mkT = sb("mkT", [128, 8, 256], BF16)
    mv16 = sb("mv16", [128, 2, D], BF16)
    phase(16 * 1024)
    qcT = ar("qcT", [128, 8, 512], BF16)
    pT16 = ar("pT16", [128, 2, 512], BF16)
    rden = ar("rden", [128, 512])
    ocT = ar("ocT", [128, 8, 512], BF16)
    memout = ar("memout", [128, D])
    memsem = fw.dmasem("mem")

    def memory_kv():
        KMK = int(os.environ.get("KMK", "9"))
        for s in range(2):
            load_xT(mem, s * 128, xT, s * 128)
        if KMK < 2:
            return
        prenorm(xT, hT, 256, 4)
        if KMK < 3:
            return
        for q in range(2):
            wb, wbv = load_w(w_ck, q * 512, 512)
            if KMK < 4:
                continue
            for b4 in range(4):
                blk = q * 4 + b4
                pb = prot.next()
                for k in range(8):
                    E("pe", lambda e, k=k, b4=b4, pb=pb: e.matmul(pb[:, 0:256], lhsT=wbv[:, k, b4 * 128:(b4 + 1) * 128], rhs=hT[:, k, 0:256], start=(k == 0), stop=(k == 7)), [wb, hT], [pb])
                E("act", lambda e, blk=blk, pb=pb: e.activation(out=mkT[:, blk, :], in_=pb[:, 0:256], func=AF.Copy), [pb], [mkT])
            if KMK < 5:
                continue
            for s in range(2):
                pb = prot.next()
                tm_proj(wbv, wb, s, pb)
                E("act", lambda e, pb=pb: e.activation(out=memout[:, q * 512:(q + 1) * 512], in_=pb[:], func=AF.Copy), [pb], [memout])
                dma("pool", mk_o[s * 128:(s + 1) * 128, q * 512:(q + 1) * 512], memout[:, q * 512:(q + 1) * 512], [memout], [mk_o], memsem)
        if KMK < 6:
            return
        for q in range(2):
            wb, wbv = load_w(w_cv, q * 512, 512)
            for s in range(2):
                pb = prot.next()
                tm_proj(wbv, wb, s, pb)
                E("act", lambda e, pb=pb: e.activation(out=memout[:, q * 512:(q + 1) * 512], in_=pb[:], func=AF.Copy), [pb], [memout])
                E("dve", lambda e, pb=pb, s=s: e.tensor_copy(out=mv16[:, s, q * 512:(q + 1) * 512], in_=pb[:]), [pb], [mv16])
                dma("pool", mv_o[s * 128:(s + 1) * 128, q * 512:(q + 1) * 512], memout[:, q * 512:(q + 1) * 512], [memout], [mv_o], memsem)

    def cross_attn(nt):
        prenorm(xT, hT, nt, 2)
        for q in range(2):
            wb, wbv = load_w(w_cq, q * 512, 512)
            for b4 in range(4):
                blk = q * 4 + b4
                pb = prot.next()
                for k in range(8):
                    E("pe", lambda e, k=k, b4=b4, pb=pb: e.matmul(pb[:, 0:nt], lhsT=wbv[:, k, b4 * 128:(b4 + 1) * 128], rhs=hT[:, k, 0:nt], start=(k == 0), stop=(k == 7)), [wb, hT], [pb])
                E("act", lambda e, blk=blk, pb=pb: e.activation(out=qcT[:, blk, 0:nt], in_=pb[:, 0:nt], func=AF.Copy), [pb], [qcT])
        for hd in range(4):
            for mb in range(2):
                pb = prot.next()
                for j in range(2):
                    E("pe", lambda e, j=j, pb=pb, mb=mb: e.matmul(pb[:, 0:nt], lhsT=mkT[:, 2 * hd + j, mb * 128:(mb + 1) * 128], rhs=qcT[:, 2 * hd + j, 0:nt], start=(j == 0), stop=(j == 1)),
                      [mkT, qcT], [pb])
                E("act", lambda e, pb=pb, mb=mb: e.activation(out=pT16[:, mb, 0:nt], in_=pb[:, 0:nt], func=AF.Exp, scale=1.0 / 16), [pb], [pT16])
            pdn = P[4]
            for mb in range(2):
                E("pe", lambda e, mb=mb: e.matmul(pdn[:, 0:nt], lhsT=cstb[:, ONES, :], rhs=pT16[:, mb, 0:nt], start=(mb == 0), stop=(mb == 1)), [cstb, pT16], [pdn])
            E("dve", lambda e: e.reciprocal(out=rden[:, 0:nt], in_=pdn[:, 0:nt]), [pdn], [rden])
            for j in range(2):
                pb = prot.next()
                for mb in range(2):
                    E("pe", lambda e, mb=mb, pb=pb, j=j: e.matmul(pb[:, 0:nt], lhsT=mv16[:, mb, (2 * hd + j) * 128:(2 * hd + j + 1) * 128], rhs=pT16[:, mb, 0:nt], start=(mb == 0), stop=(mb == 1)),
                      [mv16, pT16], [pb])
                E("dve", lambda e, pb=pb, j=j: e.tensor_tensor(out=ocT[:, 2 * hd + j, 0:nt], in0=pb[:, 0:nt], in1=rden[:, 0:nt], op=ALU.mult), [pb, rden], [ocT])
        linear_fm_to_brT(ocT, 128, 8, w_co, nt)
        postnorm_add(xT, nt, 3)

    phase(16 * 1024)
    uext = [ar("uext%d" % i, [128, 514]) for i in range(2)]
    c_g = ar("c_g", [128, 512])
    c_v = ar("c_v", [128, 512])
    t_a = ar("t_a", [128, 512])
    t_b = ar("t_b", [128, 512])
    actT = ar("actT", [128, 22, 512], BF16)
    cvrow = ar("cvrow", [2, 512])
    cvsem = fw.dmasem("cv")

    def conv_block_s(blk, pb, ue, cdst):
        u3 = ue[:, 0:160].rearrange("p (s t) -> p s t", t=10)
        c3 = cdst[:, 0:128].rearrange("p (s t) -> p s t", t=8)
        E("act", lambda e: e.activation(out=u3[:, :, 2:10], in_=pb[:, 0:128].rearrange("p (s t) -> p s t", t=8), func=AF.Copy), [pb], [ue])
        E("dve", lambda e: e.tensor_copy(out=u3[:, :, 0:2], in_=ubuf_s[:, blk, :, :]), [ubuf_s], [ue])
        E("dve", lambda e: e.tensor_scalar(out=c3, in0=u3[:, :, 0:8], scalar1=convp_sb[:, 0, blk:blk + 1], scalar2=convp_sb[:, 3, blk:blk + 1], op0=ALU.mult, op1=ALU.add),
          [ue, convp_sb], [cdst])
        E("dve", lambda e: e.scalar_tensor_tensor(out=c3, in0=u3[:, :, 1:9], scalar=convp_sb[:, 1, blk:blk + 1], in1=c3, op0=ALU.mult, op1=ALU.add), [ue, convp_sb, cdst], [cdst])
        E("dve", lambda e: e.scalar_tensor_tensor(out=c3, in0=u3[:, :, 2:10], scalar=convp_sb[:, 2, blk:blk + 1], in1=c3, op0=ALU.mult, op1=ALU.add), [ue, convp_sb, cdst], [cdst])

    def conv_block(blk, pb, ue, nt, cdst, use_halo_mask):
        E("act", lambda e: e.activation(out=ue[:, 2:2 + nt], in_=pb[:, 0:nt], func=AF.Copy), [pb], [ue])
        if use_halo_mask:
            E("dve", lambda e: e.tensor_scalar(out=ue[:, 0:2], in0=ubuf[:, blk, :], scalar1=halo_sb[:, 0:1], scalar2=None, op0=ALU.mult), [ubuf, halo_sb], [ue])
        else:
            E("dve", lambda e: e.tensor_copy(out=ue[:, 0:2], in_=ubuf[:, blk, :]), [ubuf], [ue])
        E("dve", lambda e: e.tensor_copy(out=ubuf[:, blk, :], in_=ue[:, nt:nt + 2]), [ue], [ubuf])
        E("dve", lambda e: e.tensor_scalar(out=cdst[:, 0:nt], in0=ue[:, 0:nt], scalar1=convp_sb[:, 0, blk:blk + 1], scalar2=convp_sb[:, 3, blk:blk + 1], op0=ALU.mult, op1=ALU.add),
          [ue, convp_sb], [cdst])
        E("dve", lambda e: e.scalar_tensor_tensor(out=cdst[:, 0:nt], in0=ue[:, 1:1 + nt], scalar=convp_sb[:, 1, blk:blk + 1], in1=cdst[:, 0:nt], op0=ALU.mult, op1=ALU.add),
          [ue, convp_sb, cdst], [cdst])
        E("dve", lambda e: e.scalar_tensor_tensor(out=cdst[:, 0:nt], in0=ue[:, 2:2 + nt], scalar=convp_sb[:, 2, blk:blk + 1], in1=cdst[:, 0:nt], op0=ALU.mult, op1=ALU.add),
          [ue, convp_sb, cdst], [cdst])

    def conv_ffn(nt, first, last, smp=False):
        prenorm(xT, hT, nt, 5)
        for q in range(11):
            wbg, wbgv = load_w(w_up, q * 256, 256)
            wbv_, wbvv = load_w(w_up, DFF + q * 256, 256)
            for b2 in range(2):
                i = 2 * q + b2
                pg, pv = P[0], P[1]
                for k in range(8):
                    E("pe", lambda e, k=k: e.matmul(pg[:, 0:nt], lhsT=wbgv[:, k, b2 * 128:(b2 + 1) * 128], rhs=hT[:, k, 0:nt], start=(k == 0), stop=(k == 7)), [wbg, hT], [pg])
                for k in range(8):
                    E("pe", lambda e, k=k: e.matmul(pv[:, 0:nt], lhsT=wbvv[:, k, b2 * 128:(b2 + 1) * 128], rhs=hT[:, k, 0:nt], start=(k == 0), stop=(k == 7)), [wbv_, hT], [pv])
                if smp:
                    conv_block_s(i, pg, uext[0], c_g)
                    conv_block_s(22 + i, pv, uext[1], c_v)
                else:
                    conv_block(i, pg, uext[0], nt, c_g, first)
                    conv_block(22 + i, pv, uext[1], nt, c_v, first)
                E("pool", lambda e: e.tensor_tensor(out=t_a[:, 0:nt], in0=c_g[:, 0:nt], in1=c_g[:, 0:nt], op=ALU.mult), [c_g], [t_a])
                E("pool", lambda e: e.tensor_scalar(out=t_a[:, 0:nt], in0=t_a[:, 0:nt], scalar1=0.044715, scalar2=1.0, op0=ALU.mult, op1=ALU.add), [t_a], [t_a])
                E("pool", lambda e: e.tensor_tensor(out=t_a[:, 0:nt], in0=t_a[:, 0:nt], in1=c_g[:, 0:nt], op=ALU.mult), [t_a, c_g], [t_a])
                E("act", lambda e: e.activation(out=t_a[:, 0:nt], in_=t_a[:, 0:nt], func=AF.Sigmoid, scale=1.5957691216057308), [t_a], [t_a])
                E("pool", lambda e: e.tensor_tensor(out=t_b[:, 0:nt], in0=c_g[:, 0:nt], in1=c_v[:, 0:nt], op=ALU.mult), [c_g, c_v], [t_b])
                E("dve", lambda e, i=i: e.tensor_tensor(out=actT[:, i, 0:nt], in0=t_a[:, 0:nt], in1=t_b[:, 0:nt], op=ALU.mult), [t_a, t_b], [actT])
            if last and smp:
                for (wbX, wbXv, c0) in ((wbg, wbgv, q * 256), (wbv_, wbvv, DFF + q * 256)):
                    pb = P[4]
                    for j in range(2):
                        for k in range(8):
                            E("pe", lambda e: e.matmul(pb[32 * j:32 * j + 16, 0:256], lhsT=hT[:, k, 0:128].rearrange("p (s t) -> p s t", t=8)[:, :, 6 + j], rhs=wbXv[:, k, :],
                                                       start=(k == 0), stop=(k == 7)), [hT, wbX], [pb])
                    E("act", lambda e: e.activation(out=cvrow_s[0:48, :], in_=pb[0:48, 0:256], func=AF.Copy), [pb], [cvrow_s])
                    for j in range(2):
                        dma("pool", cvs_o[:, c0:c0 + 256].rearrange("(s j) n -> j s n", j=2)[j], cvrow_s[32 * j:32 * j + 16, :], [cvrow_s], [cvs_o], cvsem)
            elif last:
                for (wbX, wbXv, c0) in ((wbg, wbgv, q * 256), (wbv_, wbvv, DFF + q * 256)):
                    pb = P[4]
                    for k in range(8):
                        E("pe", lambda e, k=k, wbXv=wbXv: e.matmul(pb[0:2, 0:256], lhsT=hT[:, k, nt - 2:nt], rhs=wbXv[:, k, :], start=(k == 0), stop=(k == 7)), [hT, wbX], [pb])
                    E("act", lambda e: e.activation(out=cvrow[:, 0:256], in_=pb[0:2, 0:256], func=AF.Copy), [pb], [cvrow])
                    dma("pool", cv_o[:, c0:c0 + 256], cvrow[:, 0:256], [cvrow], [cv_o], cvsem)
        for blk in range(8):
            wb, wbv = load_w(w_down, blk * 128, 128, 128, 22)
            pb = prot.next()
            for kk in range(22):
                E("pe", lambda e, kk=kk: e.matmul(pb[:, 0:nt], lhsT=wbv[:, kk, :], rhs=actT[:, kk, 0:nt], start=(kk == 0), stop=(kk == 21)), [wb, actT], [pb])
            E("act", lambda e: e.activation(out=brT[:, blk, 0:nt], in_=pb[:, 0:nt], func=AF.Copy), [pb], [brT])
        postnorm_add(xT, nt, 6)

    def w_o_proj(nt):
        for q in range(2):
            wbs = []
            for part in range(2):
                wbs.append(load_w(w_o, q * 512, 512, kp=64, kc=8, r0=part * 512))
            for b4 in range(4):
                blk = q * 4 + b4
                pb = prot.next()
                for kk in range(16):
                    wb, wbv = wbs[kk // 8]
                    E("pe", lambda e, kk=kk, wbv=wbv, pb=pb, b4=b4: e.matmul(pb[:, 0:nt], lhsT=wbv[:, kk % 8, b4 * 128:(b4 + 1) * 128], rhs=mixT[:, kk, 0:nt], start=(kk == 0), stop=(kk == 15)),
                      [wb, mixT], [pb])
                E("act", lambda e, blk=blk, pb=pb: e.activation(out=brT[:, blk, 0:nt], in_=pb[:, 0:nt], func=AF.Copy), [pb], [brT])

    phase(0)
    yhi = ar("yhi", [128, 8, 128], BF16)
    ylo = ar("ylo", [128, 8, 128], BF16)
    yout = ar("yout", [128, D])
    ysem = fw.dmasem("y")

    def store_y(dst, row0, col0):
        E("dve", lambda e: e.tensor_copy(out=yhi[:], in_=xT[:, :, col0:col0 + 128]), [xT], [yhi])
        E("pool", lambda e: e.tensor_tensor(out=ylo[:], in0=xT[:, :, col0:col0 + 128], in1=yhi[:], op=ALU.subtract), [xT, yhi], [ylo])
        for c in range(8):
            E("pe", lambda e, c=c: e.transpose(out=PT[0][:, c * 128:(c + 1) * 128], in_=yhi[:, c, :], identity=cstb[:, IDENT, :]), [yhi, cstb], [PT[0]])
        for c in range(8):
            E("pe", lambda e, c=c: e.transpose(out=PT[1][:, c * 128:(c + 1) * 128], in_=ylo[:, c, :], identity=cstb[:, IDENT, :]), [ylo, cstb], [PT[1]])
        E("act", lambda e: e.activation(out=yout[:], in_=PT[0][:], func=AF.Copy), [PT[0]], [yout])
        E("dve", lambda e: e.tensor_tensor(out=yout[:], in0=yout[:], in1=PT[1][:], op=ALU.add), [yout, PT[1]], [yout])
        dma("pool", dst[row0:row0 + 128, :], yout[:], [yout], [dst], ysem)

    STG = os.environ.get("KSTG", "mabswcfyS")
    NA = int(os.environ.get("KNA", "12"))
    NB = int(os.environ.get("KNB", "5"))
    precast_weights()
    fw.barrier()
    if "m" in STG:
        memory_kv()
    fw.barrier()

    a_tiles = [4] * 11 + [3]
    sub0 = 0
    for nsub in (a_tiles[:NA] if 'a' in STG else []):
        for s in range(nsub):
            load_xT(xa, (sub0 + s) * 128, xT, s * 128)
        fw.barrier()
        prenorm(xT, hT, nsub * 128, 0)
        token_mix_proj(nsub, sub0, False, None)
        fw.barrier()
        sub0 += nsub

    b_tiles = [1, 4, 4, 4, 4]
    sub0 = 0
    for ti, nsub in enumerate(b_tiles[:NB] if 'b' in STG else []):
        nt = nsub * 128
        for s in range(nsub):
            load_xT(xb, (sub0 + s) * 128, xT, s * 128)
        fw.barrier()
        prenorm(xT, hT, nt, 0)
        token_mix_proj(nsub, NSP + sub0, True, (sub0 - 1) * 128 if ti > 0 else None)
        fw.barrier()
        if 's' in STG:
            sb_attention(nt, NSP + sub0 + nsub, NSP + sub0, 0)
        fw.barrier()
        if 'w' in STG:
            w_o_proj(nt)
            postnorm_add(xT, nt, 1)
        fw.barrier()
        if 'c' in STG:
            cross_attn(nt)
        fw.barrier()
        if 'f' in STG:
            conv_ffn(nt, ti == 1, ti == len(b_tiles) - 1)
        fw.barrier()
        if ti > 0 and 'y' in STG:
            for s in range(nsub):
                store_y(y_o, (sub0 - 1 + s) * 128, s * 128)
        fw.barrier()
        sub0 += nsub

    if "S" in STG:
        fw.barrier()
        phase(0)
        sqT = ar("sqT_s", [64, 8, 128], BF16)
        kst = ar("kst_s", [64, 8, 128], BF16)
        svs16 = ar("svs16", [128, 512], BF16)
        phase(8 * 1024)
        qT = ar("qT_s", [64, 8, 128], BF16)
        gT = ar("gT_s", [64, 8, 128], BF16)
        f_sb = ar("f_s", [128, 512]); lf = ar("lf_s", [128, 512]); lfh = ar("lfh_s", [128, 512], BF16); lfl = ar("lfl_s", [128, 512], BF16)
        k16 = ar("k16_s", [128, 512], BF16); kdd = ar("kdd_s", [128, 512], BF16); eD = ar("eD_s", [128, 512]); v16 = ar("v16_s", [128, 512], BF16)
        ebT = ar("ebT_s", [64, 8, 128]); enbT = ar("enbT_s", [64, 8, 128]); orst = enbT; otmp = ebT
        qtT = ar("qtT_s", [64, 8, 128], BF16); ktT = ar("ktT_s", [64, 8, 128], BF16); attm = ar("attm_s", [128, 8, 128], BF16)
        osq = ar("osq_s", [64, 8, 128], BF16); kvout = ar("kvout_s", [128, 512])
        ebl16 = ar("ebl16", [64, 8, 16]); stmp16 = ar("stmp16", [64, 16, 64])
        s0f_rot = Rot([ar("s0f%d" % i, [64, 16, 64]) for i in range(2)])
        s16h_rot = Rot([ar("s16h%d" % i, [64, 16, 64], BF16) for i in range(2)])
        vmh_rot = Rot([ar("vmh%d" % i, [128, 16, 64], BF16) for i in range(2)])
        for b_ in s0f_rot.bufs:
            b_.sem = fw.dmasem("sh")
            b_.sem2 = fw.dmasem("hss")
        load_xT(xs_d, 0, xT, 0)
        fw.barrier()
        prenorm(xT, hT, 128, 0)
        token_mix_proj(1, 0, True, 0, ks_o, vs_o, True)
        fw.barrier()

        phase(8 * 1024)
        pgK_rot = Rot([ar("pgK%d" % i, [128, 512]) for i in range(4)])
        pgV_rot = Rot([ar("pgV%d" % i, [128, 512]) for i in range(4)])
        for b_ in pgK_rot.bufs + pgV_rot.bufs:
            b_.sem = fw.dmasem("pg")
        pgK16 = ar("pgK16", [128, 512], BF16)
        KTp_rot = Rot([ar("KTp%d" % i, [64, 8, 128], BF16) for i in range(2)])
        V16 = ar("V16", [128, 16, 512], BF16)
        e_s = ar("e_s", [128, 17, 64]); g_s = ar("g_s", [128, 17, 64])
        sp_s = ar("sp_s", [128, 17, 64], BF16); a_s = ar("a_s", [128, 17, 64], BF16)
        zb = [P[0], P[1], P[2]]
        po_s = [P[3], P[4]]
        for sq in range(16):
            for pg in range(16):
                pk = pgK_rot.next(); pv = pgV_rot.next(); ktp = KTp_rot.next()
                col = sq * 16 + pg
                fw.op("pool", lambda e: e.indirect_dma_start(out=pk[:, :], out_offset=None, in_=csk[:, :],
                                                             in_offset=bass.IndirectOffsetOnAxis(ap=idx[:, col:col + 1], axis=0)), [csk, idx], [pk], dsem=pk.sem)
                fw.op("pool", lambda e: e.indirect_dma_start(out=pv[:, :], out_offset=None, in_=csv[:, :],
                                                             in_offset=bass.IndirectOffsetOnAxis(ap=idx[:, col:col + 1], axis=0)), [csv, idx], [pv], dsem=pv.sem)
                E("dve", lambda e: e.tensor_copy(out=pgK16[:], in_=pk[:]), [pk], [pgK16])
                E("act", lambda e: e.activation(out=V16[:, pg, :], in_=pv[:], func=AF.Copy), [pv], [V16])
                for h in range(8):
                    E("pe", lambda e: e.transpose(out=PT[0][0:64, h * 128:(h + 1) * 128], in_=pgK16[:, h * 64:(h + 1) * 64], identity=cstb[:, IDENT, :]), [pgK16, cstb], [PT[0]])
                E("act", lambda e: e.activation(out=ktp[:], in_=PT[0][0:64, :].rearrange("p (h t) -> p h t", h=8), func=AF.Copy), [PT[0]], [ktp])
                zbk = zb[pg // 8]
                for h in range(8):
                    c0 = (pg % 8) * 64 + h * 8
                    E("pe", lambda e: e.matmul(zbk[:, c0:c0 + 8], lhsT=ktp[:, h, :], rhs=sqT[:, h, sq * 8:sq * 8 + 8], start=True, stop=True), [ktp, sqT], [zbk])
            for h in range(8):
                E("pe", lambda e: e.matmul(zb[2][:, h * 8:h * 8 + 8], lhsT=kst[:, h, 0:128], rhs=sqT[:, h, sq * 8:sq * 8 + 8], start=True, stop=True), [kst, sqT], [zb[2]])
            for bk in range(3):
                nb = 8 if bk < 2 else 1
                E("act", lambda e: e.activation(out=e_s[:, bk * 8:bk * 8 + nb, :], in_=zb[bk][:, 0:nb * 64].rearrange("p (b c) -> p b c", c=64), func=AF.Exp, scale=0.125),
                  [zb[bk]], [e_s])
            E("dve", lambda e: e.tensor_tensor(out=e_s[:].rearrange("p b (h q) -> p b h q", h=8), in0=e_s[:].rearrange("p b (h q) -> p b h q", h=8),
                                               in1=expb[:].unsqueeze(1).unsqueeze(3).to_broadcast([128, 17, 8, 8]), op=ALU.mult), [e_s, expb], [e_s])
            E("dve", lambda e: e.tensor_tensor(out=e_s[:, 16, :].rearrange("p (h q) -> p h q", h=8), in0=e_s[:, 16, :].rearrange("p (h q) -> p h q", h=8),
                                               in1=smask[:, sq, :].unsqueeze(1).to_broadcast([128, 8, 8]), op=ALU.mult), [e_s, smask], [e_s])
            E("act", lambda e: e.activation(out=sp_s[:], in_=e_s[:], func=AF.Ln, bias=1.0), [e_s], [sp_s])
            for blk in range(17):
                zbk = zb[blk // 8]
                c0 = (blk % 8) * 64
                E("pe", lambda e: e.matmul(zbk[:, c0:c0 + 64], lhsT=cstb[:, UINC, :], rhs=sp_s[:, blk, :], start=True, stop=(blk == 16)), [cstb, sp_s], [zbk])
                for b2 in range(blk + 1, 17):
                    E("pe", lambda e: e.matmul(zbk[:, c0:c0 + 64], lhsT=cstb[:, ONES, :], rhs=sp_s[:, b2, :], start=False, stop=(b2 == 16)), [cstb, sp_s], [zbk])
            for bk in range(3):
                nb = 8 if bk < 2 else 1
                E("act", lambda e: e.activation(out=g_s[:, bk * 8:bk * 8 + nb, :], in_=zb[bk][:, 0:nb * 64].rearrange("p (b c) -> p b c", c=64), func=AF.Exp, scale=-1.0),
                  [zb[bk]], [g_s])
            E("dve", lambda e: e.tensor_tensor(out=a_s[:], in0=e_s[:], in1=g_s[:], op=ALU.mult), [e_s, g_s], [a_s])
            for h in range(8):
                pob = po_s[h // 4]
                c0 = (h % 4) * 128 + sq * 8
                for blk in range(17):
                    lh = V16[:, blk, h * 64:(h + 1) * 64] if blk < 16 else svs16[:, h * 64:(h + 1) * 64]
                    E("pe", lambda e: e.matmul(pob[0:64, c0:c0 + 8], lhsT=lh, rhs=a_s[:, blk, h * 8:h * 8 + 8], start=(blk == 0), stop=(blk == 16)),
                      [V16, svs16, a_s], [pob])
        for half in range(2):
            E("act", lambda e: e.activation(out=mixT[:, 8 + half * 4:8 + (half + 1) * 4, 0:128], in_=po_s[half][0:64, :].rearrange("p (h t) -> p h t", h=4), func=AF.Copy),
              [po_s[half]], [mixT])
        fw.barrier()
        w_o_proj(128)
        postnorm_add(xT, 128, 1)
        fw.barrier()

        phase(16 * 1024)
        qcT = ar("qcT_s", [128, 8, 512], BF16); pT16 = ar("pT16_s", [128, 2, 512], BF16); rden = ar("rden_s", [128, 512]); ocT = ar("ocT_s", [128, 8, 512], BF16)
        mks_rot = Rot([ar("mks%d" % i, [128, 2, D]) for i in range(1)])
        mvs_rot = Rot([ar("mvs%d" % i, [128, 2, D]) for i in range(1)])
        mk16 = ar("mk16", [128, 2, D], BF16); mv16s = ar("mv16s", [128, 2, D], BF16); mkTs = ar("mkTs", [128, 8, 256], BF16)
        pTs = ar("pTs", [128, 64], BF16); rdens = ar("rdens", [128, 32])
        cmsem = [fw.dmasem("cmk"), fw.dmasem("cmv")]
        prenorm(xT, hT, 128, 2)
        for q in range(2):
            wb, wbv = load_w(w_cq, q * 512, 512)
            for b4 in range(4):
                blk = q * 4 + b4
                pb = prot.next()
                for k in range(8):
                    E("pe", lambda e: e.matmul(pb[:, 0:128], lhsT=wbv[:, k, b4 * 128:(b4 + 1) * 128], rhs=hT[:, k, 0:128], start=(k == 0), stop=(k == 7)), [wb, hT], [pb])
                E("act", lambda e: e.activation(out=qcT[:, blk, 0:128], in_=pb[:, 0:128], func=AF.Copy), [pb], [qcT])
        for sq in range(16):
            mks = mks_rot.next(); mvs = mvs_rot.next()
            dma("sp", mks[:], cmk[sq].rearrange("(b p) n -> p b n", p=128), [cmk], [mks], cmsem[0])
            dma("sp", mvs[:], cmv[sq].rearrange("(b p) n -> p b n", p=128), [cmv], [mvs], cmsem[1])
            E("dve", lambda e: e.tensor_copy(out=mk16[:], in_=mks[:]), [mks], [mk16])
            E("pool", lambda e: e.tensor_copy(out=mv16s[:], in_=mvs[:]), [mvs], [mv16s])
            for mb in range(2):
                for blk in range(8):
                    E("pe", lambda e: e.transpose(out=PT[0][:, blk * 128:(blk + 1) * 128], in_=mk16[:, mb, blk * 128:(blk + 1) * 128], identity=cstb[:, IDENT, :]), [mk16, cstb], [PT[0]])
                E("act", lambda e: e.activation(out=mkTs[:, :, mb * 128:(mb + 1) * 128], in_=PT[0][:].rearrange("p (b m) -> p b m", b=8), func=AF.Copy), [PT[0]], [mkTs])
            psc, pdn, ppv = P[0], P[1], P[2]
            for hd in range(4):
                for mb in range(2):
                    c0 = (hd * 2 + mb) * 8
                    for j in range(2):
                        E("pe", lambda e: e.matmul(psc[:, c0:c0 + 8], lhsT=mkTs[:, 2 * hd + j, mb * 128:(mb + 1) * 128], rhs=qcT[:, 2 * hd + j, sq * 8:sq * 8 + 8],
                                                   start=(j == 0), stop=(j == 1)), [mkTs, qcT], [psc])
            E("act", lambda e: e.activation(out=pTs[:], in_=psc[:, 0:64], func=AF.Exp, scale=1.0 / 16), [psc], [pTs])
            for hd in range(4):
                for mb in range(2):
                    c0 = (hd * 2 + mb) * 8
                    E("pe", lambda e: e.matmul(pdn[:, hd * 8:hd * 8 + 8], lhsT=cstb[:, ONES, :], rhs=pTs[:, c0:c0 + 8], start=(mb == 0), stop=(mb == 1)), [cstb, pTs], [pdn])
            E("dve", lambda e: e.reciprocal(out=rdens[:], in_=pdn[:, 0:32]), [pdn], [rdens])
            for hd in range(4):
                for j in range(2):
                    for mb in range(2):
                        c0 = (hd * 2 + mb) * 8
                        E("pe", lambda e: e.matmul(ppv[:, (hd * 2 + j) * 8:(hd * 2 + j) * 8 + 8], lhsT=mv16s[:, mb, (2 * hd + j) * 128:(2 * hd + j + 1) * 128], rhs=pTs[:, c0:c0 + 8],
                                                   start=(mb == 0), stop=(mb == 1)), [mv16s, pTs], [ppv])
            E("dve", lambda e: e.tensor_tensor(out=ocT[:, :, sq * 8:sq * 8 + 8].rearrange("p (hd j) t -> p hd j t", j=2), in0=ppv[:, 0:64].rearrange("p (hd j t) -> p hd j t", hd=4, j=2),
                                               in1=rdens[:].rearrange("p (hd t) -> p hd t", hd=4).unsqueeze(2).to_broadcast([128, 4, 2, 8]), op=ALU.mult), [ppv, rdens], [ocT])
        linear_fm_to_brT(ocT, 128, 8, w_co, 128)
        postnorm_add(xT, 128, 3)
        fw.barrier()

        phase(16 * 1024)
        uext = [ar("uext_s%d" % i, [128, 514]) for i in range(2)]
        c_g = ar("c_g_s", [128, 512]); c_v = ar("c_v_s", [128, 512]); t_a = ar("t_a_s", [128, 512]); t_b = ar("t_b_s", [128, 512])
        actT = ar("actT_s", [128, 22, 512], BF16)
        cvrow_s = ar("cvrow_s", [64, 256])
        ubuf_s = ar("ubuf_s", [128, 44, 16, 2])
        sct = ar("sct", [32, 1408]); schi = ar("schi", [32, 1408], BF16); sclo = ar("sclo", [32, 1408], BF16); utmp = ar("utmp", [128, 352])
        scsem = fw.dmasem("sc")
        for ci in range(4):
            dma("sp", sct[:], sc_d[:, ci * 1408:(ci + 1) * 1408], [sc_d], [sct], scsem)
            E("dve", lambda e: e.tensor_copy(out=schi[:], in_=sct[:]), [sct], [schi])
            E("pool", lambda e: e.tensor_tensor(out=sclo[:], in0=sct[:], in1=schi[:], op=ALU.subtract), [sct, schi], [sclo])
            for b in range(11):
                E("pe", lambda e: e.transpose(out=PT[0][:, b * 32:(b + 1) * 32], in_=schi[:, b * 128:(b + 1) * 128], identity=cstb[0:32, IDENT, 0:32]), [schi, cstb], [PT[0]])
            for b in range(11):
                E("pe", lambda e: e.transpose(out=PT[1][:, b * 32:(b + 1) * 32], in_=sclo[:, b * 128:(b + 1) * 128], identity=cstb[0:32, IDENT, 0:32]), [sclo, cstb], [PT[1]])
            E("act", lambda e: e.activation(out=utmp[:], in_=PT[0][:, 0:352], func=AF.Copy), [PT[0]], [utmp])
            E("dve", lambda e: e.tensor_tensor(out=ubuf_s[:, ci * 11:(ci + 1) * 11, :, :].rearrange("p b s j -> p (b s j)"), in0=utmp[:], in1=PT[1][:, 0:352], op=ALU.add),
              [utmp, PT[1]], [ubuf_s])
        conv_ffn(128, False, True, True)
        fw.barrier()
        phase(0)
        yhi = ar("yhi_s", [128, 8, 128], BF16); ylo = ar("ylo_s", [128, 8, 128], BF16); yout = ar("yout_s", [128, D])
        store_y(ys_o, 0, 0)
        fw.barrier()

    hsem = fw.dmasem("hs")
    dma("pool", hs_o[:].rearrange("h k v -> k h v"), S[:], [S], [hs_o], hsem)

    fw.barrier()
    fw.replay()
    es.close()
    fw.close()
    return nc


_CACHE = {}


def _consts():
    c = np.zeros((128, 8, 128), np.float32)
    i = np.arange(128)
    same = (i[:, None] // 64) == (i[None, :] // 64)
    c[:, 0, :] = np.eye(128)
    c[:, 1, :] = ((i[:, None] <= i[None, :]) & same)
    c[:, 2, :] = ((i[:, None] > i[None, :]) & same)
    c[:, 3, :] = ((i[:, None] <= i[None, :]) & same)
    c[:, 4, :] = (i[:, None] >= i[None, :])
    c[:, 5, :] = (i[:, None] < i[None, :])
    c[:, 6, :] = 1.0
    c[:, 7, 0] = (i < 64)
    c[:, 7, 1] = (i >= 64)
    q = np.arange(512)
    dm = np.zeros((128, 4, 512), np.float32)
    for b in range(4):
        dm[:, b, :] = ((b * 128 + i)[:, None] < q[None, :])
    return c, dm


def _consts8():
    i = np.arange(128)
    same = (i[:, None] // 8) == (i[None, :] // 8)
    c = np.zeros((128, 4, 128), np.float32)
    c[:, 0, :] = ((i[:, None] <= i[None, :]) & same)
    c[:, 1, :] = ((i[:, None] > i[None, :]) & same)
    c[:, 2, :] = ((i[:, None] <= i[None, :]) & same)
    c[:, 3, 0:16] = (i[:, None] // 8 == np.arange(16)[None, :])
    sm = np.zeros((128, 16, 8), np.float32)
    for sq in range(16):
        sm[:, sq, :] = ((i[:, None] // 8 == sq) & ((i[:, None] % 8) < np.arange(8)[None, :]))
    return c, sm


def kernel(x_prompt, x_sample, cache_sb_k, cache_sb_v, state_hgrn, state_ffn_conv,
           cache_mem_k, cache_mem_v, page_table, mem_prompt,
           w_in, hg_norm, hg_lb, sb_bias, w_o, g_mix_pre, g_mix_post, g_ca_pre, g_ca_post, g_mem,
           w_cq, w_ck, w_cv, w_co, g_ffn_pre, g_ffn_post, w_up, conv_w, conv_b, w_down):
    f = np.float32
    csk_full = np.asarray(cache_sb_k, f)[0]
    n_pool = csk_full.shape[0]
    key = ("nc", n_pool)
    if key not in _CACHE:
        _CACHE[key] = build_program(n_pool)
    nc = _CACHE[key]
    csk_flat = csk_full.reshape(n_pool * 128, 512)
    csv_flat = np.asarray(cache_sb_v, f)[0].reshape(n_pool * 128, 512)
    c8, sm8 = _consts8()
    cst, dm = _consts()
    gs = np.stack([np.asarray(g, f)[0].reshape(8, 128).T for g in (g_mix_pre, g_mix_post, g_ca_pre, g_ca_post, g_mem, g_ffn_pre, g_ffn_post)], axis=1)
    hgn = np.ascontiguousarray(np.asarray(hg_norm, f)[0].reshape(8, 64).T)
    lbraw = np.ascontiguousarray(np.broadcast_to(np.asarray(hg_lb, f)[None], (128, 2, 512)))
    sbb = np.ascontiguousarray(np.broadcast_to(np.asarray(sb_bias, f)[0][None], (128, 8)))
    cw = np.asarray(conv_w, f)[0]
    cb = np.asarray(conv_b, f)[0]
    convp = np.ascontiguousarray(np.stack([cw[0], cw[1], cw[2], cb], 0).reshape(4, 44, 128).transpose(2, 0, 1))
    shared = dict(w_in=np.asarray(w_in, f)[0], w_o=np.asarray(w_o, f)[0], w_cq=np.asarray(w_cq, f)[0], w_ck=np.asarray(w_ck, f)[0],
                  w_cv=np.asarray(w_cv, f)[0], w_co=np.asarray(w_co, f)[0], w_up=np.asarray(w_up, f)[0], w_down=np.asarray(w_down, f)[0],
                  gvecs=np.ascontiguousarray(gs), hgn=hgn, lbraw=lbraw, sbb=sbb, convp=convp, cst=cst, dmask=dm,
                  csk=csk_flat, csv=csv_flat, cst8=c8, smask=sm8, iotaf=np.arange(128, dtype=f).reshape(128, 1))
    xsmp = np.asarray(x_sample, f)
    shg = np.asarray(state_hgrn, f)[0]
    sfc = np.asarray(state_ffn_conv, f)[0]
    cmk_ = np.asarray(cache_mem_k, f)[0]
    cmv_ = np.asarray(cache_mem_v, f)[0]
    ptab = np.asarray(page_table, np.int32)
    xp = np.asarray(x_prompt, f)
    in_maps = []
    for c in range(8):
        b, j = c // 4, c % 4
        lo = 2048 * j - 128 - NSP * 128
        full = np.zeros((NKB * 128, D), f)
        src_lo = max(lo, 0)
        full[src_lo - lo:] = xp[b, src_lo:2048 * j + 2048]
        kvalid = np.zeros((128, NKB), f)
        nvalid_from = (src_lo - lo) // 128
        kvalid[:, :nvalid_from] = -30000.0
        m = dict(shared)
        m.update(xa=np.ascontiguousarray(full[:NSP * 128]), xb=np.ascontiguousarray(full[NSP * 128:]), kvalid=kvalid,
                 halo_valid=np.full((128, 1), 0.0 if j == 0 else 1.0, f), mem=np.asarray(mem_prompt, f)[b],
                 xs=np.ascontiguousarray(xsmp[16 * c:16 * c + 16].reshape(128, D)),
                 pt=np.ascontiguousarray(ptab[16 * c:16 * c + 16].reshape(256)),
                 sh=np.ascontiguousarray(shg[16 * c:16 * c + 16]),
                 sc=np.ascontiguousarray(sfc[16 * c:16 * c + 16].reshape(32, 2 * DFF)),
                 cmk=np.ascontiguousarray(cmk_[16 * c:16 * c + 16].reshape(16, 256, D)),
                 cmv=np.ascontiguousarray(cmv_[16 * c:16 * c + 16].reshape(16, 256, D)))
        in_maps.append(m)
    res = run_bass_kernel_spmd(nc, in_maps, core_ids=list(range(8)))
    R = res.results
    yp = np.zeros((2, 8192, D), f)
    kp = np.zeros((1, 2, 8192, 8, 64), f)
    vp = np.zeros((1, 2, 8192, 8, 64), f)
    for c in range(8):
        b, j = c // 4, c % 4
        yp[b, 2048 * j:2048 * j + 2048] = R[c]["y"]
        kp[0, b, 2048 * j:2048 * j + 2048] = R[c]["kout"].reshape(2048, 8, 64)
        vp[0, b, 2048 * j:2048 * j + 2048] = R[c]["vout"].reshape(2048, 8, 64)
    hsp = np.stack([R[3]["hstate"], R[7]["hstate"]])[None]
    cvp = np.stack([R[3]["convout"], R[7]["convout"]])[None]
    mkp = np.stack([R[0]["memk"], R[4]["memk"]]).reshape(1, 2, 256, 4, 256)
    mvp = np.stack([R[0]["memv"], R[4]["memv"]]).reshape(1, 2, 256, 4, 256)
    ys = np.concatenate([R[c]["ys"].reshape(16, 8, D) for c in range(8)], 0)
    ks = np.concatenate([R[c]["ksout"].reshape(16, 8, 8, 64) for c in range(8)], 0)[None]
    vs = np.concatenate([R[c]["vsout"].reshape(16, 8, 8, 64) for c in range(8)], 0)[None]
    hss = np.concatenate([R[c]["hss"] for c in range(8)], 0)[None]
    cvs = np.concatenate([R[c]["cvs"].reshape(16, 2, 2 * DFF) for c in range(8)], 0)[None]
    return (yp, ys, kp, vp, hsp.astype(f), cvp.astype(f), mkp, mvp, ks, vs, hss, cvs)
```

```python
import contextlib
import os
import types
import numpy as np
import concourse.bass as bass
import concourse.mybir as mybir
from concourse.bass_utils import run_bass_kernel_spmd

F32 = mybir.dt.float32
BF16 = mybir.dt.bfloat16
I32 = mybir.dt.int32
AF = mybir.ActivationFunctionType
ALU = mybir.AluOpType

NSP = 47
NSO = 17
NKB = NSP + NSO
D = 1024
DFF = 2816
EPS = 1e-6


class Reg:
    __slots__ = ("w", "rs", "name", "excl")

    def __init__(self, name=""):
        self.w = None
        self.rs = []
        self.name = name
        self.excl = False


class Buf:
    def __init__(self, t, name):
        self.t = t
        self.r = Reg(name)

    def __getitem__(self, k):
        return self.t[k]


class Fw:
    COMPUTE = ("pe", "act", "dve", "pool")

    def __init__(self, nc):
        self.nc = nc
        self.streams = {e: [] for e in ("pe", "act", "dve", "pool", "sp")}
        self.sems = {}
        self.semval = {}
        self.waited = {e: {} for e in self.streams}
        self._cms = []
        for e in self.COMPUTE:
            self._newsem("E_" + e)
        self.nd = 0

    def _newsem(self, key):
        cm = self.nc.semaphore(key)
        h = cm.__enter__()
        self._cms.append(cm)
        self.sems[key] = h
        self.semval[key] = 0
        return key

    def dmasem(self, name=None):
        self.nd += 1
        return self._newsem("D_%s_%d" % (name or "x", self.nd))

    def _wait(self, eng, toks):
        need = {}
        for t in toks:
            if t is None:
                continue
            k, v = t
            if v > need.get(k, 0):
                need[k] = v
        for k, v in need.items():
            if v > self.waited[eng].get(k, 0):
                self.waited[eng][k] = v
                self.streams[eng].append(("wait", k, v))

    @staticmethod
    def freeze(fn):
        if fn.__closure__ is None:
            return fn
        cells = []
        for c in fn.__closure__:
            try:
                cells.append(types.CellType(c.cell_contents))
            except ValueError:
                cells.append(c)
        return types.FunctionType(fn.__code__, fn.__globals__, fn.__name__, fn.__defaults__, tuple(cells))

    def op(self, eng, fn, reads=(), writes=(), dsem=None):
        fn = self.freeze(fn)
        toks = []
        own = "E_" + eng
        reads = [b.r if isinstance(b, Buf) else b for b in reads]
        writes = [b.r if isinstance(b, Buf) else b for b in writes]
        writes = writes + [r for r in reads if r.excl and r not in writes]
        for r in reads:
            toks.append(r.w)
        for w in writes:
            toks.append(w.w)
            toks.extend(w.rs)
        if eng == "pe":
            toks = [t for t in toks if t is not None and t[0] != own]
        self._wait(eng, toks)
        if dsem is not None:
            k, inc = dsem, 16
        else:
            k, inc = own, 1
        self.semval[k] += inc
        tok = (k, self.semval[k])
        self.streams[eng].append(("op", fn, k, inc))
        for w in writes:
            w.w = tok
            w.rs = []
        for r in reads:
            if r not in writes:
                r.rs.append(tok)
        return tok

    def barrier(self):
        allt = [(k, v) for k, v in self.semval.items() if v > 0]
        for e in self.streams:
            self._wait(e, allt)

    def replay(self):
        nc = self.nc
        names = {"pe": "tensor", "act": "scalar", "dve": "vector", "pool": "gpsimd", "sp": "sync"}
        with nc.Block() as block:
            for e, bn in names.items():
                items = self.streams[e]

                def body(engine, items=items):
                    for it in items:
                        if it[0] == "wait":
                            engine.wait_ge(self.sems[it[1]], it[2])
                        else:
                            it[1](engine).then_inc(self.sems[it[2]], it[3])
                getattr(block, bn)(body)

    def close(self):
        for cm in reversed(self._cms):
            cm.__exit__(None, None, None)


class Rot:
    def __init__(self, bufs):
        self.bufs = bufs
        self.i = 0

    def next(self):
        b = self.bufs[self.i % len(self.bufs)]
        self.i += 1
        return b


def build_program(n_pool=2560):
    nc = bass.Bass("TRN2", target_bir_lowering=False)
    fw = Fw(nc)
    es = contextlib.ExitStack()

    def din(name, shape, dt=F32):
        return Buf(nc.dram_tensor(name, list(shape), dt, kind="ExternalInput").ap(), name)

    def dout(name, shape, dt=F32):
        return Buf(nc.dram_tensor(name, list(shape), dt, kind="ExternalOutput").ap(), name)

    def dscr(name, shape, dt):
        return Buf(nc.dram_tensor(name, list(shape), dt, kind="Internal").ap(), name)

    def sb(name, shape, dt=F32, stack=None):
        return Buf((stack or es).enter_context(nc.sbuf_tensor(name, list(shape), dt)), name)

    def ps(name, shape, dt=F32, stack=None):
        b = Buf((stack or es).enter_context(nc.psum_tensor(name, list(shape), dt)), name)
        b.r.excl = True
        return b

    ARENA = 80 * 1024
    arena_t = es.enter_context(nc.sbuf_tensor("arena", [128, ARENA // 2], BF16))
    cur = [0]

    def phase(base):
        cur[0] = base

    def ar(name, shape, dt=F32):
        esz = 4 if dt == F32 else 2
        n = 1
        for d in shape[1:]:
            n *= d
        nb = (n * esz + 31) // 32 * 32
        off = cur[0]
        cur[0] += nb
        assert cur[0] <= ARENA, (name, cur[0])
        v = arena_t[0:shape[0], off // 2:(off + n * esz) // 2]
        if dt == F32:
            v = v.bitcast(F32)
        if len(shape) == 3:
            v = v.rearrange("p (a b) -> p a b", a=shape[1])
        elif len(shape) == 4:
            v = v.rearrange("p (a b c) -> p a b c", a=shape[1], b=shape[2])
        return Buf(v, name)

    xa = din("xa", [NSP * 128, D])
    xb = din("xb", [NSO * 128, D])
    kvalid = din("kvalid", [128, NKB])
    halo_valid = din("halo_valid", [128, 1])
    mem = din("mem", [256, D])
    w_in = din("w_in", [D, 3584])
    w_o = din("w_o", [D, D])
    w_cq = din("w_cq", [D, D])
    w_ck = din("w_ck", [D, D])
    w_cv = din("w_cv", [D, D])
    w_co = din("w_co", [D, D])
    w_up = din("w_up", [D, 2 * DFF])
    w_down = din("w_down", [DFF, D])
    gvecs = din("gvecs", [128, 7, 8])
    hgn = din("hgn", [64, 8])
    lbraw = din("lbraw", [128, 2, 512])
    sbb = din("sbb", [128, 8])
    convp = din("convp", [128, 4, 44])
    cst = din("cst", [128, 8, 128])
    dmask_d = din("dmask", [128, 4, 512])

    xs_d = din("xs", [128, D])
    csk = din("csk", [n_pool * 128, 512])
    csv = din("csv", [n_pool * 128, 512])
    pt_d = din("pt", [256], I32)
    iota_d = din("iotaf", [128, 1])
    sh_d = din("sh", [16, 8, 64, 64])
    sc_d = din("sc", [32, 2 * DFF])
    cmk = din("cmk", [16, 256, D])
    cmv = din("cmv", [16, 256, D])
    cst8 = din("cst8", [128, 4, 128])
    smask_d = din("smask", [128, 16, 8])
    ys_o = dout("ys", [128, D])
    ks_o = dout("ksout", [128, 512])
    vs_o = dout("vsout", [128, 512])
    hss_o = dout("hss", [16, 8, 64, 64])
    cvs_o = dout("cvs", [32, 2 * DFF])

    y_o = dout("y", [2048, D])
    k_o = dout("kout", [2048, 512])
    v_o = dout("vout", [2048, 512])
    hs_o = dout("hstate", [8, 64, 64])
    cv_o = dout("convout", [2, 2 * DFF])
    mk_o = dout("memk", [256, D])
    mv_o = dout("memv", [256, D])

    WB = {}
    for wd_, shp in ((w_in, [D, 3584]), (w_o, [D, D]), (w_cq, [D, D]), (w_ck, [D, D]), (w_cv, [D, D]), (w_co, [D, D]), (w_up, [D, 2 * DFF]), (w_down, [DFF, D])):
        WB[wd_.r.name] = dscr(wd_.r.name + "_bf", shp, BF16)
    WD = {w_.r.name: w_ for w_ in (w_in, w_o, w_cq, w_ck, w_cv, w_co, w_up, w_down)}
    kT_scr = dscr("kT_scr", [8, 64, NKB * 128], BF16)
    v_scr = dscr("v_scr", [8, 128, NKB, 64], BF16)

    outs = [y_o, k_o, v_o, hs_o, cv_o, mk_o, mv_o]

    cstf = sb("cstf", [128, 8, 128])
    cstb = sb("cstb", [128, 8, 128], BF16)
    IDENT, TRILI, TRIUS, BDM, UINC, LSTR, ONES, CSEL = range(8)
    dmaskf = sb("dmaskf", [128, 4, 512], BF16)
    phase(0)
    dmask32 = ar("dmask32", [128, 4, 512])
    gv = sb("gv", [128, 7, 8])
    hgn_sb = sb("hgn_sb", [64, 8])
    lb = sb("lb", [128, 512])
    oml = sb("oml", [128, 512])
    lbtmp = ar("lbtmp", [128, 2, 512])
    sbb_sb = sb("sbb_sb", [128, 8])
    kval_sb = sb("kval_sb", [128, NKB])
    biasv = sb("biasv", [128, NKB, 8])
    halo_sb = sb("halo_sb", [128, 1])
    convp_sb = sb("convp_sb", [128, 4, 44])
    epsb = sb("epsb", [128, 1])
    S = sb("S", [64, 8, 64])
    S16 = sb("S16", [64, 8, 64], BF16)
    Sb16 = sb("Sb16", [64, 8, 64], BF16)
    ubuf = sb("ubuf", [128, 44, 2])

    c8f = sb("c8f", [128, 4, 128])
    c8b = sb("c8b", [128, 4, 128], BF16)
    smask = sb("smask_sb", [128, 16, 8])
    expb = sb("expb", [128, 8])
    iotaf = sb("iotaf_sb", [128, 1])
    ptb = sb("ptb", [128, 256], I32)
    ptf = sb("ptf", [128, 256])
    idx = sb("idx", [128, 256], I32)

    def E(eng, fn, reads=(), writes=()):
        return fw.op(eng, fn, reads, writes)

    def dma(eng, out_ap, in_ap, reads, writes, sem, **kw):
        return fw.op(eng, lambda e: e.dma_start(out=out_ap, in_=in_ap, **kw), reads, writes, dsem=sem)

    dma("sp", cstf[:], cst[:], [cst], [cstf], fw.dmasem("setup"))
    dma("sp", dmask32[:], dmask_d[:], [dmask_d], [dmask32], fw.dmasem("setup"))
    dma("sp", gv[:], gvecs[:], [gvecs], [gv], fw.dmasem("setup"))
    dma("sp", hgn_sb[:], hgn[:], [hgn], [hgn_sb], fw.dmasem("setup"))
    dma("sp", lbtmp[:], lbraw[:], [lbraw], [lbtmp], fw.dmasem("setup"))
    dma("sp", sbb_sb[:], sbb[:], [sbb], [sbb_sb], fw.dmasem("setup"))
    dma("sp", kval_sb[:], kvalid[:], [kvalid], [kval_sb], fw.dmasem("setup"))
    dma("sp", halo_sb[:], halo_valid[:], [halo_valid], [halo_sb], fw.dmasem("setup"))
    dma("sp", convp_sb[:], convp[:], [convp], [convp_sb], fw.dmasem("setup"))
    dma("sp", c8f[:], cst8[:], [cst8], [c8f], fw.dmasem("setup"))
    dma("sp", smask[:], smask_d[:], [smask_d], [smask], fw.dmasem("setup"))
    dma("sp", iotaf[:], iota_d[:], [iota_d], [iotaf], fw.dmasem("setup"))
    dma("sp", ptb[:], pt_d[:].partition_broadcast(128), [pt_d], [ptb], fw.dmasem("setup"))
    E("dve", lambda e: e.tensor_copy(out=c8b[:], in_=c8f[:]), [c8f], [c8b])
    E("act", lambda e: e.activation(out=expb[:], in_=sbb_sb[:], func=AF.Exp), [sbb_sb], [expb])
    E("dve", lambda e: e.tensor_copy(out=ptf[:], in_=ptb[:]), [ptb], [ptf])
    E("dve", lambda e: e.tensor_scalar(out=ptf[:], in0=ptf[:], scalar1=128.0, scalar2=iotaf[:, 0:1], op0=ALU.mult, op1=ALU.add), [ptf, iotaf], [ptf])
    E("dve", lambda e: e.tensor_copy(out=idx[:], in_=ptf[:]), [ptf], [idx])
    E("dve", lambda e: e.tensor_copy(out=cstb[:], in_=cstf[:]), [cstf], [cstb])
    E("dve", lambda e: e.tensor_copy(out=dmaskf[:], in_=dmask32[:]), [dmask32], [dmaskf])
    E("pool", lambda e: e.memset(epsb[:], EPS), [], [epsb])
    E("pool", lambda e: e.memset(S[:], 0.0), [], [S])
    E("pool", lambda e: e.memset(S16[:], 0.0), [], [S16])
    E("pool", lambda e: e.memset(ubuf[:], 0.0), [], [ubuf])
    E("dve", lambda e: e.tensor_tensor(out=lb[:], in0=lbtmp[:, 0, :], in1=lbtmp[:, 1, :], op=ALU.subtract), [lbtmp], [lb])
    E("act", lambda e: e.activation(out=lb[:], in_=lb[:], func=AF.Sigmoid), [lb], [lb])
    E("dve", lambda e: e.tensor_scalar(out=oml[:], in0=lb[:], scalar1=-1.0, scalar2=1.0, op0=ALU.mult, op1=ALU.add), [lb], [oml])
    E("dve", lambda e: e.tensor_tensor(out=biasv[:], in0=kval_sb[:].unsqueeze(2).to_broadcast([128, NKB, 8]),
                                       in1=sbb_sb[:].unsqueeze(1).to_broadcast([128, NKB, 8]), op=ALU.add),
      [kval_sb, sbb_sb], [biasv])

    P = [ps("P%d" % i, [128, 512]) for i in range(6)]
    PT = [ps("PT%d" % i, [128, 1024], BF16) for i in range(2)]

    wst_rot = Rot([sb("wst%d" % i, [128, 2048]) for i in range(2)])
    wbf_rot = Rot([sb("wbf%d" % i, [128, 8, 512], BF16) for i in range(2)])
    for b_ in wst_rot.bufs + wbf_rot.bufs:
        b_.sem = fw.dmasem("w")
        b_.sem2 = fw.dmasem("wo")

    def precast_weights():
        n = 0
        for name, wb_d in WB.items():
            wd = WD[name]
            K, N = wd.t.shape
            for r0 in range(0, K, 128):
                for c0 in range(0, N, 2048):
                    w = min(2048, N - c0)
                    st = wst_rot.next()
                    tb = wbf_rot.next()
                    tbv = tb.t.rearrange("p a b -> p (a b)")[:, 0:w]
                    dma("sp", st[:, 0:w], wd[r0:r0 + 128, c0:c0 + w], [wd], [st], st.sem)
                    if n % 2 == 0:
                        E("dve", lambda e: e.tensor_copy(out=tbv, in_=st[:, 0:w]), [st], [tb])
                    else:
                        E("act", lambda e: e.activation(out=tbv, in_=st[:, 0:w], func=AF.Copy), [st], [tb])
                    dma("pool", wb_d[r0:r0 + 128, c0:c0 + w], tbv, [tb], [wb_d], tb.sem2)
                    n += 1

    def load_w(wd, c0, ncols, kp=128, kc=8, r0=0):
        wb = wbf_rot.next()
        wsrc = WB[wd.r.name]
        src = wsrc[r0:r0 + kc * kp, c0:c0 + ncols].rearrange("(c p) n -> p c n", p=kp)
        assert kc * ncols <= 8 * 512
        wbv = wb.t[0:kp].rearrange("p a b -> p (a b)")[:, 0:kc * ncols].rearrange("p (c n) -> p c n", c=kc)
        dma("sp", wbv, src, [wsrc], [wb], wb.sem)
        return wb, wbv

    sqb = sb("sqb", [128, 8, 512], BF16)
    rstd = sb("rstd", [128, 512])

    def rms_T(srcT, nt, scale_n):
        for c in range(8):
            E("act", lambda e, c=c: e.activation(out=sqb[:, c, 0:nt], in_=srcT[:, c, 0:nt], func=AF.Square), [srcT], [sqb])
        pb = P[5]
        for c in range(8):
            E("pe", lambda e, c=c: e.matmul(pb[:, 0:nt], lhsT=cstb[:, ONES, :], rhs=sqb[:, c, 0:nt], start=(c == 0), stop=(c == 7)),
              [cstb, sqb], [pb])
        E("act", lambda e: e.activation(out=rstd[:, 0:nt], in_=pb[:, 0:nt], func=AF.Ln, scale=1.0 / scale_n, bias=epsb[:]), [pb, epsb], [rstd])
        E("act", lambda e: e.activation(out=rstd[:, 0:nt], in_=rstd[:, 0:nt], func=AF.Exp, scale=-0.5), [rstd], [rstd])

    def prenorm(xT, hT, nt, gi):
        rms_T(xT, nt, 1024.0)
        for c in range(8):
            E("dve", lambda e, c=c: e.scalar_tensor_tensor(out=hT[:, c, 0:nt], in0=xT[:, c, 0:nt], scalar=gv[:, gi, c:c + 1],
                                                            in1=rstd[:, 0:nt], op0=ALU.mult, op1=ALU.mult), [xT, gv, rstd], [hT])

    phase(0)
    brT = ar("brT", [128, 8, 512])

    def postnorm_add(xT, nt, gi):
        rms_T(brT, nt, 1024.0)
        for c in range(8):
            E("dve", lambda e, c=c: e.tensor_tensor(out=brT[:, c, 0:nt], in0=brT[:, c, 0:nt], in1=rstd[:, 0:nt], op=ALU.mult), [brT, rstd], [brT])
            E("dve", lambda e, c=c: e.scalar_tensor_tensor(out=xT[:, c, 0:nt], in0=brT[:, c, 0:nt], scalar=gv[:, gi, c:c + 1],
                                                            in1=xT[:, c, 0:nt], op0=ALU.mult, op1=ALU.add), [brT, gv, xT], [xT])

    prot = Rot([P[0], P[1]])

    def linear_fm_to_brT(inT, kp, kc, wd, nt):
        for q in range(2):
            wb, wbv = load_w(wd, q * 512, 512)
            for b4 in range(4):
                blk = q * 4 + b4
                pb = prot.next()
                for k in range(8):
                    E("pe", lambda e: e.matmul(pb[:, 0:nt], lhsT=wbv[:, k, b4 * 128:(b4 + 1) * 128], rhs=inT[:, k, 0:nt], start=(k == 0), stop=(k == 7)), [wb, inT], [pb])
                E("act", lambda e: e.activation(out=brT[:, blk, 0:nt], in_=pb[:, 0:nt], func=AF.Copy), [pb], [brT])

    phase(0)
    xt_rot = Rot([ar("xt%d" % i, [128, D]) for i in range(2)])
    xsem = [fw.dmasem("x") for _ in range(2)]
    xcnt = [0]
    xhi = ar("xhi", [128, D], BF16)
    xlo = ar("xlo", [128, D], BF16)
    xtmp = ar("xtmp", [128, 8, 128])

    def load_xT(src, row0, xT, col0):
        i = xcnt[0] % 2
        xcnt[0] += 1
        xt = xt_rot.next()
        dma("sp", xt[:], src[row0:row0 + 128, :], [src], [xt], xsem[i])
        E("dve", lambda e: e.tensor_copy(out=xhi[:], in_=xt[:]), [xt], [xhi])
        E("pool", lambda e: e.tensor_tensor(out=xlo[:], in0=xt[:], in1=xhi[:], op=ALU.subtract), [xt, xhi], [xlo])
        for c in range(8):
            E("pe", lambda e, c=c: e.transpose(out=PT[0][:, c * 128:(c + 1) * 128], in_=xhi[:, c * 128:(c + 1) * 128], identity=cstb[:, IDENT, :]),
              [xhi, cstb], [PT[0]])
        for c in range(8):
            E("pe", lambda e, c=c: e.transpose(out=PT[1][:, c * 128:(c + 1) * 128], in_=xlo[:, c * 128:(c + 1) * 128], identity=cstb[:, IDENT, :]),
              [xlo, cstb], [PT[1]])
        E("act", lambda e: e.activation(out=xtmp[:], in_=PT[0][:].rearrange("p (c n) -> p c n", c=8), func=AF.Copy), [PT[0]], [xtmp])
        E("dve", lambda e: e.tensor_tensor(out=xT[:, :, col0:col0 + 128], in0=xtmp[:], in1=PT[1][:].rearrange("p (c n) -> p c n", c=8), op=ALU.add),
          [xtmp, PT[1]], [xT])

    xT = sb("xT", [128, 8, 512])
    hT = sb("hT", [128, 8, 512], BF16)
    phase(0)
    sqT = ar("sqT", [64, 8, 512], BF16)
    qT = ar("qT", [64, 8, 512], BF16)
    gT = ar("gT", [64, 8, 512], BF16)
    f_sb = ar("f_sb", [128, 512])
    lf = ar("lf", [128, 512])
    lfh = ar("lfh", [128, 512], BF16)
    lfl = ar("lfl", [128, 512], BF16)
    k16 = ar("k16", [128, 512], BF16)
    kdd = ar("kdd", [128, 512], BF16)
    eD = ar("eD", [128, 512])
    v16 = ar("v16", [128, 512], BF16)
    vm = ar("vm", [128, 8, 2, 64], BF16)
    ebT = ar("ebT", [64, 8, 128])
    enbT = ar("enbT", [64, 8, 128])
    ebl = ar("ebl", [64, 8, 2])
    qtT = ar("qtT", [64, 8, 128], BF16)
    ktT = ar("ktT", [64, 8, 128], BF16)
    attm = ar("attm", [128, 8, 128], BF16)
    stmp = ar("stmp", [64, 8, 64])
    osq = ar("osq", [64, 8, 128], BF16)
    orst = enbT
    otmp = ebT
    mixT = sb("mixT", [64, 16, 512], BF16)
    kvout = ar("kvout", [128, 512])
    kvsem = fw.dmasem("kv")
    scrsem = fw.dmasem("scrk")
    scrsemV = fw.dmasem("scrv")

    def tm_proj(wbv, wb, s, pb):
        for k in range(8):
            E("pe", lambda e, k=k: e.matmul(pb[:], lhsT=hT[:, k, s * 128:(s + 1) * 128], rhs=wbv[:, k, :], start=(k == 0), stop=(k == 7)), [hT, wb], [pb])

    def fm_proj64(wbv, wb, h, nt, pb):
        for k in range(8):
            E("pe", lambda e, k=k: e.matmul(pb[0:64, 0:nt], lhsT=wbv[:, k, h * 64:(h + 1) * 64], rhs=hT[:, k, 0:nt], start=(k == 0), stop=(k == 7)), [wb, hT], [pb])

    def hgrn_subtile(s, own, hf_ps, hi_ps, smp=False):
        tus = c8b[:, 1, :] if smp else cstb[:, TRIUS, :]
        tli = c8b[:, 0, :] if smp else cstb[:, TRILI, :]
        bdm = c8f[:, 2, :] if smp else cstf[:, BDM, :]
        cbuf = c8b if smp else cstb
        cfbuf = c8f if smp else cstf
        E("act", lambda e: e.activation(out=f_sb[:], in_=hf_ps[:], func=AF.Sigmoid), [hf_ps], [f_sb])
        E("dve", lambda e: e.tensor_tensor(out=f_sb[:], in0=f_sb[:], in1=oml[:], op=ALU.mult), [f_sb, oml], [f_sb])
        E("dve", lambda e: e.tensor_tensor(out=f_sb[:], in0=f_sb[:], in1=lb[:], op=ALU.add), [f_sb, lb], [f_sb])
        E("act", lambda e: e.activation(out=lf[:], in_=f_sb[:], func=AF.Ln), [f_sb], [lf])
        E("dve", lambda e: e.tensor_scalar(out=k16[:], in0=f_sb[:], scalar1=-1.0, scalar2=1.0, op0=ALU.mult, op1=ALU.add), [f_sb], [k16])
        E("dve", lambda e: e.tensor_copy(out=lfh[:], in_=lf[:]), [lf], [lfh])
        E("pool", lambda e: e.tensor_tensor(out=lfl[:], in0=lf[:], in1=lfh[:], op=ALU.subtract), [lf, lfh], [lfl])
        E("act", lambda e: e.activation(out=v16[:], in_=hi_ps[:], func=AF.Copy), [hi_ps], [v16])
        pd = P[4]
        E("pe", lambda e: e.matmul(pd[:], lhsT=tus, rhs=lfh[:], start=True, stop=False), [cbuf, lfh], [pd])
        E("pe", lambda e: e.matmul(pd[:], lhsT=tus, rhs=lfl[:], start=False, stop=True), [cbuf, lfl], [pd])
        E("act", lambda e: e.activation(out=eD[:], in_=pd[:], func=AF.Exp), [pd], [eD])
        E("dve", lambda e: e.tensor_tensor(out=kdd[:], in0=k16[:], in1=eD[:], op=ALU.mult), [k16, eD], [kdd])
        if not smp:
            E("pool", lambda e: e.tensor_tensor(out=vm[:], in0=v16[:].rearrange("p (h v) -> p h v", h=8).unsqueeze(2).to_broadcast([128, 8, 2, 64]),
                                                in1=cstb[:, CSEL, 0:2].unsqueeze(1).unsqueeze(3).to_broadcast([128, 8, 2, 64]), op=ALU.mult),
              [v16, cstb], [vm])
        pbt = P[2], P[3]
        for h in range(8):
            pb_ = pbt[h // 4]
            o = pb_[0:64, (h % 4) * 128:(h % 4 + 1) * 128]
            E("pe", lambda e, h=h, o=o: e.matmul(o, lhsT=lfh[:, h * 64:(h + 1) * 64], rhs=tli, start=True, stop=False), [lfh, cbuf], [pb_])
            E("pe", lambda e, h=h, o=o: e.matmul(o, lhsT=lfl[:, h * 64:(h + 1) * 64], rhs=tli, start=False, stop=True), [lfl, cbuf], [pb_])
        for half in range(2):
            pb_ = pbt[half]
            E("act", lambda e, half=half, pb_=pb_: e.activation(out=ebT[:, half * 4:(half + 1) * 4, :], in_=pb_[0:64, :].rearrange("p (h t) -> p h t", h=4), func=AF.Exp),
              [pb_], [ebT])
            if own:
                E("act", lambda e, half=half, pb_=pb_: e.activation(out=enbT[:, half * 4:(half + 1) * 4, :], in_=pb_[0:64, :].rearrange("p (h t) -> p h t", h=4), func=AF.Exp, scale=-1.0),
                  [pb_], [enbT])
        if smp:
            E("dve", lambda e: e.tensor_copy(out=ebl16[:], in_=ebT[:].rearrange("p h (c t) -> p h c t", c=16)[:, :, :, 7]), [ebT], [ebl16])
        else:
            E("dve", lambda e: e.tensor_copy(out=ebl[:], in_=ebT[:].rearrange("p h (c t) -> p h c t", c=2)[:, :, :, 63]), [ebT], [ebl])
        if own:
            E("dve", lambda e: e.tensor_tensor(out=qtT[:], in0=qT[:, :, s * 128:(s + 1) * 128], in1=ebT[:], op=ALU.mult), [qT, ebT], [qtT])
            for h in range(8):
                E("pe", lambda e, h=h: e.transpose(out=PT[0][0:64, h * 128:(h + 1) * 128], in_=k16[:, h * 64:(h + 1) * 64], identity=cstb[:, IDENT, :]), [k16, cstb], [PT[0]])
            E("dve", lambda e: e.tensor_tensor(out=ktT[:], in0=PT[0][0:64, :].rearrange("p (h t) -> p h t", h=8), in1=enbT[:], op=ALU.mult), [PT[0], enbT], [ktT])
            pat = P[2], P[3]
            for h in range(8):
                pb_ = pat[h // 4]
                E("pe", lambda e, h=h, pb_=pb_: e.matmul(pb_[:, (h % 4) * 128:(h % 4 + 1) * 128], lhsT=ktT[:, h, :], rhs=qtT[:, h, :], start=True, stop=True), [ktT, qtT], [pb_])
            for half in range(2):
                pb_ = pat[half]
                E("dve", lambda e, half=half, pb_=pb_: e.tensor_tensor(out=attm[:, half * 4:(half + 1) * 4, :], in0=pb_[:].rearrange("p (h t) -> p h t", h=4),
                                                                      in1=bdm.unsqueeze(1).to_broadcast([128, 4, 128]), op=ALU.mult), [pb_, cfbuf], [attm])
        po = P[2], P[3]
        if smp:
            for h in range(8):
                s0f = s0f_rot.next()
                s16h = s16h_rot.next()
                vmh = vmh_rot.next()
                dma("sp", s0f[:], sh_d[:, h, :, :].rearrange("s k v -> k s v"), [sh_d], [s0f], s0f.sem)
                E("pool", lambda e: e.tensor_copy(out=s16h[:], in_=s0f[:]), [s0f], [s16h])
                E("pool", lambda e: e.tensor_tensor(out=vmh[:], in0=v16[:, h * 64:(h + 1) * 64].unsqueeze(1).to_broadcast([128, 16, 64]),
                                                    in1=c8b[:, 3, 0:16].unsqueeze(2).to_broadcast([128, 16, 64]), op=ALU.mult), [v16, c8b], [vmh])
                for half in range(2):
                    pb_ = P[half]
                    E("pe", lambda e: e.matmul(pb_[0:64, :], lhsT=kdd[:, h * 64:(h + 1) * 64], rhs=vmh[:, half * 8:(half + 1) * 8, :].rearrange("p c v -> p (c v)"),
                                               start=True, stop=True), [kdd, vmh], [pb_])
                E("dve", lambda e: e.tensor_tensor(out=stmp16[:], in0=s0f[:], in1=ebl16[:, h, :].unsqueeze(2).to_broadcast([64, 16, 64]), op=ALU.mult), [s0f, ebl16], [stmp16])
                for half in range(2):
                    pb_ = P[half]
                    E("dve", lambda e: e.tensor_tensor(out=s0f[:, half * 8:(half + 1) * 8, :], in0=stmp16[:, half * 8:(half + 1) * 8, :],
                                                       in1=pb_[0:64, :].rearrange("p (c v) -> p c v", c=8), op=ALU.add), [stmp16, pb_], [s0f])
                dma("pool", hss_o[:, h, :, :].rearrange("s k v -> k s v"), s0f[:], [s0f], [hss_o], s0f.sem2)
                pb_ = po[h // 4]
                c0 = (h % 4) * 128
                E("pe", lambda e: e.matmul(pb_[0:64, c0:c0 + 128], lhsT=v16[:, h * 64:(h + 1) * 64], rhs=attm[:, h, :], start=True, stop=False), [v16, attm], [pb_])
                for c in range(16):
                    E("pe", lambda e: e.matmul(pb_[0:64, c0 + c * 8:c0 + c * 8 + 8], lhsT=s16h[:, c, :], rhs=qtT[:, h, c * 8:c * 8 + 8], start=False, stop=(c == 15)),
                      [s16h, qtT], [pb_])
        else:
            pp = P[0], P[1]
            for h in range(8):
                pb_ = pp[h // 4]
                E("pe", lambda e, h=h, pb_=pb_: e.matmul(pb_[0:64, (h % 4) * 128:(h % 4 + 1) * 128], lhsT=kdd[:, h * 64:(h + 1) * 64], rhs=vm[:, h, :, :].rearrange("p c v -> p (c v)"),
                                                         start=True, stop=True), [kdd, vm], [pb_])
            po = P[2], P[3]

            def chain(c, dst16):
                E("dve", lambda e: e.tensor_tensor(out=stmp[:], in0=S[:], in1=ebl[:, :, c:c + 1].to_broadcast([64, 8, 64]), op=ALU.mult), [S, ebl], [stmp])
                for half in range(2):
                    pb_ = pp[half]
                    E("dve", lambda e: e.tensor_tensor(out=S[:, half * 4:(half + 1) * 4, :], in0=stmp[:, half * 4:(half + 1) * 4, :],
                                                       in1=pb_[0:64, :].rearrange("p (h c v) -> p h c v", h=4, c=2)[:, :, c, :], op=ALU.add), [stmp, pb_], [S])
                E("pool", lambda e: e.tensor_copy(out=dst16[:], in_=S[:]), [S], [dst16])

            chain(0, Sb16)
            if own:
                for h in range(8):
                    pb_ = po[h // 4]
                    c0 = (h % 4) * 128
                    E("pe", lambda e: e.matmul(pb_[0:64, c0:c0 + 128], lhsT=v16[:, h * 64:(h + 1) * 64], rhs=attm[:, h, :], start=True, stop=False), [v16, attm], [pb_])
                    E("pe", lambda e: e.matmul(pb_[0:64, c0:c0 + 64], lhsT=S16[:, h, :], rhs=qtT[:, h, 0:64], start=False, stop=False), [S16, qtT], [pb_])
                    E("pe", lambda e: e.matmul(pb_[0:64, c0 + 64:c0 + 128], lhsT=Sb16[:, h, :], rhs=qtT[:, h, 64:128], start=False, stop=True), [Sb16, qtT], [pb_])
            chain(1, S16)
        if own:
            for half in range(2):
                pb_ = po[half]
                E("act", lambda e, half=half, pb_=pb_: e.activation(out=osq[:, half * 4:(half + 1) * 4, :], in_=pb_[0:64, :].rearrange("p (h t) -> p h t", h=4), func=AF.Square), [pb_], [osq])
            pr = P[0], P[1]
            for half in range(2):
                pb_ = pr[half]
                E("pe", lambda e, half=half, pb_=pb_: e.matmul(pb_[0:64, :], lhsT=cstb[0:64, ONES, 0:64], rhs=osq[:, half * 4:(half + 1) * 4, :].rearrange("p h t -> p (h t)"),
                                                               start=True, stop=True), [cstb, osq], [pb_])
                E("act", lambda e, half=half, pb_=pb_: e.activation(out=orst[:, half * 4:(half + 1) * 4, :], in_=pb_[0:64, :].rearrange("p (h t) -> p h t", h=4), func=AF.Ln,
                                                                    scale=1.0 / 64, bias=epsb[0:64, :]), [pb_, epsb], [orst])
            E("act", lambda e: e.activation(out=orst[:], in_=orst[:], func=AF.Exp, scale=-0.5), [orst], [orst])
            for half in range(2):
                pb_ = po[half]
                E("dve", lambda e, half=half, pb_=pb_: e.tensor_tensor(out=otmp[:, half * 4:(half + 1) * 4, :], in0=pb_[0:64, :].rearrange("p (h t) -> p h t", h=4),
                                                                      in1=orst[:, half * 4:(half + 1) * 4, :], op=ALU.mult), [pb_, orst], [otmp])
            E("dve", lambda e: e.tensor_tensor(out=otmp[:], in0=otmp[:], in1=hgn_sb[:].unsqueeze(2).to_broadcast([64, 8, 128]), op=ALU.mult), [otmp, hgn_sb], [otmp])
            E("dve", lambda e: e.tensor_tensor(out=mixT[:, 0:8, s * 128:(s + 1) * 128], in0=otmp[:], in1=gT[:, :, s * 128:(s + 1) * 128], op=ALU.mult), [otmp, gT], [mixT])

    C_HQ, C_HF, C_HI, C_HG, C_SQ, C_SK, C_SV = 0, 512, 1024, 1536, 2048, 2560, 3072
    kst = ar("kst", [64, 8, 512], BF16)

    def token_mix_proj(nsub, kb0, own, out_row0, kdst=None, vdst=None, smp=False):
        kdst = kdst or k_o
        vdst = vdst or v_o
        nt = nsub * 128
        wb, wbv = load_w(w_in, C_SK, 512)
        for h in range(8):
            pb = prot.next()
            fm_proj64(wbv, wb, h, nt, pb)
            E("act", lambda e, h=h, pb=pb: e.activation(out=kst[:, h, 0:nt], in_=pb[0:64, 0:nt], func=AF.Copy), [pb], [kst])
        if not smp:
            dma("pool", kT_scr[:, :, kb0 * 128:kb0 * 128 + nt].rearrange("h d t -> d h t"), kst[:, :, 0:nt], [kst], [kT_scr], scrsem)
        if own and out_row0 is not None:
            for s in range(nsub):
                pb = prot.next()
                tm_proj(wbv, wb, s, pb)
                E("act", lambda e, pb=pb: e.activation(out=kvout[:], in_=pb[:], func=AF.Copy), [pb], [kvout])
                dma("pool", kdst[out_row0 + s * 128:out_row0 + (s + 1) * 128, :], kvout[:], [kvout], [kdst], kvsem)
        wb, wbv = load_w(w_in, C_SV, 512)
        for s in range(nsub):
            pb = prot.next()
            tm_proj(wbv, wb, s, pb)
            E("act", lambda e, pb=pb: e.activation(out=v16[:], in_=pb[:], func=AF.Copy), [pb], [v16])
            if smp:
                E("pool", lambda e: e.tensor_copy(out=svs16[:], in_=v16[:]), [v16], [svs16])
            else:
                dma("pool", v_scr[:, :, kb0 + s, :].rearrange("h p v -> p h v"), v16[:].rearrange("p (h v) -> p h v", h=8), [v16], [v_scr], scrsemV)
            if own and out_row0 is not None:
                E("dve", lambda e, pb=pb: e.tensor_copy(out=kvout[:], in_=pb[:]), [pb], [kvout])
                dma("pool", vdst[out_row0 + s * 128:out_row0 + (s + 1) * 128, :], kvout[:], [kvout], [vdst], kvsem)
        if own:
            wb, wbv = load_w(w_in, C_HQ, 512)
            for h in range(8):
                pb = prot.next()
                fm_proj64(wbv, wb, h, nt, pb)
                E("act", lambda e, h=h, pb=pb: e.activation(out=qT[:, h, 0:nt], in_=pb[0:64, 0:nt], func=AF.Copy), [pb], [qT])
            wb, wbv = load_w(w_in, C_HG, 512)
            for h in range(8):
                pb = prot.next()
                fm_proj64(wbv, wb, h, nt, pb)
                E("act", lambda e, h=h, pb=pb: e.activation(out=gT[:, h, 0:nt], in_=pb[0:64, 0:nt], func=AF.Silu), [pb], [gT])
            wb, wbv = load_w(w_in, C_SQ, 512)
            for h in range(8):
                pb = prot.next()
                fm_proj64(wbv, wb, h, nt, pb)
                E("act", lambda e, h=h, pb=pb: e.activation(out=sqT[:, h, 0:nt], in_=pb[0:64, 0:nt], func=AF.Copy), [pb], [sqT])
        wbf_, wbfv = load_w(w_in, C_HF, 512)
        wbi_, wbiv = load_w(w_in, C_HI, 512)
        for s in range(nsub):
            p_hf, p_hi = P[0], P[1]
            tm_proj(wbfv, wbf_, s, p_hf)
            tm_proj(wbiv, wbi_, s, p_hi)
            hgrn_subtile(s, own, p_hf, p_hi, smp)

    phase(8 * 1024)
    kT_bufs = [ar("kTh%d" % i, [64, NKB * 128], BF16) for i in range(2)]
    vh_bufs = [ar("vh%d" % i, [128, NKB, 64], BF16) for i in range(2)]
    for b_ in kT_bufs + vh_bufs:
        b_.sem = fw.dmasem("kvh")
    e_rot = Rot([ar("e_sb%d" % i, [128, 512]) for i in range(4)])
    sp_rot = Rot([ar("sp16_%d" % i, [128, 512], BF16) for i in range(6)])
    g_rot = Rot([ar("g_sb%d" % i, [128, 512]) for i in range(3)])
    a_rot = Rot([ar("a16_%d" % i, [128, 512], BF16) for i in range(4)])
    z_rot = Rot([P[0], P[1]])

    def sb_attention(nq, kb_hi, kb_diag0, qcol0):
        order = list(range(kb_hi - 1, -1, -1))
        nk = kb_hi * 128
        for hp in range(4):
            ctxs = []
            for i in range(2):
                h = 2 * hp + i
                kTh = kT_bufs[i]
                vh = vh_bufs[i]
                dma("sp", kTh[:, 0:nk], kT_scr[h, :, 0:nk], [kT_scr], [kTh], kTh.sem)
                dma("sp", vh[:, 0:kb_hi, :], v_scr[h, :, 0:kb_hi, :], [v_scr], [vh], vh.sem)
                ctxs.append(dict(h=h, kTh=kTh, vh=vh, pc=P[2 + 2 * i], po=P[3 + 2 * i], st1={}))

            def stage1(c, kb):
                h, kTh = c["h"], c["kTh"]
                pz = z_rot.next()
                E("pe", lambda e: e.matmul(pz[:, 0:nq], lhsT=kTh[:, kb * 128:(kb + 1) * 128], rhs=sqT[:, h, 0:nq], start=True, stop=True), [kTh, sqT], [pz])
                eb_ = e_rot.next()
                E("act", lambda e: e.activation(out=eb_[:, 0:nq], in_=pz[:, 0:nq], func=AF.Exp, scale=0.125, bias=biasv[:, kb, h:h + 1]), [pz, biasv], [eb_])
                if kb >= kb_diag0:
                    E("pool", lambda e: e.tensor_tensor(out=eb_[:, 0:nq], in0=eb_[:, 0:nq], in1=dmaskf[:, kb - kb_diag0, 0:nq], op=ALU.mult), [eb_, dmaskf], [eb_])
                sp_ = sp_rot.next()
                E("act", lambda e: e.activation(out=sp_[:, 0:nq], in_=eb_[:, 0:nq], func=AF.Ln, bias=1.0), [eb_], [sp_])
                c["st1"][kb] = (eb_, sp_)

            def stage2(c, idx):
                kb = order[idx]
                pc, po_, vh, st1 = c["pc"], c["po"], c["vh"], c["st1"]
                eb_, sp_ = st1[kb]
                if idx > 0:
                    spp = st1[order[idx - 1]][1]
                    E("pe", lambda e: e.matmul(pc[:, 0:nq], lhsT=cstb[:, LSTR, :], rhs=spp[:, 0:nq], start=False, stop=False), [cstb, spp], [pc])
                E("pe", lambda e: e.matmul(pc[:, 0:nq], lhsT=cstb[:, UINC, :], rhs=sp_[:, 0:nq], start=(idx == 0), stop=(idx == len(order) - 1)), [cstb, sp_], [pc])
                g_ = g_rot.next()
                E("act", lambda e: e.activation(out=g_[:, 0:nq], in_=pc[:, 0:nq], func=AF.Exp, scale=-1.0), [pc], [g_])
                a_ = a_rot.next()
                E("dve", lambda e: e.tensor_tensor(out=a_[:, 0:nq], in0=eb_[:, 0:nq], in1=g_[:, 0:nq], op=ALU.mult), [eb_, g_], [a_])
                c["a"][idx] = a_
                if idx > 0:
                    del st1[order[idx - 1]]

            def stage3(c, idx):
                kb = order[idx]
                po_, vh = c["po"], c["vh"]
                a_ = c["a"].pop(idx)
                E("pe", lambda e: e.matmul(po_[0:64, 0:nq], lhsT=vh[:, kb, :], rhs=a_[:, 0:nq], start=(idx == 0), stop=(idx == len(order) - 1)), [vh, a_], [po_])

            for c in ctxs:
                c["a"] = {}
                stage1(c, order[0])
            for idx in range(len(order)):
                if idx + 1 < len(order):
                    for c in ctxs:
                        stage1(c, order[idx + 1])
                for c in ctxs:
                    stage2(c, idx)
                if idx > 0:
                    for c in ctxs:
                        stage3(c, idx - 1)
            for c in ctxs:
                stage3(c, len(order) - 1)
            for c in ctxs:
                h, po_ = c["h"], c["po"]
                E("act", lambda e: e.activation(out=mixT[:, 8 + h, qcol0:qcol0 + nq], in_=po_[0:64, 0:nq], func=AF.Copy), [po_], [mixT])

    mkT = sb("mkT", [128, 8, 256], BF16)
    mv16 = sb("mv16", [128, 2, D], BF16)
    phase(16 * 1024)
    qcT = ar("qcT", [128, 8, 512], BF16)
    pT16 = ar("pT16", [128, 2, 512], BF16)
    rden = ar("rden", [128, 512])
    ocT = ar("ocT", [128, 8, 512], BF16)
    memout = ar("memout", [128, D])
    memsem = fw.dmasem("mem")

    def memory_kv():
        KMK = int(os.environ.get("KMK", "9"))
        for s in range(2):
            load_xT(mem, s * 128, xT, s * 128)
        if KMK < 2:
            return
        prenorm(xT, hT, 256, 4)
        if KMK < 3:
            return
        for q in range(2):
            wb, wbv = load_w(w_ck, q * 512, 512)
            if KMK < 4:
                continue
            for b4 in range(4):
                blk = q * 4 + b4
                pb = prot.next()
                for k in range(8):
                    E("pe", lambda e, k=k, b4=b4, pb=pb: e.matmul(pb[:, 0:256], lhsT=wbv[:, k, b4 * 128:(b4 + 1) * 128], rhs=hT[:, k, 0:256], start=(k == 0), stop=(k == 7)), [wb, hT], [pb])
                E("act", lambda e, blk=blk, pb=pb: e.activation(out=mkT[:, blk, :], in_=pb[:, 0:256], func=AF.Copy), [pb], [mkT])
            if KMK < 5:
                continue
            for s in range(2):
                pb = prot.next()
                tm_proj(wbv, wb, s, pb)
                E("act", lambda e, pb=pb: e.activation(out=memout[:, q * 512:(q + 1) * 512], in_=pb[:], func=AF.Copy), [pb], [memout])
                dma("pool", mk_o[s * 128:(s + 1) * 128, q * 512:(q + 1) * 512], memout[:, q * 512:(q + 1) * 512], [memout], [mk_o], memsem)
        if KMK < 6:
            return
        for q in range(2):
            wb, wbv = load_w(w_cv, q * 512, 512)
            for s in range(2):
                pb = prot.next()
                tm_proj(wbv, wb, s, pb)
                E("act", lambda e, pb=pb: e.activation(out=memout[:, q * 512:(q + 1) * 512], in_=pb[:], func=AF.Copy), [pb], [memout])
                E("dve", lambda e, pb=pb, s=s: e.tensor_copy(out=mv16[:, s, q * 512:(q + 1) * 512], in_=pb[:]), [pb], [mv16])
                dma("pool", mv_o[s * 128:(s + 1) * 128, q * 512:(q + 1) * 512], memout[:, q * 512:(q + 1) * 512], [memout], [mv_o], memsem)

    def cross_attn(nt):
        prenorm(xT, hT, nt, 2)
        for q in range(2):
            wb, wbv = load_w(w_cq, q * 512, 512)
            for b4 in range(4):
                blk = q * 4 + b4
                pb = prot.next()
                for k in range(8):
                    E("pe", lambda e, k=k, b4=b4, pb=pb: e.matmul(pb[:, 0:nt], lhsT=wbv[:, k, b4 * 128:(b4 + 1) * 128], rhs=hT[:, k, 0:nt], start=(k == 0), stop=(k == 7)), [wb, hT], [pb])
                E("act", lambda e, blk=blk, pb=pb: e.activation(out=qcT[:, blk, 0:nt], in_=pb[:, 0:nt], func=AF.Copy), [pb], [qcT])
        for hd in range(4):
            for mb in range(2):
                pb = prot.next()
                for j in range(2):
                    E("pe", lambda e, j=j, pb=pb, mb=mb: e.matmul(pb[:, 0:nt], lhsT=mkT[:, 2 * hd + j, mb * 128:(mb + 1) * 128], rhs=qcT[:, 2 * hd + j, 0:nt], start=(j == 0), stop=(j == 1)),
                      [mkT, qcT], [pb])
                E("act", lambda e, pb=pb, mb=mb: e.activation(out=pT16[:, mb, 0:nt], in_=pb[:, 0:nt], func=AF.Exp, scale=1.0 / 16), [pb], [pT16])
            pdn = P[4]
            for mb in range(2):
                E("pe", lambda e, mb=mb: e.matmul(pdn[:, 0:nt], lhsT=cstb[:, ONES, :], rhs=pT16[:, mb, 0:nt], start=(mb == 0), stop=(mb == 1)), [cstb, pT16], [pdn])
            E("dve", lambda e: e.reciprocal(out=rden[:, 0:nt], in_=pdn[:, 0:nt]), [pdn], [rden])
            for j in range(2):
                pb = prot.next()
                for mb in range(2):
                    E("pe", lambda e, mb=mb, pb=pb, j=j: e.matmul(pb[:, 0:nt], lhsT=mv16[:, mb, (2 * hd + j) * 128:(2 * hd + j + 1) * 128], rhs=pT16[:, mb, 0:nt], start=(mb == 0), stop=(mb == 1)),
                      [mv16, pT16], [pb])
                E("dve", lambda e, pb=pb, j=j: e.tensor_tensor(out=ocT[:, 2 * hd + j, 0:nt], in0=pb[:, 0:nt], in1=rden[:, 0:nt], op=ALU.mult), [pb, rden], [ocT])
        linear_fm_to_brT(ocT, 128, 8, w_co, nt)
        postnorm_add(xT, nt, 3)

    phase(16 * 1024)
    uext = [ar("uext%d" % i, [128, 514]) for i in range(2)]
    c_g = ar("c_g", [128, 512])
    c_v = ar("c_v", [128, 512])
    t_a = ar("t_a", [128, 512])
    t_b = ar("t_b", [128, 512])
    actT = ar("actT", [128, 22, 512], BF16)
    cvrow = ar("cvrow", [2, 512])
    cvsem = fw.dmasem("cv")

    def conv_block_s(blk, pb, ue, cdst):
        u3 = ue[:, 0:160].rearrange("p (s t) -> p s t", t=10)
        c3 = cdst[:, 0:128].rearrange("p (s t) -> p s t", t=8)
        E("act", lambda e: e.activation(out=u3[:, :, 2:10], in_=pb[:, 0:128].rearrange("p (s t) -> p s t", t=8), func=AF.Copy), [pb], [ue])
        E("dve", lambda e: e.tensor_copy(out=u3[:, :, 0:2], in_=ubuf_s[:, blk, :, :]), [ubuf_s], [ue])
        E("dve", lambda e: e.tensor_scalar(out=c3, in0=u3[:, :, 0:8], scalar1=convp_sb[:, 0, blk:blk + 1], scalar2=convp_sb[:, 3, blk:blk + 1], op0=ALU.mult, op1=ALU.add),
          [ue, convp_sb], [cdst])
        E("dve", lambda e: e.scalar_tensor_tensor(out=c3, in0=u3[:, :, 1:9], scalar=convp_sb[:, 1, blk:blk + 1], in1=c3, op0=ALU.mult, op1=ALU.add), [ue, convp_sb, cdst], [cdst])
        E("dve", lambda e: e.scalar_tensor_tensor(out=c3, in0=u3[:, :, 2:10], scalar=convp_sb[:, 2, blk:blk + 1], in1=c3, op0=ALU.mult, op1=ALU.add), [ue, convp_sb, cdst], [cdst])

    def conv_block(blk, pb, ue, nt, cdst, use_halo_mask):
        E("act", lambda e: e.activation(out=ue[:, 2:2 + nt], in_=pb[:, 0:nt], func=AF.Copy), [pb], [ue])
        if use_halo_mask:
            E("dve", lambda e: e.tensor_scalar(out=ue[:, 0:2], in0=ubuf[:, blk, :], scalar1=halo_sb[:, 0:1], scalar2=None, op0=ALU.mult), [ubuf, halo_sb], [ue])
        else:
            E("dve", lambda e: e.tensor_copy(out=ue[:, 0:2], in_=ubuf[:, blk, :]), [ubuf], [ue])
        E("dve", lambda e: e.tensor_copy(out=ubuf[:, blk, :], in_=ue[:, nt:nt + 2]), [ue], [ubuf])
        E("dve", lambda e: e.tensor_scalar(out=cdst[:, 0:nt], in0=ue[:, 0:nt], scalar1=convp_sb[:, 0, blk:blk + 1], scalar2=convp_sb[:, 3, blk:blk + 1], op0=ALU.mult, op1=ALU.add),
          [ue, convp_sb], [cdst])
        E("dve", lambda e: e.scalar_tensor_tensor(out=cdst[:, 0:nt], in0=ue[:, 1:1 + nt], scalar=convp_sb[:, 1, blk:blk + 1], in1=cdst[:, 0:nt], op0=ALU.mult, op1=ALU.add),
          [ue, convp_sb, cdst], [cdst])
        E("dve", lambda e: e.scalar_tensor_tensor(out=cdst[:, 0:nt], in0=ue[:, 2:2 + nt], scalar=convp_sb[:, 2, blk:blk + 1], in1=cdst[:, 0:nt], op0=ALU.mult, op1=ALU.add),
          [ue, convp_sb, cdst], [cdst])

    def conv_ffn(nt, first, last, smp=False):
        prenorm(xT, hT, nt, 5)
        for q in range(11):
            wbg, wbgv = load_w(w_up, q * 256, 256)
            wbv_, wbvv = load_w(w_up, DFF + q * 256, 256)
            for b2 in range(2):
                i = 2 * q + b2
                pg, pv = P[0], P[1]
                for k in range(8):
                    E("pe", lambda e, k=k: e.matmul(pg[:, 0:nt], lhsT=wbgv[:, k, b2 * 128:(b2 + 1) * 128], rhs=hT[:, k, 0:nt], start=(k == 0), stop=(k == 7)), [wbg, hT], [pg])
                for k in range(8):
                    E("pe", lambda e, k=k: e.matmul(pv[:, 0:nt], lhsT=wbvv[:, k, b2 * 128:(b2 + 1) * 128], rhs=hT[:, k, 0:nt], start=(k == 0), stop=(k == 7)), [wbv_, hT], [pv])
                if smp:
                    conv_block_s(i, pg, uext[0], c_g)
                    conv_block_s(22 + i, pv, uext[1], c_v)
                else:
                    conv_block(i, pg, uext[0], nt, c_g, first)
                    conv_block(22 + i, pv, uext[1], nt, c_v, first)
                E("pool", lambda e: e.tensor_tensor(out=t_a[:, 0:nt], in0=c_g[:, 0:nt], in1=c_g[:, 0:nt], op=ALU.mult), [c_g], [t_a])
                E("pool", lambda e: e.tensor_scalar(out=t_a[:, 0:nt], in0=t_a[:, 0:nt], scalar1=0.044715, scalar2=1.0, op0=ALU.mult, op1=ALU.add), [t_a], [t_a])
                E("pool", lambda e: e.tensor_tensor(out=t_a[:, 0:nt], in0=t_a[:, 0:nt], in1=c_g[:, 0:nt], op=ALU.mult), [t_a, c_g], [t_a])
                E("act", lambda e: e.activation(out=t_a[:, 0:nt], in_=t_a[:, 0:nt], func=AF.Sigmoid, scale=1.5957691216057308), [t_a], [t_a])
                E("pool", lambda e: e.tensor_tensor(out=t_b[:, 0:nt], in0=c_g[:, 0:nt], in1=c_v[:, 0:nt], op=ALU.mult), [c_g, c_v], [t_b])
                E("dve", lambda e, i=i: e.tensor_tensor(out=actT[:, i, 0:nt], in0=t_a[:, 0:nt], in1=t_b[:, 0:nt], op=ALU.mult), [t_a, t_b], [actT])
            if last and smp:
                for (wbX, wbXv, c0) in ((wbg, wbgv, q * 256), (wbv_, wbvv, DFF + q * 256)):
                    pb = P[4]
                    for j in range(2):
                        for k in range(8):
                            E("pe", lambda e: e.matmul(pb[32 * j:32 * j + 16, 0:256], lhsT=hT[:, k, 0:128].rearrange("p (s t) -> p s t", t=8)[:, :, 6 + j], rhs=wbXv[:, k, :],
                                                       start=(k == 0), stop=(k == 7)), [hT, wbX], [pb])
                    E("act", lambda e: e.activation(out=cvrow_s[0:48, :], in_=pb[0:48, 0:256], func=AF.Copy), [pb], [cvrow_s])
                    for j in range(2):
                        dma("pool", cvs_o[:, c0:c0 + 256].rearrange("(s j) n -> j s n", j=2)[j], cvrow_s[32 * j:32 * j + 16, :], [cvrow_s], [cvs_o], cvsem)
            elif last:
                for (wbX, wbXv, c0) in ((wbg, wbgv, q * 256), (wbv_, wbvv, DFF + q * 256)):
                    pb = P[4]
                    for k in range(8):
                        E("pe", lambda e, k=k, wbXv=wbXv: e.matmul(pb[0:2, 0:256], lhsT=hT[:, k, nt - 2:nt], rhs=wbXv[:, k, :], start=(k == 0), stop=(k == 7)), [hT, wbX], [pb])
                    E("act", lambda e: e.activation(out=cvrow[:, 0:256], in_=pb[0:2, 0:256], func=AF.Copy), [pb], [cvrow])
                    dma("pool", cv_o[:, c0:c0 + 256], cvrow[:, 0:256], [cvrow], [cv_o], cvsem)
        for blk in range(8):
            wb, wbv = load_w(w_down, blk * 128, 128, 128, 22)
            pb = prot.next()
            for kk in range(22):
                E("pe", lambda e, kk=kk: e.matmul(pb[:, 0:nt], lhsT=wbv[:, kk, :], rhs=actT[:, kk, 0:nt], start=(kk == 0), stop=(kk == 21)), [wb, actT], [pb])
            E("act", lambda e: e.activation(out=brT[:, blk, 0:nt], in_=pb[:, 0:nt], func=AF.Copy), [pb], [brT])
        postnorm_add(xT, nt, 6)

    def w_o_proj(nt):
        for q in range(2):
            wbs = []
            for part in range(2):
                wbs.append(load_w(w_o, q * 512, 512, kp=64, kc=8, r0=part * 512))
            for b4 in range(4):
                blk = q * 4 + b4
                pb = prot.next()
                for kk in range(16):
                    wb, wbv = wbs[kk // 8]
                    E("pe", lambda e, kk=kk, wbv=wbv, pb=pb, b4=b4: e.matmul(pb[:, 0:nt], lhsT=wbv[:, kk % 8, b4 * 128:(b4 + 1) * 128], rhs=mixT[:, kk, 0:nt], start=(kk == 0), stop=(kk == 15)),
                      [wb, mixT], [pb])
                E("act", lambda e, blk=blk, pb=pb: e.activation(out=brT[:, blk, 0:nt], in_=pb[:, 0:nt], func=AF.Copy), [pb], [brT])

    phase(0)
    yhi = ar("yhi", [128, 8, 128], BF16)
    ylo = ar("ylo", [128, 8, 128], BF16)
    yout = ar("yout", [128, D])
    ysem = fw.dmasem("y")

    def store_y(dst, row0, col0):
        E("dve", lambda e: e.tensor_copy(out=yhi[:], in_=xT[:, :, col0:col0 + 128]), [xT], [yhi])
        E("pool", lambda e: e.tensor_tensor(out=ylo[:], in0=xT[:, :, col0:col0 + 128], in1=yhi[:], op=ALU.subtract), [xT, yhi], [ylo])
        for c in range(8):
            E("pe", lambda e, c=c: e.transpose(out=PT[0][:, c * 128:(c + 1) * 128], in_=yhi[:, c, :], identity=cstb[:, IDENT, :]), [yhi, cstb], [PT[0]])
        for c in range(8):
            E("pe", lambda e, c=c: e.transpose(out=PT[1][:, c * 128:(c + 1) * 128], in_=ylo[:, c, :], identity=cstb[:, IDENT, :]), [ylo, cstb], [PT[1]])
        E("act", lambda e: e.activation(out=yout[:], in_=PT[0][:], func=AF.Copy), [PT[0]], [yout])
        E("dve", lambda e: e.tensor_tensor(out=yout[:], in0=yout[:], in1=PT[1][:], op=ALU.add), [yout, PT[1]], [yout])
        dma("pool", dst[row0:row0 + 128, :], yout[:], [yout], [dst], ysem)

    STG = os.environ.get("KSTG", "mabswcfyS")
    NA = int(os.environ.get("KNA", "12"))
    NB = int(os.environ.get("KNB", "5"))
    precast_weights()
    fw.barrier()
    if "m" in STG:
        memory_kv()
    fw.barrier()

    a_tiles = [4] * 11 + [3]
    sub0 = 0
    for nsub in (a_tiles[:NA] if 'a' in STG else []):
        for s in range(nsub):
            load_xT(xa, (sub0 + s) * 128, xT, s * 128)
        fw.barrier()
        prenorm(xT, hT, nsub * 128, 0)
        token_mix_proj(nsub, sub0, False, None)
        fw.barrier()
        sub0 += nsub

    b_tiles = [1, 4, 4, 4, 4]
    sub0 = 0
    for ti, nsub in enumerate(b_tiles[:NB] if 'b' in STG else []):
        nt = nsub * 128
        for s in range(nsub):
            load_xT(xb, (sub0 + s) * 128, xT, s * 128)
        fw.barrier()
        prenorm(xT, hT, nt, 0)
        token_mix_proj(nsub, NSP + sub0, True, (sub0 - 1) * 128 if ti > 0 else None)
        fw.barrier()
        if 's' in STG:
            sb_attention(nt, NSP + sub0 + nsub, NSP + sub0, 0)
        fw.barrier()
        if 'w' in STG:
            w_o_proj(nt)
            postnorm_add(xT, nt, 1)
        fw.barrier()
        if 'c' in STG:
            cross_attn(nt)
        fw.barrier()
        if 'f' in STG:
            conv_ffn(nt, ti == 1, ti == len(b_tiles) - 1)
        fw.barrier()
        if ti > 0 and 'y' in STG:
            for s in range(nsub):
                store_y(y_o, (sub0 - 1 + s) * 128, s * 128)
        fw.barrier()
        sub0 += nsub

    if "S" in STG:
        fw.barrier()
        phase(0)
        sqT = ar("sqT_s", [64, 8, 128], BF16)
        kst = ar("kst_s", [64, 8, 128], BF16)
        svs16 = ar("svs16", [128, 512], BF16)
        phase(8 * 1024)
        qT = ar("qT_s", [64, 8, 128], BF16)
        gT = ar("gT_s", [64, 8, 128], BF16)
        f_sb = ar("f_s", [128, 512]); lf = ar("lf_s", [128, 512]); lfh = ar("lfh_s", [128, 512], BF16); lfl = ar("lfl_s", [128, 512], BF16)
        k16 = ar("k16_s", [128, 512], BF16); kdd = ar("kdd_s", [128, 512], BF16); eD = ar("eD_s", [128, 512]); v16 = ar("v16_s", [128, 512], BF16)
        ebT = ar("ebT_s", [64, 8, 128]); enbT = ar("enbT_s", [64, 8, 128]); orst = enbT; otmp = ebT
        qtT = ar("qtT_s", [64, 8, 128], BF16); ktT = ar("ktT_s", [64, 8, 128], BF16); attm = ar("attm_s", [128, 8, 128], BF16)
        osq = ar("osq_s", [64, 8, 128], BF16); kvout = ar("kvout_s", [128, 512])
        ebl16 = ar("ebl16", [64, 8, 16]); stmp16 = ar("stmp16", [64, 16, 64])
        s0f_rot = Rot([ar("s0f%d" % i, [64, 16, 64]) for i in range(2)])
        s16h_rot = Rot([ar("s16h%d" % i, [64, 16, 64], BF16) for i in range(2)])
        vmh_rot = Rot([ar("vmh%d" % i, [128, 16, 64], BF16) for i in range(2)])
        for b_ in s0f_rot.bufs:
            b_.sem = fw.dmasem("sh")
            b_.sem2 = fw.dmasem("hss")
        load_xT(xs_d, 0, xT, 0)
        fw.barrier()
        prenorm(xT, hT, 128, 0)
        token_mix_proj(1, 0, True, 0, ks_o, vs_o, True)
        fw.barrier()

        phase(8 * 1024)
        pgK_rot = Rot([ar("pgK%d" % i, [128, 512]) for i in range(4)])
        pgV_rot = Rot([ar("pgV%d" % i, [128, 512]) for i in range(4)])
        for b_ in pgK_rot.bufs + pgV_rot.bufs:
            b_.sem = fw.dmasem("pg")
        pgK16 = ar("pgK16", [128, 512], BF16)
        KTp_rot = Rot([ar("KTp%d" % i, [64, 8, 128], BF16) for i in range(2)])
        V16 = ar("V16", [128, 16, 512], BF16)
        e_s = ar("e_s", [128, 17, 64]); g_s = ar("g_s", [128, 17, 64])
        sp_s = ar("sp_s", [128, 17, 64], BF16); a_s = ar("a_s", [128, 17, 64], BF16)
        zb = [P[0], P[1], P[2]]
        po_s = [P[3], P[4]]
        for sq in range(16):
            for pg in range(16):
                pk = pgK_rot.next(); pv = pgV_rot.next(); ktp = KTp_rot.next()
                col = sq * 16 + pg
                fw.op("pool", lambda e: e.indirect_dma_start(out=pk[:, :], out_offset=None, in_=csk[:, :],
                                                             in_offset=bass.IndirectOffsetOnAxis(ap=idx[:, col:col + 1], axis=0)), [csk, idx], [pk], dsem=pk.sem)
                fw.op("pool", lambda e: e.indirect_dma_start(out=pv[:, :], out_offset=None, in_=csv[:, :],
                                                             in_offset=bass.IndirectOffsetOnAxis(ap=idx[:, col:col + 1], axis=0)), [csv, idx], [pv], dsem=pv.sem)
                E("dve", lambda e: e.tensor_copy(out=pgK16[:], in_=pk[:]), [pk], [pgK16])
                E("act", lambda e: e.activation(out=V16[:, pg, :], in_=pv[:], func=AF.Copy), [pv], [V16])
                for h in range(8):
                    E("pe", lambda e: e.transpose(out=PT[0][0:64, h * 128:(h + 1) * 128], in_=pgK16[:, h * 64:(h + 1) * 64], identity=cstb[:, IDENT, :]), [pgK16, cstb], [PT[0]])
                E("act", lambda e: e.activation(out=ktp[:], in_=PT[0][0:64, :].rearrange("p (h t) -> p h t", h=8), func=AF.Copy), [PT[0]], [ktp])
                zbk = zb[pg // 8]
                for h in range(8):
                    c0 = (pg % 8) * 64 + h * 8
                    E("pe", lambda e: e.matmul(zbk[:, c0:c0 + 8], lhsT=ktp[:, h, :], rhs=sqT[:, h, sq * 8:sq * 8 + 8], start=True, stop=True), [ktp, sqT], [zbk])
            for h in range(8):
                E("pe", lambda e: e.matmul(zb[2][:, h * 8:h * 8 + 8], lhsT=kst[:, h, 0:128], rhs=sqT[:, h, sq * 8:sq * 8 + 8], start=True, stop=True), [kst, sqT], [zb[2]])
            for bk in range(3):
                nb = 8 if bk < 2 else 1
                E("act", lambda e: e.activation(out=e_s[:, bk * 8:bk * 8 + nb, :], in_=zb[bk][:, 0:nb * 64].rearrange("p (b c) -> p b c", c=64), func=AF.Exp, scale=0.125),
                  [zb[bk]], [e_s])
            E("dve", lambda e: e.tensor_tensor(out=e_s[:].rearrange("p b (h q) -> p b h q", h=8), in0=e_s[:].rearrange("p b (h q) -> p b h q", h=8),
                                               in1=expb[:].unsqueeze(1).unsqueeze(3).to_broadcast([128, 17, 8, 8]), op=ALU.mult), [e_s, expb], [e_s])
            E("dve", lambda e: e.tensor_tensor(out=e_s[:, 16, :].rearrange("p (h q) -> p h q", h=8), in0=e_s[:, 16, :].rearrange("p (h q) -> p h q", h=8),
                                               in1=smask[:, sq, :].unsqueeze(1).to_broadcast([128, 8, 8]), op=ALU.mult), [e_s, smask], [e_s])
            E("act", lambda e: e.activation(out=sp_s[:], in_=e_s[:], func=AF.Ln, bias=1.0), [e_s], [sp_s])
            for blk in range(17):
                zbk = zb[blk // 8]
                c0 = (blk % 8) * 64
                E("pe", lambda e: e.matmul(zbk[:, c0:c0 + 64], lhsT=cstb[:, UINC, :], rhs=sp_s[:, blk, :], start=True, stop=(blk == 16)), [cstb, sp_s], [zbk])
                for b2 in range(blk + 1, 17):
                    E("pe", lambda e: e.matmul(zbk[:, c0:c0 + 64], lhsT=cstb[:, ONES, :], rhs=sp_s[:, b2, :], start=False, stop=(b2 == 16)), [cstb, sp_s], [zbk])
            for bk in range(3):
                nb = 8 if bk < 2 else 1
                E("act", lambda e: e.activation(out=g_s[:, bk * 8:bk * 8 + nb, :], in_=zb[bk][:, 0:nb * 64].rearrange("p (b c) -> p b c", c=64), func=AF.Exp, scale=-1.0),
                  [zb[bk]], [g_s])
            E("dve", lambda e: e.tensor_tensor(out=a_s[:], in0=e_s[:], in1=g_s[:], op=ALU.mult), [e_s, g_s], [a_s])
            for h in range(8):
                pob = po_s[h // 4]
                c0 = (h % 4) * 128 + sq * 8
                for blk in range(17):
                    lh = V16[:, blk, h * 64:(h + 1) * 64] if blk < 16 else svs16[:, h * 64:(h + 1) * 64]
                    E("pe", lambda e: e.matmul(pob[0:64, c0:c0 + 8], lhsT=lh, rhs=a_s[:, blk, h * 8:h * 8 + 8], start=(blk == 0), stop=(blk == 16)),
                      [V16, svs16, a_s], [pob])
        for half in range(2):
            E("act", lambda e: e.activation(out=mixT[:, 8 + half * 4:8 + (half + 1) * 4, 0:128], in_=po_s[half][0:64, :].rearrange("p (h t) -> p h t", h=4), func=AF.Copy),
              [po_s[half]], [mixT])
        fw.barrier()
        w_o_proj(128)
        postnorm_add(xT, 128, 1)
        fw.barrier()

        phase(16 * 1024)
        qcT = ar("qcT_s", [128, 8, 512], BF16); pT16 = ar("pT16_s", [128, 2, 512], BF16); rden = ar("rden_s", [128, 512]); ocT = ar("ocT_s", [128, 8, 512], BF16)
        mks_rot = Rot([ar("mks%d" % i, [128, 2, D]) for i in range(1)])
        mvs_rot = Rot([ar("mvs%d" % i, [128, 2, D]) for i in range(1)])
        mk16 = ar("mk16", [128, 2, D], BF16); mv16s = ar("mv16s", [128, 2, D], BF16); mkTs = ar("mkTs", [128, 8, 256], BF16)
        pTs = ar("pTs", [128, 64], BF16); rdens = ar("rdens", [128, 32])
        cmsem = [fw.dmasem("cmk"), fw.dmasem("cmv")]
        prenorm(xT, hT, 128, 2)
        for q in range(2):
            wb, wbv = load_w(w_cq, q * 512, 512)
            for b4 in range(4):
                blk = q * 4 + b4
                pb = prot.next()
                for k in range(8):
                    E("pe", lambda e: e.matmul(pb[:, 0:128], lhsT=wbv[:, k, b4 * 128:(b4 + 1) * 128], rhs=hT[:, k, 0:128], start=(k == 0), stop=(k == 7)), [wb, hT], [pb])
                E("act", lambda e: e.activation(out=qcT[:, blk, 0:128], in_=pb[:, 0:128], func=AF.Copy), [pb], [qcT])
        for sq in range(16):
            mks = mks_rot.next(); mvs = mvs_rot.next()
            dma("sp", mks[:], cmk[sq].rearrange("(b p) n -> p b n", p=128), [cmk], [mks], cmsem[0])
            dma("sp", mvs[:], cmv[sq].rearrange("(b p) n -> p b n", p=128), [cmv], [mvs], cmsem[1])
            E("dve", lambda e: e.tensor_copy(out=mk16[:], in_=mks[:]), [mks], [mk16])
            E("pool", lambda e: e.tensor_copy(out=mv16s[:], in_=mvs[:]), [mvs], [mv16s])
            for mb in range(2):
                for blk in range(8):
                    E("pe", lambda e: e.transpose(out=PT[0][:, blk * 128:(blk + 1) * 128], in_=mk16[:, mb, blk * 128:(blk + 1) * 128], identity=cstb[:, IDENT, :]), [mk16, cstb], [PT[0]])
                E("act", lambda e: e.activation(out=mkTs[:, :, mb * 128:(mb + 1) * 128], in_=PT[0][:].rearrange("p (b m) -> p b m", b=8), func=AF.Copy), [PT[0]], [mkTs])
            psc, pdn, ppv = P[0], P[1], P[2]
            for hd in range(4):
                for mb in range(2):
                    c0 = (hd * 2 + mb) * 8
                    for j in range(2):
                        E("pe", lambda e: e.matmul(psc[:, c0:c0 + 8], lhsT=mkTs[:, 2 * hd + j, mb * 128:(mb + 1) * 128], rhs=qcT[:, 2 * hd + j, sq * 8:sq * 8 + 8],
                                                   start=(j == 0), stop=(j == 1)), [mkTs, qcT], [psc])
            E("act", lambda e: e.activation(out=pTs[:], in_=psc[:, 0:64], func=AF.Exp, scale=1.0 / 16), [psc], [pTs])
            for hd in range(4):
                for mb in range(2):
                    c0 = (hd * 2 + mb) * 8
                    E("pe", lambda e: e.matmul(pdn[:, hd * 8:hd * 8 + 8], lhsT=cstb[:, ONES, :], rhs=pTs[:, c0:c0 + 8], start=(mb == 0), stop=(mb == 1)), [cstb, pTs], [pdn])
            E("dve", lambda e: e.reciprocal(out=rdens[:], in_=pdn[:, 0:32]), [pdn], [rdens])
            for hd in range(4):
                for j in range(2):
                    for mb in range(2):
                        c0 = (hd * 2 + mb) * 8
                        E("pe", lambda e: e.matmul(ppv[:, (hd * 2 + j) * 8:(hd * 2 + j) * 8 + 8], lhsT=mv16s[:, mb, (2 * hd + j) * 128:(2 * hd + j + 1) * 128], rhs=pTs[:, c0:c0 + 8],
                                                   start=(mb == 0), stop=(mb == 1)), [mv16s, pTs], [ppv])
            E("dve", lambda e: e.tensor_tensor(out=ocT[:, :, sq * 8:sq * 8 + 8].rearrange("p (hd j) t -> p hd j t", j=2), in0=ppv[:, 0:64].rearrange("p (hd j t) -> p hd j t", hd=4, j=2),
                                               in1=rdens[:].rearrange("p (hd t) -> p hd t", hd=4).unsqueeze(2).to_broadcast([128, 4, 2, 8]), op=ALU.mult), [ppv, rdens], [ocT])
        linear_fm_to_brT(ocT, 128, 8, w_co, 128)
        postnorm_add(xT, 128, 3)
        fw.barrier()

        phase(16 * 1024)
        uext = [ar("uext_s%d" % i, [128, 514]) for i in range(2)]
        c_g = ar("c_g_s", [128, 512]); c_v = ar("c_v_s", [128, 512]); t_a = ar("t_a_s", [128, 512]); t_b = ar("t_b_s", [128, 512])
        actT = ar("actT_s", [128, 22, 512], BF16)
        cvrow_s = ar("cvrow_s", [64, 256])
        ubuf_s = ar("ubuf_s", [128, 44, 16, 2])
        sct = ar("sct", [32, 1408]); schi = ar("schi", [32, 1408], BF16); sclo = ar("sclo", [32, 1408], BF16); utmp = ar("utmp", [128, 352])
        scsem = fw.dmasem("sc")
        for ci in range(4):
            dma("sp", sct[:], sc_d[:, ci * 1408:(ci + 1) * 1408], [sc_d], [sct], scsem)
            E("dve", lambda e: e.tensor_copy(out=schi[:], in_=sct[:]), [sct], [schi])
            E("pool", lambda e: e.tensor_tensor(out=sclo[:], in0=sct[:], in1=schi[:], op=ALU.subtract), [sct, schi], [sclo])
            for b in range(11):
                E("pe", lambda e: e.transpose(out=PT[0][:, b * 32:(b + 1) * 32], in_=schi[:, b * 128:(b + 1) * 128], identity=cstb[0:32, IDENT, 0:32]), [schi, cstb], [PT[0]])
            for b in range(11):
                E("pe", lambda e: e.transpose(out=PT[1][:, b * 32:(b + 1) * 32], in_=sclo[:, b * 128:(b + 1) * 128], identity=cstb[0:32, IDENT, 0:32]), [sclo, cstb], [PT[1]])
            E("act", lambda e: e.activation(out=utmp[:], in_=PT[0][:, 0:352], func=AF.Copy), [PT[0]], [utmp])
            E("dve", lambda e: e.tensor_tensor(out=ubuf_s[:, ci * 11:(ci + 1) * 11, :, :].rearrange("p b s j -> p (b s j)"), in0=utmp[:], in1=PT[1][:, 0:352], op=ALU.add),
              [utmp, PT[1]], [ubuf_s])
        conv_ffn(128, False, True, True)
        fw.barrier()
        phase(0)
        yhi = ar("yhi_s", [128, 8, 128], BF16); ylo = ar("ylo_s", [128, 8, 128], BF16); yout = ar("yout_s", [128, D])
        store_y(ys_o, 0, 0)
        fw.barrier()

    hsem = fw.dmasem("hs")
    dma("pool", hs_o[:].rearrange("h k v -> k h v"), S[:], [S], [hs_o], hsem)

    fw.barrier()
    fw.replay()
    es.close()
    fw.close()
    return nc


_CACHE = {}


def _consts():
    c = np.zeros((128, 8, 128), np.float32)
    i = np.arange(128)
    same = (i[:, None] // 64) == (i[None, :] // 64)
    c[:, 0, :] = np.eye(128)
    c[:, 1, :] = ((i[:, None] <= i[None, :]) & same)
    c[:, 2, :] = ((i[:, None] > i[None, :]) & same)
    c[:, 3, :] = ((i[:, None] <= i[None, :]) & same)
    c[:, 4, :] = (i[:, None] >= i[None, :])
    c[:, 5, :] = (i[:, None] < i[None, :])
    c[:, 6, :] = 1.0
    c[:, 7, 0] = (i < 64)
    c[:, 7, 1] = (i >= 64)
    q = np.arange(512)
    dm = np.zeros((128, 4, 512), np.float32)
    for b in range(4):
        dm[:, b, :] = ((b * 128 + i)[:, None] < q[None, :])
    return c, dm


def _consts8():
    i = np.arange(128)
    same = (i[:, None] // 8) == (i[None, :] // 8)
    c = np.zeros((128, 4, 128), np.float32)
    c[:, 0, :] = ((i[:, None] <= i[None, :]) & same)
    c[:, 1, :] = ((i[:, None] > i[None, :]) & same)
    c[:, 2, :] = ((i[:, None] <= i[None, :]) & same)
    c[:, 3, 0:16] = (i[:, None] // 8 == np.arange(16)[None, :])
    sm = np.zeros((128, 16, 8), np.float32)
    for sq in range(16):
        sm[:, sq, :] = ((i[:, None] // 8 == sq) & ((i[:, None] % 8) < np.arange(8)[None, :]))
    return c, sm


def kernel(x_prompt, x_sample, cache_sb_k, cache_sb_v, state_hgrn, state_ffn_conv,
           cache_mem_k, cache_mem_v, page_table, mem_prompt,
           w_in, hg_norm, hg_lb, sb_bias, w_o, g_mix_pre, g_mix_post, g_ca_pre, g_ca_post, g_mem,
           w_cq, w_ck, w_cv, w_co, g_ffn_pre, g_ffn_post, w_up, conv_w, conv_b, w_down):
    f = np.float32
    csk_full = np.asarray(cache_sb_k, f)[0]
    n_pool = csk_full.shape[0]
    key = ("nc", n_pool)
    if key not in _CACHE:
        _CACHE[key] = build_program(n_pool)
    nc = _CACHE[key]
    csk_flat = csk_full.reshape(n_pool * 128, 512)
    csv_flat = np.asarray(cache_sb_v, f)[0].reshape(n_pool * 128, 512)
    c8, sm8 = _consts8()
    cst, dm = _consts()
    gs = np.stack([np.asarray(g, f)[0].reshape(8, 128).T for g in (g_mix_pre, g_mix_post, g_ca_pre, g_ca_post, g_mem, g_ffn_pre, g_ffn_post)], axis=1)
    hgn = np.ascontiguousarray(np.asarray(hg_norm, f)[0].reshape(8, 64).T)
    lbraw = np.ascontiguousarray(np.broadcast_to(np.asarray(hg_lb, f)[None], (128, 2, 512)))
    sbb = np.ascontiguousarray(np.broadcast_to(np.asarray(sb_bias, f)[0][None], (128, 8)))
    cw = np.asarray(conv_w, f)[0]
    cb = np.asarray(conv_b, f)[0]
    convp = np.ascontiguousarray(np.stack([cw[0], cw[1], cw[2], cb], 0).reshape(4, 44, 128).transpose(2, 0, 1))
    shared = dict(w_in=np.asarray(w_in, f)[0], w_o=np.asarray(w_o, f)[0], w_cq=np.asarray(w_cq, f)[0], w_ck=np.asarray(w_ck, f)[0],
                  w_cv=np.asarray(w_cv, f)[0], w_co=np.asarray(w_co, f)[0], w_up=np.asarray(w_up, f)[0], w_down=np.asarray(w_down, f)[0],
                  gvecs=np.ascontiguousarray(gs), hgn=hgn, lbraw=lbraw, sbb=sbb, convp=convp, cst=cst, dmask=dm,
                  csk=csk_flat, csv=csv_flat, cst8=c8, smask=sm8, iotaf=np.arange(128, dtype=f).reshape(128, 1))
    xsmp = np.asarray(x_sample, f)
    shg = np.asarray(state_hgrn, f)[0]
    sfc = np.asarray(state_ffn_conv, f)[0]
    cmk_ = np.asarray(cache_mem_k, f)[0]
    cmv_ = np.asarray(cache_mem_v, f)[0]
    ptab = np.asarray(page_table, np.int32)
    xp = np.asarray(x_prompt, f)
    in_maps = []
    for c in range(8):
        b, j = c // 4, c % 4
        lo = 2048 * j - 128 - NSP * 128
        full = np.zeros((NKB * 128, D), f)
        src_lo = max(lo, 0)
        full[src_lo - lo:] = xp[b, src_lo:2048 * j + 2048]
        kvalid = np.zeros((128, NKB), f)
        nvalid_from = (src_lo - lo) // 128
        kvalid[:, :nvalid_from] = -30000.0
        m = dict(shared)
        m.update(xa=np.ascontiguousarray(full[:NSP * 128]), xb=np.ascontiguousarray(full[NSP * 128:]), kvalid=kvalid,
                 halo_valid=np.full((128, 1), 0.0 if j == 0 else 1.0, f), mem=np.asarray(mem_prompt, f)[b],
                 xs=np.ascontiguousarray(xsmp[16 * c:16 * c + 16].reshape(128, D)),
                 pt=np.ascontiguousarray(ptab[16 * c:16 * c + 16].reshape(256)),
                 sh=np.ascontiguousarray(shg[16 * c:16 * c + 16]),
                 sc=np.ascontiguousarray(sfc[16 * c:16 * c + 16].reshape(32, 2 * DFF)),
                 cmk=np.ascontiguousarray(cmk_[16 * c:16 * c + 16].reshape(16, 256, D)),
                 cmv=np.ascontiguousarray(cmv_[16 * c:16 * c + 16].reshape(16, 256, D)))
        in_maps.append(m)
    res = run_bass_kernel_spmd(nc, in_maps, core_ids=list(range(8)))
    R = res.results
    yp = np.zeros((2, 8192, D), f)
    kp = np.zeros((1, 2, 8192, 8, 64), f)
    vp = np.zeros((1, 2, 8192, 8, 64), f)
    for c in range(8):
        b, j = c // 4, c % 4
        yp[b, 2048 * j:2048 * j + 2048] = R[c]["y"]
        kp[0, b, 2048 * j:2048 * j + 2048] = R[c]["kout"].reshape(2048, 8, 64)
        vp[0, b, 2048 * j:2048 * j + 2048] = R[c]["vout"].reshape(2048, 8, 64)
    hsp = np.stack([R[3]["hstate"], R[7]["hstate"]])[None]
    cvp = np.stack([R[3]["convout"], R[7]["convout"]])[None]
    mkp = np.stack([R[0]["memk"], R[4]["memk"]]).reshape(1, 2, 256, 4, 256)
    mvp = np.stack([R[0]["memv"], R[4]["memv"]]).reshape(1, 2, 256, 4, 256)
    ys = np.concatenate([R[c]["ys"].reshape(16, 8, D) for c in range(8)], 0)
    ks = np.concatenate([R[c]["ksout"].reshape(16, 8, 8, 64) for c in range(8)], 0)[None]
    vs = np.concatenate([R[c]["vsout"].reshape(16, 8, 8, 64) for c in range(8)], 0)[None]
    hss = np.concatenate([R[c]["hss"] for c in range(8)], 0)[None]
    cvs = np.concatenate([R[c]["cvs"].reshape(16, 2, 2 * DFF) for c in range(8)], 0)[None]
    return (yp, ys, kp, vp, hsp.astype(f), cvp.astype(f), mkp, mvp, ks, vs, hss, cvs)
```
